# Optimizing a Trainium2 kernel written in Bass

```python
import math
import jax, jax.numpy as jnp
from jax import lax
import numpy as np

D_MODEL = 1024
BATCH = 8
SEQ = 2048
DEPTH = 4
DEC_BATCH = 2
DEC_SEQ = 8192
PAST_LEN = 128

GRID_W = 64
Q_BLOCK = 128
EPS = 1e-6
DA_HEADS = 8
DA_HEAD_DIM = 64
DA_V_DIM = 2 * DA_HEAD_DIM
NA_HEADS = 16
NA_HEAD_DIM = 64
NA_ROWS_MAX = 8
NA_COLS = 16
GQ_HEADS = 8
GQ_KV_HEADS = 2
GQ_HEAD_DIM = 128
ROPE_THETA = 10000.0
D_FF = 4 * D_MODEL
PLE_DIM = 256
N_BRANCH = 3

DA_Q = DA_HEADS * 2 * DA_HEAD_DIM
DA_K = DA_HEADS * 2 * DA_HEAD_DIM
DA_V = DA_HEADS * DA_V_DIM
NA_W = NA_HEADS * NA_HEAD_DIM
GQ_Q = GQ_HEADS * GQ_HEAD_DIM
GQ_KV = GQ_KV_HEADS * GQ_HEAD_DIM
GATE_W = N_BRANCH * D_MODEL
IN_SIZES = (DA_Q, DA_K, DA_V, NA_W, NA_W, NA_W, GQ_Q, GQ_KV, GQ_KV, GATE_W)
IN_W = DA_Q + DA_K + DA_V + 3 * NA_W + GQ_Q + 2 * GQ_KV + GATE_W

kernel_name = "hybrid_gated_parallel_encoder"


def rmsnorm(x, g):
    xf = x.astype(jnp.float32)
    y = xf * lax.rsqrt(jnp.mean(xf * xf, axis=-1, keepdims=True) + EPS)
    return (y * g.astype(jnp.float32)).astype(x.dtype)


def split_cols(z):
    outs = []
    off = 0
    for n in IN_SIZES:
        outs.append(z[..., off:off + n])
        off += n
    return outs


def diff_attention(q, k, v, lam):
    B, S, H = q.shape[0], q.shape[1], q.shape[2]
    nb = S // Q_BLOCK
    scale = DA_HEAD_DIM ** -0.5
    slopes = jnp.exp2(-8.0 * jnp.arange(1, H + 1, dtype=jnp.float32) / H)
    pos = jnp.arange(S)
    qb = q.reshape(B, nb, Q_BLOCK, H, 2, DA_HEAD_DIM).transpose(1, 0, 2, 3, 4, 5)

    def block(args):
        qblk, start = args
        s = jnp.einsum('bqhmd,bkhmd->bmhqk', qblk, k).astype(jnp.float32) * scale
        qpos = start + jnp.arange(Q_BLOCK)
        dist = jnp.abs(qpos[:, None] - pos[None, :]).astype(jnp.float32)
        s = s - slopes[:, None, None] * dist
        p = jax.nn.softmax(s, axis=-1)
        a = p[:, 0] - lam * p[:, 1]
        return jnp.einsum('bhqk,bkhe->bqhe', a.astype(v.dtype), v)

    out = lax.map(block, (qb, jnp.arange(nb) * Q_BLOCK))
    return out.transpose(1, 0, 2, 3, 4).reshape(B, S, H, DA_V_DIM)


def neighbourhood_attention(q, k, v, rpb):
    B, S, H, d = q.shape
    rows = S // GRID_W
    kr = min(NA_ROWS_MAX, rows)
    scale = NA_HEAD_DIM ** -0.5
    r = jnp.arange(rows)
    row_start = jnp.clip(r - kr // 2, 0, rows - kr)
    c = jnp.arange(GRID_W)
    col_start = jnp.clip(c - NA_COLS // 2, 0, GRID_W - NA_COLS)
    in_win = (c[None, :] >= col_start[:, None]) & (c[None, :] < col_start[:, None] + NA_COLS)
    dc = jnp.clip(c[None, :] - c[:, None], -(NA_COLS - 1), NA_COLS - 1) + (NA_COLS - 1)
    qg = q.reshape(B, rows, GRID_W, H, d).transpose(1, 0, 2, 3, 4)
    kg = k.reshape(B, rows, GRID_W, H, d)
    vg = v.reshape(B, rows, GRID_W, H, d)

    def row_block(args):
        q_row, r0, start = args
        k_rows = lax.dynamic_slice_in_dim(kg, start, kr, axis=1)
        v_rows = lax.dynamic_slice_in_dim(vg, start, kr, axis=1)
        dr = start + jnp.arange(kr) - r0 + (NA_ROWS_MAX - 1)
        bias = rpb[:, dr[:, None, None], dc[None, :, :]].transpose(0, 2, 1, 3)
        s = jnp.einsum('bqhd,bikhd->bhqik', q_row, k_rows).astype(jnp.float32) * scale
        s = s + bias[None].astype(jnp.float32)
        s = jnp.where(in_win[:, None, :], s, -jnp.inf)
        p = jax.nn.softmax(s.reshape(B, H, GRID_W, kr * GRID_W), axis=-1).reshape(s.shape)
        return jnp.einsum('bhqik,bikhd->bqhd', p.astype(v.dtype), v_rows)

    out = lax.map(row_block, (qg, r, row_start))
    return out.transpose(1, 0, 2, 3, 4).reshape(B, S, H, d)


def axial_rope_tables(S):
    t = jnp.arange(S)
    row = (t // GRID_W).astype(jnp.float32)
    col = (t % GRID_W).astype(jnp.float32)
    half = GQ_HEAD_DIM // 2
    freqs = ROPE_THETA ** (-jnp.arange(0, half, 2, dtype=jnp.float32) / half)
    ang = jnp.concatenate([row[:, None] * freqs, col[:, None] * freqs], axis=-1)
    return jnp.cos(ang), jnp.sin(ang)


def apply_rope(x, cos, sin):
    x2 = x.astype(jnp.float32).reshape(x.shape[:-1] + (x.shape[-1] // 2, 2))
    x0, x1 = x2[..., 0], x2[..., 1]
    c = cos[None, :, None, :]
    s = sin[None, :, None, :]
    out = jnp.stack([x0 * c - x1 * s, x0 * s + x1 * c], axis=-1)
    return out.reshape(x.shape).astype(x.dtype)


def gqa_attention(q, k, v):
    B, S = q.shape[0], q.shape[1]
    G = GQ_HEADS // GQ_KV_HEADS
    nb = S // Q_BLOCK
    scale = GQ_HEAD_DIM ** -0.5
    qb = q.reshape(B, nb, Q_BLOCK, GQ_KV_HEADS, G, GQ_HEAD_DIM).transpose(1, 0, 2, 3, 4, 5)

    def block(qblk):
        s = jnp.einsum('bqngd,bknd->bngqk', qblk, k).astype(jnp.float32) * scale
        p = jax.nn.softmax(s, axis=-1)
        return jnp.einsum('bngqk,bknd->bqngd', p.astype(v.dtype), v)

    out = lax.map(block, qb)
    return out.transpose(1, 0, 2, 3, 4, 5).reshape(B, S, GQ_Q)


def run_trunk(x, ple, w_in, da_lambda, da_norm, na_rpb, gq_q_norm, gq_k_norm,
              w_br_a, w_br_b, w_br_c, w_o, g_pre_mix, g_post_mix, g_pre_mlp, g_post_mlp,
              w_up, w_down, w_ple, w_ple_gate, g_ple):
    B, S = x.shape[0], x.shape[1]
    cos, sin = axial_rope_tables(S)
    h = x
    for i in range(DEPTH):
        lambda_init = 0.8 - 0.6 * math.exp(-0.3 * i)
        u = rmsnorm(h, g_pre_mix[i])
        z = u @ w_in[i]
        qa, ka, va, qn, kn, vn, qc, kc, vc, gz = split_cols(z)
        lq1, lk1, lq2, lk2 = (da_lambda[i, j].astype(jnp.float32) for j in range(4))
        lam = jnp.exp(jnp.sum(lq1 * lk1)) - jnp.exp(jnp.sum(lq2 * lk2)) + lambda_init
        oa = diff_attention(qa.reshape(B, S, DA_HEADS, 2, DA_HEAD_DIM),
                            ka.reshape(B, S, DA_HEADS, 2, DA_HEAD_DIM),
                            va.reshape(B, S, DA_HEADS, DA_V_DIM), lam)
        oa = (rmsnorm(oa, da_norm[i]) * (1.0 - lambda_init)).reshape(B, S, DA_V)
        ob = neighbourhood_attention(qn.reshape(B, S, NA_HEADS, NA_HEAD_DIM),
                                     kn.reshape(B, S, NA_HEADS, NA_HEAD_DIM),
                                     vn.reshape(B, S, NA_HEADS, NA_HEAD_DIM),
                                     na_rpb[i]).reshape(B, S, NA_W)
        qc = apply_rope(rmsnorm(qc.reshape(B, S, GQ_HEADS, GQ_HEAD_DIM), gq_q_norm[i]), cos, sin)
        kc = apply_rope(rmsnorm(kc.reshape(B, S, GQ_KV_HEADS, GQ_HEAD_DIM), gq_k_norm[i]), cos, sin)
        oc = gqa_attention(qc, kc, vc.reshape(B, S, GQ_KV_HEADS, GQ_HEAD_DIM))
        gates = jax.nn.sigmoid(gz.reshape(B, S, N_BRANCH, D_MODEL))
        merged = (gates[:, :, 0] * (oa @ w_br_a[i]) + gates[:, :, 1] * (ob @ w_br_b[i])
                  + gates[:, :, 2] * (oc @ w_br_c[i]))
        h = h + rmsnorm(merged @ w_o[i], g_post_mix[i])
        u = rmsnorm(h, g_pre_mlp[i])
        f = jnp.square(jax.nn.relu(u @ w_up[i])) @ w_down[i]
        h = h + rmsnorm(f, g_post_mlp[i])
        e = ple[i] @ w_ple[i]
        gate = jax.nn.sigmoid(h @ w_ple_gate[i])
        h = h + rmsnorm(e * gate, g_ple[i])
    return h


def setup_inputs(seed: int = 0) -> dict:
    key = jax.random.key(seed)
    ks = jax.random.split(key, 24)
    f32 = jnp.float32

    def nrm(k, shape, scale):
        return jax.random.normal(k, shape, f32) * scale

    def gain(k, shape):
        return 1.0 + 0.05 * jax.random.normal(k, shape, f32)

    return {
        "x_prompt": nrm(ks[0], (BATCH, SEQ, D_MODEL), 1.0),
        "x_sample": nrm(ks[1], (DEC_BATCH, DEC_SEQ, D_MODEL), 1.0),
        "p_prompt": nrm(ks[2], (DEPTH, BATCH, SEQ, PLE_DIM), 1.0),
        "p_sample": nrm(ks[3], (DEPTH, DEC_BATCH, DEC_SEQ, PLE_DIM), 1.0),
        "w_in": nrm(ks[4], (DEPTH, D_MODEL, IN_W), D_MODEL ** -0.5),
        "da_lambda": nrm(ks[5], (DEPTH, 4, DA_HEAD_DIM), 0.1),
        "da_norm": gain(ks[6], (DEPTH, DA_V_DIM)),
        "na_rpb": nrm(ks[7], (DEPTH, NA_HEADS, 2 * NA_ROWS_MAX - 1, 2 * NA_COLS - 1), 0.1),
        "gq_q_norm": gain(ks[8], (DEPTH, GQ_HEAD_DIM)),
        "gq_k_norm": gain(ks[9], (DEPTH, GQ_HEAD_DIM)),
        "w_br_a": nrm(ks[10], (DEPTH, DA_V, D_MODEL), DA_V ** -0.5),
        "w_br_b": nrm(ks[11], (DEPTH, NA_W, D_MODEL), NA_W ** -0.5),
        "w_br_c": nrm(ks[12], (DEPTH, GQ_Q, D_MODEL), GQ_Q ** -0.5),
        "w_o": nrm(ks[13], (DEPTH, D_MODEL, D_MODEL), D_MODEL ** -0.5),
        "g_pre_mix": gain(ks[14], (DEPTH, D_MODEL)),
        "g_post_mix": gain(ks[15], (DEPTH, D_MODEL)),
        "g_pre_mlp": gain(ks[16], (DEPTH, D_MODEL)),
        "g_post_mlp": gain(ks[17], (DEPTH, D_MODEL)),
        "w_up": nrm(ks[18], (DEPTH, D_MODEL, D_FF), D_MODEL ** -0.5),
        "w_down": nrm(ks[19], (DEPTH, D_FF, D_MODEL), D_FF ** -0.5),
        "w_ple": nrm(ks[20], (DEPTH, PLE_DIM, D_MODEL), PLE_DIM ** -0.5),
        "w_ple_gate": nrm(ks[21], (DEPTH, D_MODEL, D_MODEL), D_MODEL ** -0.5),
        "g_ple": gain(ks[22], (DEPTH, D_MODEL)),
    }


def reference(x_prompt, x_sample, p_prompt, p_sample, w_in, da_lambda, da_norm, na_rpb,
              gq_q_norm, gq_k_norm, w_br_a, w_br_b, w_br_c, w_o, g_pre_mix, g_post_mix,
              g_pre_mlp, g_post_mlp, w_up, w_down, w_ple, w_ple_gate, g_ple):
    y_prompt = run_trunk(x_prompt, p_prompt, w_in, da_lambda, da_norm, na_rpb, gq_q_norm, gq_k_norm,
                         w_br_a, w_br_b, w_br_c, w_o, g_pre_mix, g_post_mix, g_pre_mlp, g_post_mlp,
                         w_up, w_down, w_ple, w_ple_gate, g_ple)
    y_sample = run_trunk(x_sample, p_sample, w_in, da_lambda, da_norm, na_rpb, gq_q_norm, gq_k_norm,
                         w_br_a, w_br_b, w_br_c, w_o, g_pre_mix, g_post_mix, g_pre_mlp, g_post_mlp,
                         w_up, w_down, w_ple, w_ple_gate, g_ple)
    return (y_prompt, y_sample)
```

```python
import math
from contextlib import ExitStack
import numpy as np
import concourse.bass as bass
import concourse.mybir as mybir
from concourse.bass_utils import run_bass_kernel_spmd

F32 = mybir.dt.float32
BF16 = mybir.dt.bfloat16
AF = mybir.ActivationFunctionType
ALU = mybir.AluOpType
AX = mybir.AxisListType

D = 1024
DEPTH = 4
SEQ = 2048
DEC_SEQ = 8192
GRID_W = 64
EPS = 1e-6
IN_W = 10752
D_FF = 4096
PLE = 256
NEG_M = -30000.0
NEG_G = -3000.0
NA_E = 22


class _Stop(Exception):
    pass


class Buf:
    __slots__ = ("name", "w", "r")

    def __init__(self, name):
        self.name = name
        self.w = None
        self.r = {}


class Sched:
    ENG = ("pe", "act", "dve", "pool", "sp")

    def __init__(self, nc, stack):
        self.nc = nc
        self.eng = dict(pe=nc.tensor, act=nc.scalar, dve=nc.vector, pool=nc.gpsimd, sp=nc.sync)
        self.sem = {e: stack.enter_context(nc.semaphore("s_" + e)) for e in self.ENG}
        self.cnt = {e: 0 for e in self.ENG}
        self.NDS = 12
        self.dsem = {q: [stack.enter_context(nc.semaphore(f"d_{q}{i}")) for i in range(self.NDS)]
                     for q in ("sp", "pool")}
        self.dcnt = {q: [0] * self.NDS for q in ("sp", "pool")}
        self.drr = {q: 0 for q in ("sp", "pool")}
        self.waited = {e: {} for e in self.ENG}
        self.ops = {e: [] for e in self.ENG}
        self.cc_sem = stack.enter_context(nc.semaphore("s_cc"))
        self.cc_cnt = 0

    def _need(self, e, tick, waits):
        if tick is None:
            return
        key, val = tick
        if e == "pe" and key == "pe":
            return
        if self.waited[e].get(key, 0) >= val:
            return
        self.waited[e][key] = val
        waits.append((key, val))

    def _semof(self, key):
        if key == "cc":
            return self.cc_sem
        if key in self.sem:
            return self.sem[key]
        q, i = key
        return self.dsem[q][i]

    def _deps(self, e, reads, writes):
        waits = []
        for b in reads:
            self._need(e, b.w, waits)
        for b in writes:
            self._need(e, b.w, waits)
            for k, v in b.r.items():
                self._need(e, (k, v), waits)
        return waits

    def op(self, e, fn, reads=(), writes=(), signal=True):
        waits = self._deps(e, reads, writes) if (reads or writes) else []
        inc = None
        if signal:
            self.cnt[e] += 1
            tick = (e, self.cnt[e])
            inc = (e, 1)
            for b in writes:
                b.w = tick
                b.r = {}
            for b in reads:
                if b.r.get(e, 0) < tick[1]:
                    b.r[e] = tick[1]
        self.ops[e].append((fn, waits, inc))

    def dma(self, q, out, in_, reads=(), writes=()):
        waits = self._deps(q, reads, writes)
        i = self.drr[q]
        self.drr[q] = (i + 1) % self.NDS
        self.dcnt[q][i] += 16
        key = (q, i)
        tick = (key, self.dcnt[q][i])
        for b in writes:
            b.w = tick
            b.r = {}
        for b in reads:
            b.r[key] = tick[1]
        self.ops[q].append((lambda eng, o=out, s=in_: eng.dma_start(out=o, in_=s), waits, (key, 16)))

    def collective(self, fn, writes):
        self.cc_cnt += 1
        tick = ("cc", self.cc_cnt)
        for b in writes:
            b.w = tick
            b.r = {}
        self.ops["pool"].append((fn, [], ("cc", None)))

    def barrier(self):
        for e in self.ENG:
            waits = []
            for e2 in self.ENG:
                if e2 != e and self.cnt[e2] > 0:
                    self._need(e, (e2, self.cnt[e2]), waits)
            for q in ("sp", "pool"):
                for i in range(self.NDS):
                    if self.dcnt[q][i] > 0:
                        self._need(e, ((q, i), self.dcnt[q][i]), waits)
            if waits:
                self.ops[e].append((None, waits, None))

    def flush(self):
        nc = self.nc
        with nc.Block() as block:
            for e, deco in (("pe", block.tensor), ("act", block.scalar), ("dve", block.vector),
                            ("pool", block.gpsimd), ("sp", block.sync)):
                lst = self.ops[e]

                def body(eng, lst=lst):
                    for fn, waits, inc in lst:
                        for key, val in waits:
                            eng.wait_ge(self._semof(key), val)
                        if fn is not None:
                            ins = fn(eng)
                            if inc is not None:
                                if inc[1] is None:
                                    ins.then_inc(self._semof(inc[0]))
                                else:
                                    ins.then_inc(self._semof(inc[0]), inc[1])
                deco(body)
        self.ops = {e: [] for e in self.ENG}


def emit_pipelined(tasks, LOOK=2, PRE=24, DEFER=4):
    n = len(tasks)
    state = {"pre": 0}

    def do_pre(upto):
        while state["pre"] < min(upto, n):
            p = tasks[state["pre"]][0]
            if p:
                p()
            state["pre"] += 1

    deferred = []
    do_pre(PRE)
    for i in range(min(LOOK, n)):
        tasks[i][1]()
    for i in range(n):
        do_pre(i + PRE + 1)
        tasks[i][2]()
        if i + LOOK < n:
            tasks[i + LOOK][1]()
        tasks[i][3]()
        if tasks[i][4]:
            d = tasks[i][4]()
            if d:
                deferred.append((i + DEFER, d))
        while deferred and deferred[0][0] <= i:
            deferred.pop(0)[1]()
    for _, d in deferred:
        d()


class Rot:
    def __init__(self, tiles):
        self.tiles = tiles
        self.bufs = [Buf("rot") for _ in tiles]
        self.i = 0

    def next(self):
        i = self.i
        self.i = (i + 1) % len(self.tiles)
        return self.tiles[i], self.bufs[i]


def build_program(depth, segs, stop_after=None):
    nc = bass.Bass("TRN2", target_bir_lowering=False)
    NT = sum(s for _, s, _r in segs)
    NKV = sum(s * r for _, s, r in segs)
    seg_off, kv_off = [], []
    o = ko = 0
    for _, s, r in segs:
        seg_off.append(o)
        kv_off.append(ko)
        o += s
        ko += s * r

    def din(name, shape, dt=F32):
        return nc.dram_tensor(name, list(shape), dt, kind="ExternalInput").ap()

    def dscr(name, shape, dt=BF16):
        import os as _os2
        if name in _os2.environ.get("KDBG_DUMP", "").split(","):
            return nc.dram_tensor(name, list(shape), dt, kind="ExternalOutput").ap()
        return nc.dram_tensor(name, list(shape), dt).ap()

    x_in = din("x_in", [NT, D])
    p_in = din("p_in", [depth, NT, PLE])
    w_in = din("w_in", [depth, D, IN_W])
    da_lambda = din("da_lambda", [depth, 256])
    da_norm = din("da_norm", [depth, 128])
    gq_q_norm = din("gq_q_norm", [depth, 128])
    gq_k_norm = din("gq_k_norm", [depth, 128])
    w_br = [din("w_br_a", [depth, D, D]), din("w_br_b", [depth, D, D]), din("w_br_c", [depth, D, D])]
    w_o = din("w_o", [depth, D, D])
    g_pre_mix = din("g_pre_mix", [depth, D])
    g_post_mix = din("g_post_mix", [depth, D])
    g_pre_mlp = din("g_pre_mlp", [depth, D])
    g_post_mlp = din("g_post_mlp", [depth, D])
    w_up = din("w_up", [depth, D, D_FF])
    w_down = din("w_down", [depth, D_FF, D])
    w_ple = din("w_ple", [depth, PLE, D])
    w_ple_gate = din("w_ple_gate", [depth, D, D])
    g_ple = din("g_ple", [depth, D])
    cs_tab = din("cs_tab", [NT, 64])
    sn_tab = din("sn_tab", [NT, 64])
    qaug = din("qaug", [2, 5, NT])
    kaug = din("kaug", [8, 5, NKV])
    dcorr = din("dcorr", [len(segs), 8, 128, 128])
    dcorr2 = din("dcorr2", [8, 4, 128, 128])
    na_g = din("na_g", [depth, 16, 128, NA_E * 64])
    na_mq = din("na_mq", [3, 2, 8, 512])
    na_ki = din("na_ki", [2, 1024])
    has_S = any(r > 1 for _, _, r in segs)
    if has_S:
        na_gs = din("na_gs", [depth, 16, 128, 4, 1152])
        na_mqs = din("na_mqs", [3, 8, 6, 512])
        na_kis = din("na_kis", [8, 768])
    idents = din("idents", [2, 128, 128])

    y_out = nc.dram_tensor("y_out", [NT, D], F32, kind="ExternalOutput").ap()

    h_d = dscr("h_d", [NT, D], F32)
    qaT_d = dscr("qaT_d", [16, 64, NT])
    qnT_d = dscr("qnT_d", [16, 64, NT])
    qcT_d = dscr("qcT_d", [8, 128, NT])
    KVR = 4608

    def kvviews(a2):
        return dict(
            kaT=a2[0:1024, :].rearrange("(h d) t -> h d t", d=64),
            knT=a2[1024:2048, :].rearrange("(h d) t -> h d t", d=64),
            kcT=a2[2048:2304, :].rearrange("(h d) t -> h d t", d=128),
            va=a2[2304:3328, :].rearrange("r (two c) -> (r two) c", two=2),
            vn=a2[3328:4352, :].rearrange("r (two c) -> (r two) c", two=2),
            vc=a2[4352:4608, :].rearrange("r (e c) -> (r e) c", e=8),
        )

    class LocalKV:
        def __init__(self, slab):
            self.v = kvviews(slab)

        def kaT(self, hm):
            return self.v["kaT"][hm]

        def knT(self, h, a, b):
            return self.v["knT"][h, :, a:b]

        def kcT(self, n):
            return self.v["kcT"][n]

        def vpieces(self, name, ta, tb, ca, cb):
            return [(0, (tb - ta) // 128, self.v[name][ta:tb, ca:cb].rearrange("(k p) e -> p k e", p=128))]

    class GatheredKV:
        def __init__(self, allbuf, rho, nr):
            self.a, self.rho, self.nr = allbuf, rho, nr

        def blk(self, i):
            r0 = (i * self.nr + self.rho) * 128
            return self.a[r0:r0 + 128, :]

        def kaT(self, hm):
            return self.blk(hm // 2)[(hm % 2) * 64:(hm % 2) * 64 + 64, :]

        def knT(self, h, a, b):
            return self.blk(8 + h // 2)[(h % 2) * 64:(h % 2) * 64 + 64, a:b]

        def kcT(self, n):
            return self.blk(16 + n)

        def vpieces(self, name, ta, tb, ca, cb):
            base, tpb = {"va": (18, 256), "vn": (26, 256), "vc": (34, 1024)}[name]
            out = []
            for j in range(ta // tpb, (tb + tpb - 1) // tpb):
                a = max(ta, j * tpb)
                b = min(tb, (j + 1) * tpb)
                if name == "vc":
                    v = self.blk(base + j).rearrange("r (e c) -> (r e) c", e=8)
                else:
                    v = self.blk(base + j).rearrange("r (two c) -> (r two) c", two=2)
                out.append(((a - ta) // 128, (b - a) // 128, v[a - j * tpb:b - j * tpb, ca:cb].rearrange("(k p) e -> p k e", p=128)))
            return out

    seg_dst, seg_src = [], []
    b_gath = Buf("gathered")
    kv_src = kv_all = None
    NBLK = KVR // 128
    for si, (sname, s, r) in enumerate(segs):
        if r == 1:
            loc = dscr(f"kv_loc{si}", [KVR, 2048])
            seg_dst.append(kvviews(loc))
            seg_src.append([LocalKV(loc)])
        else:
            kv_src = dscr("kv_src", [KVR, 2048])
            kv_all = dscr("kv_all", [r * KVR, 2048])
            seg_dst.append(kvviews(kv_src))
            seg_src.append([GatheredKV(kv_all, q, r) for q in range(r)])

    gT_d = dscr("gT_d", [24, 128, NT])
    oT_d = [dscr("oaT_d", [8, 128, NT]), dscr("obT_d", [8, 128, NT]), dscr("ocT_d", [8, 128, NT])]
    u2T_d = dscr("u2T_d", [8, 128, NT])
    aT_d = dscr("aT_d", [32, 128, NT])

    top = ExitStack()
    with top:
        S = Sched(nc, top)
        E = S.eng

        uid = [0]

        def sb(st, name, shape, dt):
            uid[0] += 1
            return st.enter_context(nc.sbuf_tensor(f"{name}_{uid[0]}", list(shape), dt))

        def ps(st, name, shape, dt=F32):
            uid[0] += 1
            return st.enter_context(nc.psum_tensor(f"{name}_{uid[0]}", list(shape), dt))

        ident = sb(top, "ident", [128, 128], BF16)
        ident8 = sb(top, "ident8", [128, 128], BF16)
        gvec = sb(top, "gvec", [128, 5, D], F32)
        gsm = sb(top, "gsm", [128, 3, 128], F32)
        lamt = sb(top, "lamt", [128, 256], F32)
        lam = sb(top, "lam", [128, 8], F32)
        b_const = Buf("const")
        b_gv = Buf("gvec")
        b_lam = Buf("lam")
        epsc = sb(top, "epsc", [128, 1], F32)
        S.op("pool", lambda e: e.memset(epsc[:], EPS), writes=[b_const])
        S.dma("pool", ident[:], idents[0], writes=[b_const])
        S.dma("pool", ident8[:], idents[1], writes=[b_const])

        def load_layer_consts(l):
            for i, g in enumerate((g_pre_mix, g_post_mix, g_pre_mlp, g_post_mlp, g_ple)):
                S.dma("sp", gvec[:, i, :], g[l:l + 1, :].partition_broadcast(128), writes=[b_gv])
            for i, g in enumerate((da_norm, gq_q_norm, gq_k_norm)):
                S.dma("sp", gsm[:, i, :], g[l:l + 1, :].partition_broadcast(128), writes=[b_gv])
            S.dma("sp", lamt[:], da_lambda[l:l + 1, :].partition_broadcast(128), writes=[b_lam])
            li = 0.8 - 0.6 * math.exp(-0.3 * l)
            S.op("dve", lambda e: e.tensor_tensor(out=lamt[:, 0:64], in0=lamt[:, 0:64], in1=lamt[:, 64:128], op=ALU.mult),
                 reads=[b_lam], writes=[b_lam])
            S.op("dve", lambda e: e.tensor_tensor(out=lamt[:, 128:192], in0=lamt[:, 128:192], in1=lamt[:, 192:256], op=ALU.mult),
                 reads=[b_lam], writes=[b_lam])
            S.op("dve", lambda e: e.tensor_reduce(out=lam[:, 0:1], in_=lamt[:, 0:64], axis=AX.X, op=ALU.add),
                 reads=[b_lam], writes=[b_lam])
            S.op("dve", lambda e: e.tensor_reduce(out=lam[:, 1:2], in_=lamt[:, 128:192], axis=AX.X, op=ALU.add),
                 reads=[b_lam], writes=[b_lam])
            S.op("act", lambda e: e.activation(out=lam[:, 3:5], in_=lam[:, 0:2], func=AF.Exp), reads=[b_lam], writes=[b_lam])
            S.op("dve", lambda e: e.tensor_tensor(out=lam[:, 2:3], in0=lam[:, 3:4], in1=lam[:, 4:5], op=ALU.subtract),
                 reads=[b_lam], writes=[b_lam])
            S.op("dve", lambda e: e.tensor_scalar(out=lam[:, 5:6], in0=lam[:, 2:3], scalar1=li, scalar2=-1.0, op0=ALU.add, op1=ALU.mult),
                 reads=[b_lam], writes=[b_lam])
            return li

        def rstd_from_ss(ss_ap, out_ap, n, bufs, post_mul=None):
            S.op("act", lambda e: e.activation(out=out_ap, in_=ss_ap, func=AF.Ln, scale=1.0 / n, bias=epsc[:, 0:1]),
                 reads=list(bufs) + [b_const], writes=bufs)
            S.op("act", lambda e: e.activation(out=out_ap, in_=out_ap, func=AF.Exp, scale=-0.5), reads=bufs, writes=bufs)
            if post_mul is not None:
                S.op("dve", lambda e: e.tensor_scalar(out=out_ap, in0=out_ap, scalar1=float(post_mul), scalar2=None, op0=ALU.mult),
                     reads=bufs, writes=bufs)

        def pe_seq(fns, reads, writes):
            waits = S._deps("pe", reads, writes)
            n = len(fns)
            S.cnt["pe"] += 1
            tick = ("pe", S.cnt["pe"])
            for b in writes:
                b.w = tick
                b.r = {}
            for b in reads:
                if b.r.get("pe", 0) < tick[1]:
                    b.r["pe"] = tick[1]
            for i, fn in enumerate(fns):
                S.ops["pe"].append((fn, waits if i == 0 else [], ("pe", 1) if i == n - 1 else None))

        def mm_group(mms, reads, writes):
            n = len(mms)
            pe_seq([(lambda e, o=o, l=l, r=r, a=(i == 0), z=(i == n - 1): e.matmul(o, l, r, start=a, stop=z))
                    for i, (o, l, r) in enumerate(mms)], reads, writes)

        def maybe_stop(tag):
            if stop_after == tag:
                S.dma("sp", y_out[:, :], x_in[:, :])
                S.barrier()
                S.flush()
                raise _Stop(nc)

        for l in range(depth):
            li = load_layer_consts(l)
            src_h = x_in if l == 0 else h_d
            dst_final = y_out if l == depth - 1 else h_d

            with ExitStack() as st:
                uT = sb(st, "uT", [128, 8, 2048], BF16)
                hb = [sb(st, f"hb{i}", [128, D], F32) for i in range(2)]
                junk = sb(st, "junkA", [128, D], F32)
                ub = [sb(st, f"ub{i}", [128, D], BF16) for i in range(2)]
                ssA = sb(st, "ssA", [128, 4], F32)
                wb = [sb(st, f"wb{i}", [128, 8, 512], BF16) for i in range(3)]
                stg = [sb(st, f"stg{i}", [128, 512], BF16) for i in range(4)]
                xs_f = [sb(st, f"xsf{i}", [128, 4, 128], F32) for i in range(2)]
                sq_f = sb(st, "sqf", [128, 4, 128], F32)
                xr_b = [sb(st, f"xrb{i}", [128, 4, 128], BF16) for i in range(2)]
                t_a = sb(st, "t_a", [128, 4, 64], F32)
                t_b = sb(st, "t_b", [128, 4, 64], F32)
                ss4 = sb(st, "ss4", [128, 8], F32)
                cst = sb(st, "cst", [128, 16, 64], F32)
                snt = sb(st, "snt", [128, 16, 64], F32)
                stT = [sb(st, f"stT{i}", [128, 4, 128], BF16) for i in range(2)]
                pA = [ps(st, f"pA{i}", [128, 512]) for i in range(4)]
                pT = [ps(st, f"pTA{i}", [128, 8, 128], BF16) for i in range(2)]
                r_hb = Rot(hb); r_ub = Rot(ub); r_wb = Rot(wb); r_stg = Rot(stg)
                r_pA = Rot(pA); r_pT = Rot(pT); r_xs = Rot(xs_f); r_xr = Rot(xr_b); r_stT = Rot(stT)
                b_uT = Buf("uT"); b_junk = Buf("junk"); b_ss = Buf("ssA"); b_tmp = Buf("tmpA"); b_cs = Buf("cs")
                evac_i = [0]

                def evac_engine():
                    evac_i[0] += 1
                    return "act" if evac_i[0] % 2 else "dve"

                def copy_op(eng, out, in_, reads, writes):
                    if eng == "act":
                        S.op("act", lambda e: e.copy(out=out, in_=in_), reads=reads, writes=writes)
                    else:
                        S.op(eng, lambda e: e.tensor_copy(out=out, in_=in_), reads=reads, writes=writes)

                for sc in range(len(segs)):
                    t0 = seg_off[sc]
                    dv = seg_dst[sc]
                    S.dma("sp", cst[:], cs_tab[t0:t0 + 2048, :].rearrange("(t p) f -> p t f", p=128), writes=[b_cs])
                    S.dma("sp", snt[:], sn_tab[t0:t0 + 2048, :].rearrange("(t p) f -> p t f", p=128), writes=[b_cs])
                    for tt in range(16):
                        ht, hbuf = r_hb.next()
                        S.dma("sp", ht[:], src_h[t0 + tt * 128:t0 + (tt + 1) * 128, :], writes=[hbuf])
                        S.op("act", lambda e, ht=ht: e.activation(out=junk[:], in_=ht[:], func=AF.Square, accum_out=ssA[:, 0:1]),
                             reads=[hbuf], writes=[b_junk, b_ss])
                        rstd_from_ss(ssA[:, 0:1], ssA[:, 1:2], D, [b_ss])
                        ut, ubuf = r_ub.next()
                        S.op("dve", lambda e, ht=ht, ut=ut: e.scalar_tensor_tensor(out=ut[:], in0=ht[:], scalar=ssA[:, 1:2], in1=gvec[:, 0, :],
                                                                                 op0=ALU.mult, op1=ALU.mult),
                             reads=[hbuf, b_ss, b_gv], writes=[ubuf])
                        pt, pbuf = r_pT.next()
                        pe_seq([(lambda e, pt=pt, ut=ut, c=c: e.transpose(pt[:, c, :], ut[:, c * 128:(c + 1) * 128], ident[:])) for c in range(8)],
                               [ubuf, b_const], [pbuf])
                        copy_op(evac_engine(), uT[:, :, tt * 128:(tt + 1) * 128], pt[:], [pbuf], [b_uT])

                    import os as _os
                    for blk in [int(v) for v in _os.environ.get('KDBG_BLKS', ','.join(map(str, range(21)))).split(',') if v != '']:
                        wt, wbuf = r_wb.next()
                        S.dma("pool", wt[:], w_in[l].rearrange("(c p) n -> p c n", p=128)[:, :, blk * 512:(blk + 1) * 512],
                              writes=[wbuf])
                        if blk in (0, 1, 2, 3, 6, 7, 8, 9):
                            dstT = {0: qaT_d, 1: qaT_d, 2: dv["kaT"], 3: dv["kaT"], 6: qnT_d, 7: qnT_d, 8: dv["knT"], 9: dv["knT"]}[blk]
                            tb0 = t0 if blk in (0, 1, 6, 7) else 0
                            for sub in range(8):
                                hm = (blk % 2) * 8 + sub
                                for tq in range(4):
                                    pa, pab = r_pA.next()
                                    mm_group([(pa[0:64, :], wt[:, c, sub * 64:(sub + 1) * 64], uT[:, c, tq * 512:(tq + 1) * 512])
                                              for c in range(8)], [wbuf, b_uT], [pab])
                                    sg, sgb = r_stg.next()
                                    copy_op(evac_engine(), sg[0:64, :], pa[0:64, :], [pab], [sgb])
                                    S.dma("sp", dstT[hm, :, tb0 + tq * 512:tb0 + (tq + 1) * 512], sg[0:64, :], reads=[sgb])
                        elif blk >= 15:
                            for sub in range(4):
                                ch = (blk - 15) * 4 + sub
                                for tq in range(4):
                                    pa, pab = r_pA.next()
                                    mm_group([(pa[:, :], wt[:, c, sub * 128:(sub + 1) * 128], uT[:, c, tq * 512:(tq + 1) * 512])
                                              for c in range(8)], [wbuf, b_uT], [pab])
                                    sg, sgb = r_stg.next()
                                    S.op("act", lambda e, sg=sg, pa=pa: e.activation(out=sg[:], in_=pa[:], func=AF.Sigmoid),
                                         reads=[pab], writes=[sgb])
                                    S.dma("sp", gT_d[ch, :, t0 + tq * 512:t0 + (tq + 1) * 512], sg[:], reads=[sgb])
                        elif blk in (4, 5, 10, 11):
                            dstV = dv["va"] if blk < 6 else dv["vn"]
                            c0 = (blk % 2) * 512
                            for tt in range(16):
                                pa, pab = r_pA.next()
                                mm_group([(pa[:, :], uT[:, c, tt * 128:(tt + 1) * 128], wt[:, c, :]) for c in range(8)],
                                         [wbuf, b_uT], [pab])
                                sg, sgb = r_stg.next()
                                copy_op(evac_engine(), sg[:], pa[:], [pab], [sgb])
                                S.dma("sp", dstV[tt * 128:(tt + 1) * 128, c0:c0 + 512], sg[:], reads=[sgb])
                        else:
                            for tt in range(16):
                                pa, pab = r_pA.next()
                                mm_group([(pa[:, :], uT[:, c, tt * 128:(tt + 1) * 128], wt[:, c, :]) for c in range(8)],
                                         [wbuf, b_uT], [pab])
                                nh = 4 if blk < 14 else 2
                                gi = 1 if blk < 14 else 2
                                xs_, xsb = r_xs.next()
                                S.op("act", lambda e, xs_=xs_, pa=pa, nh=nh: e.copy(out=xs_[:, 0:nh, :], in_=pa[:, 0:nh * 128].rearrange("p (h d) -> p h d", h=nh)),
                                     reads=[pab], writes=[xsb])
                                if blk == 14 and not _os.environ.get('KDBG_SKIPV'):
                                    sg, sgb = r_stg.next()
                                    S.op("act", lambda e, sg=sg, pa=pa: e.copy(out=sg[:, 0:256], in_=pa[:, 256:512]), reads=[pab], writes=[sgb])
                                    S.dma("sp", dv["vc"][tt * 128:(tt + 1) * 128, :], sg[:, 0:256], reads=[sgb])
                                S.op("dve", lambda e, xs_=xs_, nh=nh: e.tensor_tensor(out=sq_f[:, 0:nh, :], in0=xs_[:, 0:nh, :], in1=xs_[:, 0:nh, :], op=ALU.mult),
                                     reads=[xsb], writes=[b_tmp])
                                S.op("dve", lambda e, nh=nh: e.tensor_reduce(out=ss4[:, 0:nh], in_=sq_f[:, 0:nh, :], axis=AX.X, op=ALU.add),
                                     reads=[b_tmp], writes=[b_tmp])
                                rstd_from_ss(ss4[:, 0:nh], ss4[:, 4:4 + nh], 128, [b_tmp])
                                S.op("dve", lambda e, xs_=xs_, nh=nh: e.tensor_tensor(out=xs_[:, 0:nh, :], in0=xs_[:, 0:nh, :],
                                                                                    in1=ss4[:, 4:4 + nh].unsqueeze(2).to_broadcast([128, nh, 128]), op=ALU.mult),
                                     reads=[xsb, b_tmp], writes=[xsb])
                                S.op("dve", lambda e, xs_=xs_, nh=nh, gi=gi: e.tensor_tensor(out=xs_[:, 0:nh, :], in0=xs_[:, 0:nh, :],
                                                                                           in1=gsm[:, gi:gi + 1, :].to_broadcast([128, nh, 128]), op=ALU.mult),
                                     reads=[xsb, b_gv], writes=[xsb])
                                xr, xrb = r_xr.next()
                                x0 = xs_[:, 0:nh, :].rearrange("p h (i two) -> p h i two", two=2)[:, :, :, 0]
                                x1 = xs_[:, 0:nh, :].rearrange("p h (i two) -> p h i two", two=2)[:, :, :, 1]
                                o0 = xr[:, 0:nh, :].rearrange("p h (i two) -> p h i two", two=2)[:, :, :, 0]
                                o1 = xr[:, 0:nh, :].rearrange("p h (i two) -> p h i two", two=2)[:, :, :, 1]
                                cb = cst[:, tt:tt + 1, :].to_broadcast([128, nh, 64])
                                sbb = snt[:, tt:tt + 1, :].to_broadcast([128, nh, 64])
                                ta = t_a[:, 0:nh, :]
                                tb = t_b[:, 0:nh, :]
                                S.op("dve", lambda e, ta=ta, x0=x0, cb=cb: e.tensor_tensor(out=ta, in0=x0, in1=cb, op=ALU.mult), reads=[xsb, b_cs], writes=[b_tmp])
                                S.op("dve", lambda e, tb=tb, x1=x1, sbb=sbb: e.tensor_tensor(out=tb, in0=x1, in1=sbb, op=ALU.mult), reads=[xsb, b_cs, b_tmp], writes=[b_tmp])
                                S.op("dve", lambda e, ta=ta, tb=tb, o0=o0: e.tensor_tensor(out=o0, in0=ta, in1=tb, op=ALU.subtract), reads=[b_tmp], writes=[xrb])
                                S.op("dve", lambda e, ta=ta, x0=x0, sbb=sbb: e.tensor_tensor(out=ta, in0=x0, in1=sbb, op=ALU.mult), reads=[xsb, b_cs, xrb], writes=[b_tmp])
                                S.op("dve", lambda e, tb=tb, x1=x1, cb=cb: e.tensor_tensor(out=tb, in0=x1, in1=cb, op=ALU.mult), reads=[xsb, b_cs, b_tmp], writes=[b_tmp])
                                S.op("dve", lambda e, ta=ta, tb=tb, o1=o1: e.tensor_tensor(out=o1, in0=ta, in1=tb, op=ALU.add), reads=[b_tmp, xrb], writes=[xrb])
                                pt, pbuf = r_pT.next()
                                pe_seq([(lambda e, pt=pt, xr=xr, hh=hh: e.transpose(pt[:, hh, :], xr[:, hh, :], ident[:])) for hh in range(nh)],
                                       [xrb, b_const], [pbuf])
                                sT, sTb = r_stT.next()
                                copy_op(evac_engine(), sT[:, 0:nh, :], pt[:, 0:nh, :], [pbuf], [sTb])
                                dT = qcT_d[(blk - 12) * 4:(blk - 12) * 4 + 4] if blk < 14 else dv["kcT"]
                                tb0 = t0 if blk < 14 else 0
                                S.dma("sp", dT[:, :, tb0 + tt * 128:tb0 + (tt + 1) * 128].rearrange("h d t -> d h t"), sT[:, 0:nh, :], reads=[sTb])
                S.barrier()
                S.flush()

            def attention(kind):
                with ExitStack() as st:
                    Smax = max(s * r for _, s, r in segs)
                    nkmax = Smax // 128
                    KR = 69 if kind == "da" else 128
                    nmap = 2 if kind == "da" else 1
                    KT = [[sb(st, f"KT{i}{m}", [KR, Smax], BF16) for m in range(nmap)] for i in range(2)]
                    VT = [sb(st, f"VT{i}", [128, nkmax, 129], BF16) for i in range(2)]
                    b_KV = [Buf("kv0"), Buf("kv1")]
                    b_dc = Buf("dc")
                    if kind == "da":
                        QT = [sb(st, f"QT{i}", [KR, 2, 2, 512], BF16) for i in range(2)]
                        dct = sb(st, "dct", [128, len(segs), 8, 128], BF16)
                        for si_ in range(len(segs)):
                            S.dma("pool", dct[:, si_], dcorr[si_].rearrange("h k q -> k h q"), writes=[b_dc])
                        if has_S:
                            dct2 = sb(st, "dct2", [128, 8, 4, 128], BF16)
                            S.dma("pool", dct2[:], dcorr2.rearrange("h r k q -> k h r q"), writes=[b_dc])
                    else:
                        QT = [sb(st, f"QT{i}", [KR, 512], BF16) for i in range(3)]
                    r_QT = Rot(QT)
                    PT = [sb(st, f"PT{i}", [128, 512], BF16) for i in range(3)]
                    r_PT = Rot(PT)
                    Of = [sb(st, f"Of{i}", [128, 4, 129], F32) for i in range(4)]
                    r_Of = Rot(Of)
                    rr = sb(st, "rr", [128, 16], F32)
                    oc = sb(st, "oc", [128, 4, 128], F32)
                    oc2 = sb(st, "oc2", [128, 4, 128], F32)
                    on = [sb(st, f"on{i}", [128, 4, 128], BF16) for i in range(2)]
                    r_on = Rot(on)
                    sto = [sb(st, f"sto{i}", [128, 512], BF16) for i in range(2)]
                    r_sto = Rot(sto)
                    b_post = Buf("post")
                    pS = [ps(st, f"pS{i}", [128, 512]) for i in range(3)]
                    r_pS = Rot(pS)
                    pO = [ps(st, f"pO{i}", [128, 512])[:, 0:258].rearrange("p (s e) -> p s e", s=2) for i in range(4)]
                    pObuf = [Buf("pOa"), Buf("pOb")]
                    pTt = ps(st, "pTt", [128, 8, 128], BF16)[:, 0:4, :]
                    b_pTt = Buf("pTt")
                    for i in range(2):
                        S.op("pool", lambda e, i=i: e.memset(VT[i][:], 1.0), writes=[b_KV[i]])
                    scale = 0.125 if kind == "da" else 128.0 ** -0.5
                    tasks = []
                    it = [0]
                    pv_bank = [0]

                    def finish(grp, qh, q0):
                        ont, onb = r_on.next()
                        if kind == "da":
                            (o1, b1), (o2, b2) = grp
                            S.op("dve", lambda e: e.reciprocal(out=rr[:, 0:4], in_=o1[:, :, 128]), reads=[b1], writes=[b_post])
                            S.op("dve", lambda e: e.reciprocal(out=rr[:, 4:8], in_=o2[:, :, 128]), reads=[b2, b_post], writes=[b_post])
                            S.op("dve", lambda e: e.tensor_scalar(out=rr[:, 4:8], in0=rr[:, 4:8], scalar1=lam[:, 5:6], scalar2=None, op0=ALU.mult),
                                 reads=[b_post, b_lam], writes=[b_post])
                            S.op("dve", lambda e: e.tensor_tensor(out=oc[:], in0=o1[:, :, 0:128], in1=rr[:, 0:4].unsqueeze(2).to_broadcast([128, 4, 128]), op=ALU.mult),
                                 reads=[b1, b_post], writes=[b_post])
                            S.op("dve", lambda e: e.tensor_tensor(out=oc2[:], in0=o2[:, :, 0:128], in1=rr[:, 4:8].unsqueeze(2).to_broadcast([128, 4, 128]), op=ALU.mult),
                                 reads=[b2, b_post], writes=[b_post])
                            S.op("dve", lambda e: e.tensor_tensor(out=oc[:], in0=oc[:], in1=oc2[:], op=ALU.add), reads=[b_post], writes=[b_post])
                            S.op("dve", lambda e: e.tensor_tensor(out=oc2[:], in0=oc[:], in1=oc[:], op=ALU.mult), reads=[b_post], writes=[b_post])
                            S.op("dve", lambda e: e.tensor_reduce(out=rr[:, 8:12], in_=oc2[:], axis=AX.X, op=ALU.add), reads=[b_post], writes=[b_post])
                            rstd_from_ss(rr[:, 8:12], rr[:, 12:16], 128, [b_post], post_mul=(1.0 - li))
                            S.op("dve", lambda e: e.tensor_tensor(out=oc[:], in0=oc[:], in1=rr[:, 12:16].unsqueeze(2).to_broadcast([128, 4, 128]), op=ALU.mult),
                                 reads=[b_post], writes=[b_post])
                            S.op("dve", lambda e: e.tensor_tensor(out=ont[:], in0=oc[:], in1=gsm[:, 0:1, :].to_broadcast([128, 4, 128]), op=ALU.mult),
                                 reads=[b_post, b_gv], writes=[onb])
                        else:
                            (o1, b1), = grp
                            S.op("dve", lambda e: e.reciprocal(out=rr[:, 0:4], in_=o1[:, :, 128]), reads=[b1], writes=[b_post])
                            S.op("dve", lambda e: e.tensor_tensor(out=ont[:], in0=o1[:, :, 0:128], in1=rr[:, 0:4].unsqueeze(2).to_broadcast([128, 4, 128]), op=ALU.mult),
                                 reads=[b1, b_post], writes=[onb])
                        def part2():
                            pe_seq([(lambda e, s=s: e.transpose(pTt[:, s, :], ont[:, s, :], ident[:])) for s in range(4)], [onb, b_const], [b_pTt])
                            so, sob = r_sto.next()
                            S.op("act", lambda e: e.copy(out=so[:], in_=pTt[:].rearrange("p s q -> p (s q)")), reads=[b_pTt], writes=[sob])
                            dst = oT_d[0] if kind == "da" else oT_d[2]
                            S.dma("sp", dst[qh, :, q0:q0 + 512], so[:], reads=[sob])
                        return part2

                    for si, (sname, NQ, RK) in enumerate(segs):
                        tok0 = seg_off[si]
                        ko = kv_off[si]
                        SL = NQ * RK
                        nkc = SL // 128
                        nqc = NQ // 512
                        srcs = seg_src[si]
                        gdep = [b_gath] if RK > 1 else []
                        nkvh = 8 if kind == "da" else 2
                        for kvh in range(nkvh):
                            slot = it[0] % 2
                            it[0] += 1
                            kvb = b_KV[slot]

                            def load_kv(kvh=kvh, slot=slot, kvb=kvb, SL=SL, srcs=srcs, gdep=gdep, ko=ko, RK=RK):
                                for rho in range(RK):
                                    c0, c1 = rho * 2048, (rho + 1) * 2048
                                    if kind == "da":
                                        for m in range(2):
                                            S.dma("sp", KT[slot][m][0:64, c0:c1], srcs[rho].kaT(kvh * 2 + m), reads=gdep, writes=[kvb])
                                        vname = "va"
                                    else:
                                        S.dma("sp", KT[slot][0][:, c0:c1], srcs[rho].kcT(kvh), reads=gdep, writes=[kvb])
                                        vname = "vc"
                                    for (ko_, nk_, vap) in srcs[rho].vpieces(vname, 0, 2048, kvh * 128, (kvh + 1) * 128):
                                        S.dma("sp", VT[slot][:, rho * 16 + ko_:rho * 16 + ko_ + nk_, 0:128], vap, reads=gdep, writes=[kvb])
                                if kind == "da":
                                    for m in range(2):
                                        S.dma("pool", KT[slot][m][64:69, 0:SL], kaug[kvh, :, ko:ko + SL], writes=[kvb])

                            qheads = [kvh] if kind == "da" else [kvh * 4 + g for g in range(4)]
                            first = [True]
                            for qh in qheads:
                                for qc in range(nqc):
                                    q0 = tok0 + qc * 512
                                    qt, qb_ = r_QT.next()

                                    def load_q(qt=qt, qb_=qb_, qh=qh, q0=q0):
                                        if kind == "da":
                                            for m in range(2):
                                                for lr in range(2):
                                                    S.dma("sp", qt[0:64, m, lr, :], qaT_d[qh * 2 + m, :, q0:q0 + 512], writes=[qb_])
                                                    S.dma("pool", qt[64:69, m, lr, :], qaug[lr, :, q0:q0 + 512], writes=[qb_])
                                        else:
                                            S.dma("sp", qt[:, :], qcT_d[qh, :, q0:q0 + 512], writes=[qb_])

                                    grp_Of = []
                                    for m in range(nmap):
                                        for kc in range(nkc):
                                            last = (kc == nkc - 1)
                                            state = {}

                                            def qk(kc=kc, m=m, qt=qt, qb_=qb_, slot=slot, kvb=kvb, qc=qc, state=state, kvh=kvh, si=si, RK=RK):
                                                pst, psb = r_pS.next()
                                                state["ps"] = (pst, psb)
                                                kt = KT[slot][m]
                                                if kind != "da":
                                                    mm_group([(pst[:, :], kt[:, kc * 128:(kc + 1) * 128], qt[:, :])], [kvb, qb_], [psb])
                                                    return
                                                rho, c = kc // 16, kc % 16
                                                if c < 4 * qc or c >= 4 * qc + 4:
                                                    lr = 0 if c < 4 * qc else 1
                                                    mm_group([(pst[:, :], kt[0:69, kc * 128:(kc + 1) * 128], qt[0:69, m, lr, :])], [kvb, qb_], [psb])
                                                    return
                                                t = c - 4 * qc
                                                fns = []
                                                for s in range(4):
                                                    lr = 0 if s >= t else 1
                                                    fns.append(lambda e, pst=pst, kt=kt, qt=qt, s=s, lr=lr, d=(s == t): e.matmul(
                                                        pst[:, s * 128:(s + 1) * 128], kt[0:69, kc * 128:(kc + 1) * 128], qt[0:69, m, lr, s * 128:(s + 1) * 128],
                                                        start=True, stop=not d))
                                                    if s == t:
                                                        two = RK > 1
                                                        fns.append(lambda e, pst=pst, s=s, two=two: e.matmul(pst[:, s * 128:(s + 1) * 128], ident[:], dct[:, si, kvh, :],
                                                                                                             start=False, stop=not two))
                                                        if two:
                                                            fns.append(lambda e, pst=pst, s=s, rho=rho: e.matmul(pst[:, s * 128:(s + 1) * 128], ident[:], dct2[:, kvh, rho, :],
                                                                                                                 start=False, stop=True))
                                                pe_seq(fns, [kvb, qb_, b_dc, b_const], [psb])

                                            def ex(state=state):
                                                pst, psb = state["ps"]
                                                ptt, ptb = r_PT.next()
                                                state["pt"] = (ptt, ptb)
                                                S.op("act", lambda e: e.activation(out=ptt[:], in_=pst[:], func=AF.Exp, scale=scale),
                                                     reads=[psb], writes=[ptb])

                                            def pv(kc=kc, state=state, slot=slot, kvb=kvb, last=last):
                                                ptt, ptb = state["pt"]
                                                bank = pv_bank[0]
                                                pe_seq([(lambda e, s=s: e.matmul(pO[bank * 2 + s // 2][:, s % 2, :], ptt[:, s * 128:(s + 1) * 128], VT[slot][:, kc, :],
                                                                                 start=(kc == 0 and s % 2 == 0), stop=last, skip_group_check=True)) for s in range(4)],
                                                       [ptb, kvb], [pObuf[bank]])

                                            pre = None
                                            if kc == 0 and m == 0:
                                                def pre(load_q=load_q, load_kv=load_kv, f=first[0]):
                                                    if f:
                                                        load_kv()
                                                    load_q()
                                                first[0] = False
                                            postf = None
                                            if last:
                                                def postf(m=m, qh=qh, q0=q0, grp_Of=grp_Of):
                                                    oft, ofb = r_Of.next()
                                                    bank = pv_bank[0]
                                                    for half in range(2):
                                                        S.op("dve", lambda e, oft=oft, bank=bank, half=half: e.tensor_copy(out=oft[:, half * 2:half * 2 + 2, :], in_=pO[bank * 2 + half][:]),
                                                             reads=[pObuf[bank]], writes=[ofb])
                                                    pv_bank[0] = 1 - bank
                                                    grp_Of.append((oft, ofb))
                                                    if m == nmap - 1:
                                                        return finish(grp_Of, qh, q0)
                                                    return None
                                            tasks.append((pre, qk, ex, pv, postf))

                    emit_pipelined(tasks, LOOK=2, PRE=24, DEFER=4)
                    S.barrier()
                    S.flush()

            def na_attention():
                with ExitStack() as st:
                    Gt = sb(st, "Gt", [128, 16, NA_E * 64], BF16)
                    b_G = Buf("G")
                    for h in range(16):
                        S.dma("pool", Gt[:, h, :], na_g[l, h], writes=[b_G])
                    if has_S:
                        Gs = [sb(st, f"Gs{i}", [128, 4, 1152], BF16) for i in range(2)]
                        b_Gs = [Buf("Gs0"), Buf("Gs1")]
                        KTs = [sb(st, f"sKT{i}", [72, 4, 768], BF16) for i in range(2)]
                        VTs = [sb(st, f"sVT{i}", [128, 4, 6, 65], BF16) for i in range(2)]
                        QTs = [sb(st, f"sQT{i}", [72, 6, 512], BF16) for i in range(2)]
                        kvqs = [Buf("skvq0"), Buf("skvq1")]
                        for i in range(2):
                            S.op("pool", lambda e, i=i: e.memset(VTs[i][:], 1.0), writes=[kvqs[i]])
                            for rho in range(4):
                                S.dma("pool", KTs[i][64:72, rho, :], na_kis[:, :], writes=[kvqs[i]])
                    KT = [sb(st, f"nKT{i}", [66, 1024], BF16) for i in range(3)]
                    VT = [sb(st, f"nVT{i}", [128, 8, 65], BF16) for i in range(3)]
                    QT = [sb(st, f"nQT{i}", [66, 8, 512], BF16) for i in range(3)]
                    kvq = [Buf(f"nkvq{i}") for i in range(3)]
                    for i in range(3):
                        S.op("pool", lambda e, i=i: e.memset(VT[i][:], 1.0), writes=[kvq[i]])
                        S.dma("pool", KT[i][64:66, :], na_ki[:, :], writes=[kvq[i]])
                    PT = [sb(st, f"nPT{i}", [128, 512], BF16) for i in range(3)]
                    r_PT = Rot(PT)
                    Of = [sb(st, f"nOf{i}", [128, 4, 65], F32) for i in range(2)]
                    r_Of = Rot(Of)
                    rr = sb(st, "nrr", [128, 4], F32)
                    on = [sb(st, f"non{i}", [128, 4, 64], BF16) for i in range(2)]
                    r_on = Rot(on)
                    sto = [sb(st, f"nsto{i}", [64, 512], BF16) for i in range(2)]
                    r_sto = Rot(sto)
                    b_post = Buf("npost")
                    pS = [ps(st, f"npS{i}", [128, 512]) for i in range(3)]
                    r_pS = Rot(pS)
                    pO = [ps(st, f"npO{i}", [128, 512])[:, 0:260].rearrange("p (s e) -> p s e", s=4) for i in range(2)]
                    pObuf = [Buf("npOa"), Buf("npOb")]
                    pTt = ps(st, "npTt", [128, 8, 128], BF16)[0:64, 0:4, :]
                    b_pTt = Buf("npTt")
                    tasks = []
                    it = [0]
                    its = [0]
                    pv_bank = [0]

                    def mk_post(h, q0):
                        def postf():
                            bank = pv_bank[0]
                            pv_bank[0] = 1 - bank
                            oft, ofb = r_Of.next()
                            S.op("dve", lambda e: e.tensor_copy(out=oft[:], in_=pO[bank][:]), reads=[pObuf[bank]], writes=[ofb])
                            S.op("dve", lambda e: e.reciprocal(out=rr[:, 0:4], in_=oft[:, :, 64]), reads=[ofb], writes=[b_post])
                            ont, onb = r_on.next()
                            S.op("dve", lambda e: e.tensor_tensor(out=ont[:], in0=oft[:, :, 0:64], in1=rr[:, 0:4].unsqueeze(2).to_broadcast([128, 4, 64]), op=ALU.mult),
                                 reads=[ofb, b_post], writes=[onb])
                            def part2():
                                pe_seq([(lambda e, s=s: e.transpose(pTt[:, s, :], ont[:, s, :], ident[:])) for s in range(4)], [onb, b_const], [b_pTt])
                                so, sob = r_sto.next()
                                S.op("act", lambda e: e.copy(out=so[:], in_=pTt[:].rearrange("p s q -> p (s q)")), reads=[b_pTt], writes=[sob])
                                S.dma("sp", oT_d[1][h // 2, (h % 2) * 64:(h % 2) * 64 + 64, q0:q0 + 512], so[:], reads=[sob])
                            return part2
                        return postf

                    def mk_ex(state):
                        def ex():
                            pst, psb = state["ps"]
                            ptt, ptb = r_PT.next()
                            state["pt"] = (ptt, ptb)
                            S.op("act", lambda e: e.activation(out=ptt[:], in_=pst[:], func=AF.Exp, scale=0.125), reads=[psb], writes=[ptb])
                        return ex

                    def mk_pv(state, vt_ap, kb, firstt, lastt):
                        def pv():
                            ptt, ptb = state["pt"]
                            po = pO[pv_bank[0]]
                            pe_seq([(lambda e, s=s: e.matmul(po[:, s, :], ptt[:, s * 128:(s + 1) * 128], vt_ap, start=(firstt and s == 0), stop=lastt, skip_group_check=True))
                                    for s in range(4)], [ptb, kb], [pObuf[pv_bank[0]]])
                        return pv

                    for si, (sname, NQ, RK) in enumerate(segs):
                        tok0 = seg_off[si]
                        if RK == 1:
                            sv = seg_src[si][0]
                            rows = NQ // 64
                            nqb = rows // 8
                            for qb in range(nqb):
                                var = 0 if qb == 0 else (2 if qb == nqb - 1 else 1)
                                R0 = 8 * qb
                                tlist = [t for t in range(8) if 0 <= R0 - 4 + 2 * t and R0 - 4 + 2 * t + 1 < rows]
                                for h in range(16):
                                    slot = it[0] % 3
                                    it[0] += 1
                                    kb = kvq[slot]

                                    def pre(slot=slot, kb=kb, h=h, R0=R0, tok0=tok0, tlist=tlist, sv=sv, var_of=(var,)):
                                        ta, tb = tlist[0], tlist[-1] + 1
                                        k0 = (R0 - 4) * 64
                                        S.dma("sp", KT[slot][0:64, ta * 128:tb * 128], sv.knT(h, k0 + ta * 128, k0 + tb * 128), writes=[kb])
                                        for (ko_, nk_, vap) in sv.vpieces("vn", k0 + ta * 128, k0 + tb * 128, h * 64, (h + 1) * 64):
                                            S.dma("sp", VT[slot][:, ta + ko_:ta + ko_ + nk_, 0:64], vap, writes=[kb])
                                        for t_ in tlist:
                                            S.dma("sp", QT[slot][0:64, t_, :], qnT_d[h, :, tok0 + R0 * 64:tok0 + R0 * 64 + 512], writes=[kb])
                                        S.dma("pool", QT[slot][64:66, :, :], na_mq[var_of[0]], writes=[kb])

                                    for ti, t in enumerate(tlist):
                                        state = {}
                                        lastt = (ti == len(tlist) - 1)

                                        def qk(t=t, slot=slot, kb=kb, h=h, var=var, state=state):
                                            pst, psb = r_pS.next()
                                            state["ps"] = (pst, psb)
                                            off = (14 - 2 * t) * 64
                                            mm_group([(pst[:, :], KT[slot][0:66, t * 128:(t + 1) * 128], QT[slot][0:66, t, :]),
                                                      (pst[:, :], ident8[:], Gt[:, h, off:off + 512])], [kb, b_G, b_const], [psb])

                                        tasks.append((pre if ti == 0 else None, qk, mk_ex(state), mk_pv(state, VT[slot][:, t, :], kb, ti == 0, lastt),
                                                      mk_post(h, tok0 + R0 * 64) if lastt else None))
                        else:
                            srcs = seg_src[si]
                            for h in range(16):
                                gslot = h % 2

                                def load_g(h=h, gslot=gslot):
                                    S.dma("pool", Gs[gslot][:], na_gs[l, h], writes=[b_Gs[gslot]])

                                for qb in range(4):
                                    var = 0 if qb == 0 else (2 if qb == 3 else 1)
                                    dl = [dd for dd in range(-1, 5) if 0 <= 4 * qb + dd <= 15]
                                    slot = its[0] % 2
                                    its[0] += 1
                                    kb = kvqs[slot]

                                    def pre(slot=slot, kb=kb, h=h, qb=qb, dl=dl, srcs=srcs, tok0=tok0, load_g=load_g, var_of=(var,)):
                                        if qb == 0:
                                            load_g()
                                        j0, j1 = dl[0] + 1, dl[-1] + 2
                                        k0 = 128 * (4 * qb - 1)
                                        for rho in range(4):
                                            S.dma("sp", KTs[slot][0:64, rho, j0 * 128:j1 * 128], srcs[rho].knT(h, k0 + j0 * 128, k0 + j1 * 128), reads=[b_gath], writes=[kb])
                                            for (ko_, nk_, vap) in srcs[rho].vpieces("vn", k0 + j0 * 128, k0 + j1 * 128, h * 64, (h + 1) * 64):
                                                S.dma("sp", VTs[slot][:, rho, j0 + ko_:j0 + ko_ + nk_, 0:64], vap, reads=[b_gath], writes=[kb])
                                        for j_ in range(j0, j1):
                                            S.dma("sp", QTs[slot][0:64, j_, :], qnT_d[h, :, tok0 + qb * 512:tok0 + qb * 512 + 512], writes=[kb])
                                        S.dma("pool", QTs[slot][64:72, :, :], na_mqs[var_of[0]], writes=[kb])

                                    combos = [(rho, dd) for rho in range(4) for dd in dl]
                                    for ci_, (rho, dd) in enumerate(combos):
                                        state = {}
                                        lastt = (ci_ == len(combos) - 1)

                                        def qk(rho=rho, dd=dd, slot=slot, kb=kb, gslot=gslot, var=var, state=state):
                                            pst, psb = r_pS.next()
                                            state["ps"] = (pst, psb)
                                            off = (32 - 8 * dd) * 16
                                            j = dd + 1
                                            mm_group([(pst[:, :], KTs[slot][0:72, rho, j * 128:(j + 1) * 128], QTs[slot][0:72, j, :]),
                                                      (pst[:, :], ident8[:], Gs[gslot][:, rho, off:off + 512])], [kb, b_Gs[gslot], b_G, b_const], [psb])

                                        tasks.append((pre if ci_ == 0 else None, qk, mk_ex(state), mk_pv(state, VTs[slot][:, rho, dd + 1, :], kb, ci_ == 0, lastt),
                                                      mk_post(h, tok0 + qb * 512) if lastt else None))
                    emit_pipelined(tasks, LOOK=2, PRE=6, DEFER=3)
                    S.barrier()
                    S.flush()

            if has_S:
                for bi in range(NBLK):
                    S.collective(lambda e, bi=bi: e.collective_compute("AllGather", ALU.bypass, replica_groups=[[0, 1, 2, 3], [4, 5, 6, 7]],
                                                                       ins=[kv_src[bi * 128:(bi + 1) * 128, :].opt()],
                                                                       outs=[kv_all[bi * 512:(bi + 1) * 512, :].opt()]), writes=[b_gath])
            maybe_stop("A")
            attention("da")
            maybe_stop("da")
            na_attention()
            maybe_stop("na")
            attention("gq")
            maybe_stop("gq")

            with ExitStack() as st:
                CH = 256
                Wb = [sb(st, f"Wb{i}", [128, 8, D], BF16) for i in range(4)]
                b_W = Buf("W")
                for i, w in enumerate(w_br + [w_o]):
                    S.dma("pool", Wb[i][:], w[l].rearrange("(c p) n -> p c n", p=128), writes=[b_W])
                oT = [[sb(st, f"oT{j}{i}", [128, 8, CH], BF16) for i in range(3)] for j in range(2)]
                gT = [sb(st, f"gT{j}", [128, 24, CH], BF16) for j in range(2)]
                b_in = [Buf("cin0"), Buf("cin1")]
                mm = [sb(st, f"mm{i}", [128, CH], F32) for i in range(3)]
                b_mm = [Buf(f"mm{i}") for i in range(3)]
                mT = sb(st, "mT", [128, 8, CH], BF16)
                b_mT = Buf("mT")
                hb = [sb(st, f"chb{i}", [128, D], F32) for i in range(2)]
                r_hb = Rot(hb)
                yb = sb(st, "yb", [128, D], F32)
                junk = sb(st, "junkC", [128, D], F32)
                ssC = sb(st, "ssC", [128, 4], F32)
                ub = [sb(st, f"cub{i}", [128, D], BF16) for i in range(2)]
                r_ub = Rot(ub)
                sT = [sb(st, f"csT{i}", [128, 8, 128], BF16) for i in range(2)]
                r_sT = Rot(sT)
                b_y = Buf("y"); b_ss = Buf("ssC"); b_junk = Buf("junkC")
                pB = [ps(st, f"pB{i}", [128, 512]) for i in range(4)]
                r_pB = Rot(pB)
                pOo = ps(st, "pOo", [128, D])
                b_pOo = Buf("pOo")
                pT = ps(st, "pTC", [128, 8, 128], BF16)
                b_pT = Buf("pTC")
                nch = NT // CH

                def load_c(ci):
                    j = ci % 2
                    t0 = ci * CH
                    for b in range(3):
                        S.dma("sp", oT[j][b][:], oT_d[b][:, :, t0:t0 + CH].rearrange("c p t -> p c t"), writes=[b_in[j]])
                    S.dma("sp", gT[j][:], gT_d[:, :, t0:t0 + CH].rearrange("c p t -> p c t"), writes=[b_in[j]])

                load_c(0)
                for ci in range(nch):
                    j = ci % 2
                    t0 = ci * CH
                    if ci + 1 < nch:
                        load_c(ci + 1)
                    for cc in range(8):
                        for b in range(3):
                            pb, pbb = r_pB.next()
                            mm_group([(pb[:, 0:CH], Wb[b][:, e_, cc * 128:(cc + 1) * 128], oT[j][b][:, e_, :]) for e_ in range(8)], [b_W, b_in[j]], [pbb])
                            S.op("dve", lambda e, pb=pb, b=b, cc=cc, j=j: e.tensor_tensor(out=mm[b][:], in0=pb[:, 0:CH], in1=gT[j][:, b * 8 + cc, :], op=ALU.mult),
                                 reads=[pbb, b_in[j]], writes=[b_mm[b]])
                        S.op("pool", lambda e: e.tensor_tensor(out=mm[0][:], in0=mm[0][:], in1=mm[1][:], op=ALU.add), reads=[b_mm[0], b_mm[1]], writes=[b_mm[0]])
                        S.op("pool", lambda e, cc=cc: e.tensor_tensor(out=mT[:, cc, :], in0=mm[0][:], in1=mm[2][:], op=ALU.add), reads=[b_mm[0], b_mm[2]], writes=[b_mT])
                    for tt in range(CH // 128):
                        tk = t0 + tt * 128
                        ht, hbuf = r_hb.next()
                        S.dma("sp", ht[:], src_h[tk:tk + 128, :], writes=[hbuf])
                        for nn in range(2):
                            mm_group([(pOo[:, nn * 512:(nn + 1) * 512], mT[:, cc, tt * 128:(tt + 1) * 128], Wb[3][:, cc, nn * 512:(nn + 1) * 512]) for cc in range(8)],
                                     [b_mT, b_W], [b_pOo])
                        S.op("act", lambda e: e.activation(out=junk[:], in_=pOo[:], func=AF.Square, accum_out=ssC[:, 0:1]), reads=[b_pOo], writes=[b_junk, b_ss])
                        rstd_from_ss(ssC[:, 0:1], ssC[:, 1:2], D, [b_ss])
                        S.op("dve", lambda e: e.scalar_tensor_tensor(out=yb[:], in0=pOo[:], scalar=ssC[:, 1:2], in1=gvec[:, 1, :], op0=ALU.mult, op1=ALU.mult),
                             reads=[b_pOo, b_ss, b_gv], writes=[b_y])
                        S.op("pool", lambda e, ht=ht: e.tensor_tensor(out=ht[:], in0=ht[:], in1=yb[:], op=ALU.add), reads=[hbuf, b_y], writes=[hbuf])
                        S.dma("sp", h_d[tk:tk + 128, :], ht[:], reads=[hbuf])
                        S.op("act", lambda e, ht=ht: e.activation(out=junk[:], in_=ht[:], func=AF.Square, accum_out=ssC[:, 2:3]), reads=[hbuf], writes=[b_junk, b_ss])
                        rstd_from_ss(ssC[:, 2:3], ssC[:, 3:4], D, [b_ss])
                        ut, ubuf = r_ub.next()
                        S.op("dve", lambda e, ht=ht, ut=ut: e.scalar_tensor_tensor(out=ut[:], in0=ht[:], scalar=ssC[:, 3:4], in1=gvec[:, 2, :], op0=ALU.mult, op1=ALU.mult),
                             reads=[hbuf, b_ss, b_gv], writes=[ubuf])
                        pe_seq([(lambda e, ut=ut, c=c: e.transpose(pT[:, c, :], ut[:, c * 128:(c + 1) * 128], ident[:])) for c in range(8)], [ubuf, b_const], [b_pT])
                        stt, stb = r_sT.next()
                        S.op("act", lambda e, stt=stt: e.copy(out=stt[:], in_=pT[:]), reads=[b_pT], writes=[stb])
                        S.dma("sp", u2T_d[:, :, tk:tk + 128].rearrange("c p t -> p c t"), stt[:], reads=[stb])
                S.barrier()
                S.flush()

            maybe_stop("C")
            with ExitStack() as st:
                uT = sb(st, "u2T", [128, 8, 2048], BF16)
                b_uT = Buf("u2T")
                wb = [sb(st, f"dwb{i}", [128, 8, 512], BF16) for i in range(3)]
                r_wb = Rot(wb)
                rl = [sb(st, f"rl{i}", [128, 512], F32) for i in range(3)]
                r_rl = Rot(rl)
                stg = [sb(st, f"dstg{i}", [128, 512], BF16) for i in range(3)]
                r_stg = Rot(stg)
                pA = [ps(st, f"dpA{i}", [128, 512]) for i in range(4)]
                r_pA = Rot(pA)
                for sc in range(NT // 2048):
                    t0 = sc * 2048
                    S.dma("sp", uT[:], u2T_d[:, :, t0:t0 + 2048].rearrange("c p t -> p c t"), writes=[b_uT])
                    for blk in range(8):
                        wt, wbuf = r_wb.next()
                        S.dma("pool", wt[:], w_up[l].rearrange("(c p) n -> p c n", p=128)[:, :, blk * 512:(blk + 1) * 512], writes=[wbuf])
                        for sub in range(4):
                            for tq in range(4):
                                pa, pab = r_pA.next()
                                mm_group([(pa[:, :], wt[:, c, sub * 128:(sub + 1) * 128], uT[:, c, tq * 512:(tq + 1) * 512]) for c in range(8)], [wbuf, b_uT], [pab])
                                rt, rb = r_rl.next()
                                S.op("act", lambda e, rt=rt, pa=pa: e.activation(out=rt[:], in_=pa[:], func=AF.Relu), reads=[pab], writes=[rb])
                                sg, sgb = r_stg.next()
                                S.op("pool", lambda e, rt=rt, sg=sg: e.tensor_tensor(out=sg[:], in0=rt[:], in1=rt[:], op=ALU.mult), reads=[rb], writes=[sgb])
                                S.dma("sp", aT_d[blk * 4 + sub, :, t0 + tq * 512:t0 + (tq + 1) * 512], sg[:], reads=[sgb])
                S.barrier()
                S.flush()

            maybe_stop("D1")
            with ExitStack() as st:
                Wd = sb(st, "Wd", [128, 32, D], BF16)
                Wg = sb(st, "Wg", [128, 8, D], BF16)
                Wp = sb(st, "Wp", [128, 2, D], BF16)
                b_W = Buf("W2")
                for q4 in range(4):
                    S.dma("pool", Wd[:, q4 * 8:(q4 + 1) * 8, :], w_down[l, q4 * 1024:(q4 + 1) * 1024, :].rearrange("(c p) n -> p c n", p=128), writes=[b_W])
                S.dma("pool", Wg[:], w_ple_gate[l].rearrange("(c p) n -> p c n", p=128), writes=[b_W])
                S.dma("pool", Wp[:], w_ple[l].rearrange("(c p) n -> p c n", p=128), writes=[b_W])
                aT = [sb(st, f"aT{j}", [128, 32, 512], BF16) for j in range(2)]
                b_a = [Buf("a0"), Buf("a1")]
                hb = [sb(st, f"ehb{i}", [128, D], F32) for i in range(2)]
                r_hb = Rot(hb)
                pl = [sb(st, f"pl{i}", [128, PLE], F32) for i in range(2)]
                r_pl = Rot(pl)
                plb = sb(st, "plb", [128, PLE], BF16)
                b_plb = Buf("plb")
                yb = sb(st, "eyb", [128, D], F32)
                gt = sb(st, "egt", [128, D], F32)
                junk = sb(st, "junkE", [128, D], F32)
                ssE = sb(st, "ssE", [128, 4], F32)
                hbf = sb(st, "hbf", [128, D], BF16)
                hT = sb(st, "hT", [128, 8, 128], BF16)
                pTs = sb(st, "pTs", [128, 2, 128], BF16)
                b_y = Buf("ey"); b_g = Buf("eg"); b_ss = Buf("ssE"); b_junk = Buf("junkE"); b_hbf = Buf("hbf"); b_hT = Buf("hT"); b_pTs = Buf("pTs")
                pF = ps(st, "pF", [128, D]); b_pF = Buf("pF")
                pG = ps(st, "pG", [128, D]); b_pG = Buf("pG")
                pE = ps(st, "pE", [128, D]); b_pE = Buf("pE")
                pT = ps(st, "pTE", [128, 8, 128], BF16); b_pT = Buf("pTE")
                pT2 = ps(st, "pTE2", [128, 8, 128], BF16)[:, 0:2, :]; b_pT2 = Buf("pTE2")
                nch = NT // 512

                def load_a(ci):
                    j = ci % 2
                    for q4 in range(4):
                        S.dma("sp", aT[j][:, q4 * 8:(q4 + 1) * 8, :], aT_d[q4 * 8:(q4 + 1) * 8, :, ci * 512:(ci + 1) * 512].rearrange("c p t -> p c t"), writes=[b_a[j]])

                load_a(0)
                for ci in range(nch):
                    j = ci % 2
                    if ci + 1 < nch:
                        load_a(ci + 1)
                    for tt in range(4):
                        tk = ci * 512 + tt * 128
                        ht, hbuf = r_hb.next()
                        S.dma("sp", ht[:], h_d[tk:tk + 128, :], writes=[hbuf])
                        plt, plbuf = r_pl.next()
                        S.dma("sp", plt[:], p_in[l, tk:tk + 128, :], writes=[plbuf])
                        for nn in range(2):
                            mm_group([(pF[:, nn * 512:(nn + 1) * 512], aT[j][:, ch, tt * 128:(tt + 1) * 128], Wd[:, ch, nn * 512:(nn + 1) * 512]) for ch in range(32)],
                                     [b_a[j], b_W], [b_pF])
                        S.op("act", lambda e: e.activation(out=junk[:], in_=pF[:], func=AF.Square, accum_out=ssE[:, 0:1]), reads=[b_pF], writes=[b_junk, b_ss])
                        rstd_from_ss(ssE[:, 0:1], ssE[:, 1:2], D, [b_ss])
                        S.op("dve", lambda e: e.scalar_tensor_tensor(out=yb[:], in0=pF[:], scalar=ssE[:, 1:2], in1=gvec[:, 3, :], op0=ALU.mult, op1=ALU.mult),
                             reads=[b_pF, b_ss, b_gv], writes=[b_y])
                        S.op("pool", lambda e, ht=ht: e.tensor_tensor(out=ht[:], in0=ht[:], in1=yb[:], op=ALU.add), reads=[hbuf, b_y], writes=[hbuf])
                        S.op("dve", lambda e, ht=ht: e.tensor_copy(out=hbf[:], in_=ht[:]), reads=[hbuf], writes=[b_hbf])
                        pe_seq([(lambda e, c=c: e.transpose(pT[:, c, :], hbf[:, c * 128:(c + 1) * 128], ident[:])) for c in range(8)], [b_hbf, b_const], [b_pT])
                        S.op("act", lambda e: e.copy(out=hT[:], in_=pT[:]), reads=[b_pT], writes=[b_hT])
                        S.op("pool", lambda e, plt=plt: e.tensor_copy(out=plb[:], in_=plt[:]), reads=[plbuf], writes=[b_plb])
                        pe_seq([(lambda e, c=c: e.transpose(pT2[:, c, :], plb[:, c * 128:(c + 1) * 128], ident[:])) for c in range(2)], [b_plb, b_const], [b_pT2])
                        S.op("dve", lambda e: e.tensor_copy(out=pTs[:], in_=pT2[:]), reads=[b_pT2], writes=[b_pTs])
                        for nn in range(2):
                            mm_group([(pG[:, nn * 512:(nn + 1) * 512], hT[:, c, :], Wg[:, c, nn * 512:(nn + 1) * 512]) for c in range(8)], [b_hT, b_W], [b_pG])
                        for nn in range(2):
                            mm_group([(pE[:, nn * 512:(nn + 1) * 512], pTs[:, c, :], Wp[:, c, nn * 512:(nn + 1) * 512]) for c in range(2)], [b_pTs, b_W], [b_pE])
                        S.op("act", lambda e: e.activation(out=gt[:], in_=pG[:], func=AF.Sigmoid), reads=[b_pG], writes=[b_g])
                        S.op("dve", lambda e: e.tensor_tensor(out=gt[:], in0=pE[:], in1=gt[:], op=ALU.mult), reads=[b_pE, b_g], writes=[b_g])
                        S.op("act", lambda e: e.activation(out=junk[:], in_=gt[:], func=AF.Square, accum_out=ssE[:, 2:3]), reads=[b_g], writes=[b_junk, b_ss])
                        rstd_from_ss(ssE[:, 2:3], ssE[:, 3:4], D, [b_ss])
                        S.op("dve", lambda e: e.scalar_tensor_tensor(out=yb[:], in0=gt[:], scalar=ssE[:, 3:4], in1=gvec[:, 4, :], op0=ALU.mult, op1=ALU.mult),
                             reads=[b_g, b_ss, b_gv], writes=[b_y])
                        S.op("pool", lambda e, ht=ht: e.tensor_tensor(out=ht[:], in0=ht[:], in1=yb[:], op=ALU.add), reads=[hbuf, b_y], writes=[hbuf])
                        S.dma("sp", dst_final[tk:tk + 128, :], ht[:], reads=[hbuf])
                S.barrier()
                S.flush()
    return nc


def _rope_tables(Sl):
    t = np.arange(Sl)
    row = (t // GRID_W).astype(np.float32)
    col = (t % GRID_W).astype(np.float32)
    half = 64
    freqs = (np.float32(10000.0) ** (-np.arange(0, half, 2, dtype=np.float32) / np.float32(half))).astype(np.float32)
    ang = np.concatenate([row[:, None] * freqs, col[:, None] * freqs], axis=-1).astype(np.float32)
    return np.cos(ang).astype(np.float32), np.sin(ang).astype(np.float32)


def _aug_tables(segs, rank):
    qcols, kcols = [], []
    dcorr = np.zeros((len(segs), 8, 128, 128), np.float32)
    dcorr2 = np.zeros((8, 4, 128, 128), np.float32)
    kk = np.arange(128)[:, None]
    qq = np.arange(128)[None, :]
    for si, (_, nq, R) in enumerate(segs):
        a = np.arange(nq)
        one = np.ones(nq, np.float32)
        qcols.append(np.stack([(a // 128).astype(np.float32), (a % 128).astype(np.float32), one, one, one]))
        mult = 1.0 if R == 1 else 4.0
        ks = np.zeros((8, 5, nq * R), np.float32)
        for h in range(8):
            m = 2.0 ** (-(h + 1))
            for rho in range(R):
                bidx = np.arange(nq)
                sl = slice(rho * nq, (rho + 1) * nq)
                ks[h, 0, sl] = -1024.0 * m * mult
                ks[h, 1, sl] = -8.0 * m * mult
                ks[h, 2, sl] = 1024.0 * m * mult * (bidx // 128)
                ks[h, 3, sl] = 8.0 * m * mult * (bidx % 128)
                ks[h, 4, sl] = 0.0 if R == 1 else -8.0 * m * (rank - rho)
                if R > 1:
                    d = 16.0 * m * (rank - rho)
                    dcorr2[h, rho] = d * (qq < kk) + min(0.0, d) * (qq == kk)
            dcorr[si, h] = -16.0 * m * mult * np.maximum(kk - qq, 0)
        kcols.append(ks)
    ql = np.concatenate(qcols, axis=1)
    qaug = np.stack([ql, -ql]).astype(np.float32)
    kaug = np.concatenate(kcols, axis=2).astype(np.float32)
    return qaug, kaug, dcorr, dcorr2


def _na_tables(na_rpb):
    c = np.arange(64)
    cs = np.clip(c - 8, 0, 48)
    colvalid = (c[:, None] >= cs[None, :]) & (c[:, None] < cs[None, :] + 16)
    dc = np.clip(c[:, None] - c[None, :], -15, 15) + 15
    L = na_rpb.shape[0]
    G = np.zeros((L, 16, 2, 64, NA_E, 64), np.float32)
    for krl in range(2):
        for e in range(NA_E):
            dr = 17 - e + krl
            if 0 <= dr <= 14:
                G[:, :, krl, :, e, :] = na_rpb[:, :, dr][:, :, dc]
    G = np.where(colvalid[None, None, None, :, None, :], G, np.float32(NEG_G)).astype(np.float32)
    G = G.reshape(L, 16, 128, NA_E * 64)
    M = np.zeros((3, 8, 2, 64, 8, 64), np.float32)
    for var in range(3):
        for t in range(8):
            for krl in range(2):
                kr = -4 + 2 * t + krl
                for qr in range(8):
                    if var == 0:
                        start = max(qr - 4, 0)
                    elif var == 1:
                        start = qr - 4
                    else:
                        start = min(qr - 4, 0)
                    ok = (start <= kr < start + 8)
                    if not ok:
                        M[var, t, krl, :, qr, :] = NEG_M
    Mq = np.ascontiguousarray(M[:, :, :, 0, :, :].transpose(0, 2, 1, 3, 4).reshape(3, 2, 8, 512))
    ki = np.zeros((2, 1024), np.float32)
    kk = np.arange(1024) % 128
    ki[0] = (kk < 64)
    ki[1] = (kk >= 64)
    return G, Mq, ki


def _na_tables_S(na_rpb, rank):
    L = na_rpb.shape[0]
    ap = np.arange(16)
    G = np.zeros((L, 16, 8, 16, 4, 72, 16), np.float32)
    for rho in range(4):
        c = 4 * ap + rank
        kc = 4 * ap + rho
        cs = np.clip(c - 8, 0, 48)
        colvalid = (kc[:, None] >= cs[None, :]) & (kc[:, None] < cs[None, :] + 16)
        dc = np.clip(kc[:, None] - c[None, :], -15, 15) + 15
        for Rk in range(8):
            for e in range(72):
                dr = Rk + 39 - e
                if 0 <= dr <= 14:
                    G[:, :, Rk, :, rho, e, :] = na_rpb[:, :, dr][:, :, dc]
        G[:, :, :, :, rho] = np.where(colvalid[None, None, None, :, None, :], G[:, :, :, :, rho], np.float32(NEG_G))
    G = G.reshape(L, 16, 128, 4, 72 * 16)
    M = np.zeros((3, 6, 8, 16, 32, 16), np.float32)
    for var in range(3):
        for j in range(6):
            dd = j - 1
            for Rk in range(8):
                kr = 8 * dd + Rk
                for Rq in range(32):
                    if var == 0:
                        start = max(Rq - 4, 0)
                    elif var == 1:
                        start = Rq - 4
                    else:
                        start = min(Rq - 4, 24)
                    if not (start <= kr < start + 8):
                        M[var, j, Rk, :, Rq, :] = NEG_M
    Mq = np.ascontiguousarray(M[:, :, :, 0, :, :].transpose(0, 2, 1, 3, 4).reshape(3, 8, 6, 512))
    ki = np.zeros((8, 768), np.float32)
    kk = (np.arange(768) % 128) // 16
    for j in range(8):
        ki[j] = (kk == j)
    return G, Mq, ki


_CACHE = {}


def _run(inputs, depth, segs_fn, n_cores=8, stop_after=None):
    key = (depth, tuple(segs_fn), stop_after)
    if key not in _CACHE:
        try:
            _CACHE[key] = build_program(depth, segs_fn, stop_after)
        except _Stop as e:
            _CACHE[key] = e.args[0]
    nc = _CACHE[key]
    f = lambda a: np.ascontiguousarray(np.asarray(a, dtype=np.float32))
    fl = lambda a: np.ascontiguousarray(np.asarray(a, dtype=np.float32)[:depth])
    has_S = any(r > 1 for _, _, r in segs_fn)
    rpb = fl(inputs["na_rpb"])
    na_g, na_mq, na_ki = _na_tables(rpb)
    idents = np.stack([np.eye(128, dtype=np.float32), 8.0 * np.eye(128, dtype=np.float32)])
    shared = {
        "w_in": fl(inputs["w_in"]), "da_lambda": fl(inputs["da_lambda"]).reshape(depth, 256),
        "da_norm": fl(inputs["da_norm"]), "gq_q_norm": fl(inputs["gq_q_norm"]), "gq_k_norm": fl(inputs["gq_k_norm"]),
        "w_br_a": fl(inputs["w_br_a"]), "w_br_b": fl(inputs["w_br_b"]), "w_br_c": fl(inputs["w_br_c"]), "w_o": fl(inputs["w_o"]),
        "g_pre_mix": fl(inputs["g_pre_mix"]), "g_post_mix": fl(inputs["g_post_mix"]), "g_pre_mlp": fl(inputs["g_pre_mlp"]),
        "g_post_mlp": fl(inputs["g_post_mlp"]), "w_up": fl(inputs["w_up"]), "w_down": fl(inputs["w_down"]),
        "w_ple": fl(inputs["w_ple"]), "w_ple_gate": fl(inputs["w_ple_gate"]), "g_ple": fl(inputs["g_ple"]),
        "na_g": na_g, "na_mq": na_mq, "na_ki": na_ki, "idents": idents,
    }
    xp, xs = f(inputs["x_prompt"]), f(inputs["x_sample"])
    pp, pS_ = f(inputs["p_prompt"]), f(inputs["p_sample"])
    cs_full = {s: _rope_tables(s) for s in (SEQ, DEC_SEQ)}
    per_rank = {}
    for rank in range(4):
        qaug, kaug, dcorr, dcorr2 = _aug_tables(segs_fn, rank)
        t = dict(qaug=qaug, kaug=kaug, dcorr=dcorr, dcorr2=dcorr2)
        if has_S:
            gs, mqs, kis = _na_tables_S(rpb, rank)
            t["na_gs"] = gs
            t["na_mqs"] = mqs
            t["na_kis"] = kis
        per_rank[rank] = t
    in_maps = []
    for c in range(n_cores):
        rank, grp = c % 4, c // 4
        parts_x, parts_p, parts_cs, parts_sn = [], [], [], []
        for name, s, R in segs_fn:
            if R == 1:
                parts_x.append(xp[c]); parts_p.append(pp[:depth, c])
                parts_cs.append(cs_full[SEQ][0]); parts_sn.append(cs_full[SEQ][1])
            else:
                parts_x.append(xs[grp, rank::4]); parts_p.append(pS_[:depth, grp, rank::4])
                parts_cs.append(cs_full[DEC_SEQ][0][rank::4]); parts_sn.append(cs_full[DEC_SEQ][1][rank::4])
        m = dict(shared)
        m.update(per_rank[rank])
        m["x_in"] = np.ascontiguousarray(np.concatenate(parts_x, axis=0))
        m["p_in"] = np.ascontiguousarray(np.concatenate(parts_p, axis=1))
        m["cs_tab"] = np.ascontiguousarray(np.concatenate(parts_cs, axis=0))
        m["sn_tab"] = np.ascontiguousarray(np.concatenate(parts_sn, axis=0))
        in_maps.append(m)
    res = run_bass_kernel_spmd(nc, in_maps, core_ids=list(range(n_cores)))
    _CACHE["last_results"] = res.results
    return [r["y_out"] for r in res.results]


def kernel(**inputs):
    segs_fn = (("P", SEQ, 1), ("S", DEC_SEQ // 4, 4))
    ys = _run(inputs, DEPTH, segs_fn)
    y_prompt = np.stack([ys[c][0:SEQ] for c in range(8)]).astype(np.float32)
    y_sample = np.zeros((2, DEC_SEQ, D), np.float32)
    for c in range(8):
        y_sample[c // 4, (c % 4)::4] = ys[c][SEQ:SEQ + DEC_SEQ // 4]
    return (y_prompt, y_sample)
```

```python
import math
from contextlib import ExitStack
import numpy as np
import concourse.bass as bass
import concourse.mybir as mybir
from concourse.bass_utils import run_bass_kernel_spmd

F32 = mybir.dt.float32
BF16 = mybir.dt.bfloat16
AF = mybir.ActivationFunctionType
ALU = mybir.AluOpType
AX = mybir.AxisListType

D = 1024
DEPTH = 4
SEQ = 2048
DEC_SEQ = 8192
GRID_W = 64
EPS = 1e-6
IN_W = 10752
D_FF = 4096
PLE = 256
NEG_M = -30000.0
NEG_G = -3000.0
NA_E = 22


class _Stop(Exception):
    pass


class Buf:
    __slots__ = ("name", "w", "r")

    def __init__(self, name):
        self.name = name
        self.w = None
        self.r = {}


class Sched:
    ENG = ("pe", "act", "dve", "pool", "sp")

    def __init__(self, nc, stack):
        self.nc = nc
        self.eng = dict(pe=nc.tensor, act=nc.scalar, dve=nc.vector, pool=nc.gpsimd, sp=nc.sync)
        self.sem = {e: stack.enter_context(nc.semaphore("s_" + e)) for e in self.ENG}
        self.cnt = {e: 0 for e in self.ENG}
        self.NDS = 12
        self.dsem = {q: [stack.enter_context(nc.semaphore(f"d_{q}{i}")) for i in range(self.NDS)]
                     for q in ("sp", "pool")}
        self.dcnt = {q: [0] * self.NDS for q in ("sp", "pool")}
        self.drr = {q: 0 for q in ("sp", "pool")}
        self.waited = {e: {} for e in self.ENG}
        self.ops = {e: [] for e in self.ENG}
        self.cc_sem = stack.enter_context(nc.semaphore("s_cc"))
        self.cc_cnt = 0

    def _need(self, e, tick, waits):
        if tick is None:
            return
        key, val = tick
        if e == "pe" and key == "pe":
            return
        if self.waited[e].get(key, 0) >= val:
            return
        self.waited[e][key] = val
        waits.append((key, val))

    def _semof(self, key):
        if key == "cc":
            return self.cc_sem
        if key in self.sem:
            return self.sem[key]
        q, i = key
        return self.dsem[q][i]

    def _deps(self, e, reads, writes):
        waits = []
        for b in reads:
            self._need(e, b.w, waits)
        for b in writes:
            self._need(e, b.w, waits)
            for k, v in b.r.items():
                self._need(e, (k, v), waits)
        return waits

    def op(self, e, fn, reads=(), writes=(), signal=True):
        waits = self._deps(e, reads, writes) if (reads or writes) else []
        inc = None
        if signal:
            self.cnt[e] += 1
            tick = (e, self.cnt[e])
            inc = (e, 1)
            for b in writes:
                b.w = tick
                b.r = {}
            for b in reads:
                if b.r.get(e, 0) < tick[1]:
                    b.r[e] = tick[1]
        self.ops[e].append((fn, waits, inc))

    def dma(self, q, out, in_, reads=(), writes=()):
        waits = self._deps(q, reads, writes)
        i = self.drr[q]
        self.drr[q] = (i + 1) % self.NDS
        self.dcnt[q][i] += 16
        key = (q, i)
        tick = (key, self.dcnt[q][i])
        for b in writes:
            b.w = tick
            b.r = {}
        for b in reads:
            b.r[key] = tick[1]
        self.ops[q].append((lambda eng, o=out, s=in_: eng.dma_start(out=o, in_=s), waits, (key, 16)))

    def collective(self, fn, writes):
        self.cc_cnt += 1
        tick = ("cc", self.cc_cnt)
        for b in writes:
            b.w = tick
            b.r = {}
        self.ops["pool"].append((fn, [], ("cc", None)))

    def barrier(self):
        for e in self.ENG:
            waits = []
            for e2 in self.ENG:
                if e2 != e and self.cnt[e2] > 0:
                    self._need(e, (e2, self.cnt[e2]), waits)
            for q in ("sp", "pool"):
                for i in range(self.NDS):
                    if self.dcnt[q][i] > 0:
                        self._need(e, ((q, i), self.dcnt[q][i]), waits)
            if waits:
                self.ops[e].append((None, waits, None))

    def flush(self):
        nc = self.nc
        with nc.Block() as block:
            for e, deco in (("pe", block.tensor), ("act", block.scalar), ("dve", block.vector),
                            ("pool", block.gpsimd), ("sp", block.sync)):
                lst = self.ops[e]

                def body(eng, lst=lst):
                    for fn, waits, inc in lst:
                        for key, val in waits:
                            eng.wait_ge(self._semof(key), val)
                        if fn is not None:
                            ins = fn(eng)
                            if inc is not None:
                                if inc[1] is None:
                                    ins.then_inc(self._semof(inc[0]))
                                else:
                                    ins.then_inc(self._semof(inc[0]), inc[1])
                deco(body)
        self.ops = {e: [] for e in self.ENG}


def emit_pipelined(tasks, LOOK=2, PRE=24, DEFER=4):
    n = len(tasks)
    state = {"pre": 0}

    def do_pre(upto):
        while state["pre"] < min(upto, n):
            p = tasks[state["pre"]][0]
            if p:
                p()
            state["pre"] += 1

    deferred = []
    do_pre(PRE)
    for i in range(min(LOOK, n)):
        tasks[i][1]()
    for i in range(n):
        do_pre(i + PRE + 1)
        tasks[i][2]()
        if i + LOOK < n:
            tasks[i + LOOK][1]()
        tasks[i][3]()
        if tasks[i][4]:
            d = tasks[i][4]()
            if d:
                deferred.append((i + DEFER, d))
        while deferred and deferred[0][0] <= i:
            deferred.pop(0)[1]()
    for _, d in deferred:
        d()


class Rot:
    def __init__(self, tiles):
        self.tiles = tiles
        self.bufs = [Buf("rot") for _ in tiles]
        self.i = 0

    def next(self):
        i = self.i
        self.i = (i + 1) % len(self.tiles)
        return self.tiles[i], self.bufs[i]


def build_program(depth, segs, stop_after=None):
    nc = bass.Bass("TRN2", target_bir_lowering=False)
    NT = sum(s for _, s, _r in segs)
    NKV = sum(s * r for _, s, r in segs)
    seg_off, kv_off = [], []
    o = ko = 0
    for _, s, r in segs:
        seg_off.append(o)
        kv_off.append(ko)
        o += s
        ko += s * r

    def din(name, shape, dt=F32):
        return nc.dram_tensor(name, list(shape), dt, kind="ExternalInput").ap()

    def dscr(name, shape, dt=BF16):
        import os as _os2
        if name in _os2.environ.get("KDBG_DUMP", "").split(","):
            return nc.dram_tensor(name, list(shape), dt, kind="ExternalOutput").ap()
        return nc.dram_tensor(name, list(shape), dt).ap()

    x_in = din("x_in", [NT, D])
    p_in = din("p_in", [depth, NT, PLE])
    w_in = din("w_in", [depth, D, IN_W])
    da_lambda = din("da_lambda", [depth, 256])
    da_norm = din("da_norm", [depth, 128])
    gq_q_norm = din("gq_q_norm", [depth, 128])
    gq_k_norm = din("gq_k_norm", [depth, 128])
    w_br = [din("w_br_a", [depth, D, D]), din("w_br_b", [depth, D, D]), din("w_br_c", [depth, D, D])]
    w_o = din("w_o", [depth, D, D])
    g_pre_mix = din("g_pre_mix", [depth, D])
    g_post_mix = din("g_post_mix", [depth, D])
    g_pre_mlp = din("g_pre_mlp", [depth, D])
    g_post_mlp = din("g_post_mlp", [depth, D])
    w_up = din("w_up", [depth, D, D_FF])
    w_down = din("w_down", [depth, D_FF, D])
    w_ple = din("w_ple", [depth, PLE, D])
    w_ple_gate = din("w_ple_gate", [depth, D, D])
    g_ple = din("g_ple", [depth, D])
    cs_tab = din("cs_tab", [NT, 64])
    sn_tab = din("sn_tab", [NT, 64])
    qaug = din("qaug", [2, 5, NT])
    kaug = din("kaug", [8, 5, NKV])
    dcorr = din("dcorr", [len(segs), 8, 128, 128])
    dcorr2 = din("dcorr2", [8, 4, 128, 128])
    na_g = din("na_g", [depth, 16, 128, NA_E * 64])
    na_mq = din("na_mq", [3, 2, 8, 512])
    na_ki = din("na_ki", [2, 1024])
    has_S = any(r > 1 for _, _, r in segs)
    if has_S:
        na_gs = din("na_gs", [depth, 16, 128, 4, 1152])
        na_mqs = din("na_mqs", [3, 8, 6, 512])
        na_kis = din("na_kis", [8, 768])
    idents = din("idents", [2, 128, 128])

    y_out = nc.dram_tensor("y_out", [NT, D], F32, kind="ExternalOutput").ap()

    h_d = dscr("h_d", [NT, D], F32)
    qaT_d = dscr("qaT_d", [16, 64, NT])
    qnT_d = dscr("qnT_d", [16, 64, NT])
    qcT_d = dscr("qcT_d", [8, 128, NT])
    KVR = 4608

    def kvviews(a2):
        return dict(
            kaT=a2[0:1024, :].rearrange("(h d) t -> h d t", d=64),
            knT=a2[1024:2048, :].rearrange("(h d) t -> h d t", d=64),
            kcT=a2[2048:2304, :].rearrange("(h d) t -> h d t", d=128),
            va=a2[2304:3328, :].rearrange("r (two c) -> (r two) c", two=2),
            vn=a2[3328:4352, :].rearrange("r (two c) -> (r two) c", two=2),
            vc=a2[4352:4608, :].rearrange("r (e c) -> (r e) c", e=8),
        )

    class LocalKV:
        def __init__(self, slab):
            self.v = kvviews(slab)

        def kaT(self, hm):
            return self.v["kaT"][hm]

        def knT(self, h, a, b):
            return self.v["knT"][h, :, a:b]

        def kcT(self, n):
            return self.v["kcT"][n]

        def vpieces(self, name, ta, tb, ca, cb):
            return [(0, (tb - ta) // 128, self.v[name][ta:tb, ca:cb].rearrange("(k p) e -> p k e", p=128))]

    class GatheredKV:
        def __init__(self, allbuf, rho, nr):
            self.a, self.rho, self.nr = allbuf, rho, nr

        def blk(self, i):
            r0 = (i * self.nr + self.rho) * 128
            return self.a[r0:r0 + 128, :]

        def kaT(self, hm):
            return self.blk(hm // 2)[(hm % 2) * 64:(hm % 2) * 64 + 64, :]

        def knT(self, h, a, b):
            return self.blk(8 + h // 2)[(h % 2) * 64:(h % 2) * 64 + 64, a:b]

        def kcT(self, n):
            return self.blk(16 + n)

        def vpieces(self, name, ta, tb, ca, cb):
            base, tpb = {"va": (18, 256), "vn": (26, 256), "vc": (34, 1024)}[name]
            out = []
            for j in range(ta // tpb, (tb + tpb - 1) // tpb):
                a = max(ta, j * tpb)
                b = min(tb, (j + 1) * tpb)
                if name == "vc":
                    v = self.blk(base + j).rearrange("r (e c) -> (r e) c", e=8)
                else:
                    v = self.blk(base + j).rearrange("r (two c) -> (r two) c", two=2)
                out.append(((a - ta) // 128, (b - a) // 128, v[a - j * tpb:b - j * tpb, ca:cb].rearrange("(k p) e -> p k e", p=128)))
            return out

    seg_dst, seg_src = [], []
    b_gath = Buf("gathered")
    kv_src = kv_all = None
    NBLK = KVR // 128
    for si, (sname, s, r) in enumerate(segs):
        if r == 1:
            loc = dscr(f"kv_loc{si}", [KVR, 2048])
            seg_dst.append(kvviews(loc))
            seg_src.append([LocalKV(loc)])
        else:
            kv_src = dscr("kv_src", [KVR, 2048])
            kv_all = dscr("kv_all", [r * KVR, 2048])
            seg_dst.append(kvviews(kv_src))
            seg_src.append([GatheredKV(kv_all, q, r) for q in range(r)])

    gT_d = dscr("gT_d", [24, 128, NT])
    oT_d = [dscr("oaT_d", [8, 128, NT]), dscr("obT_d", [8, 128, NT]), dscr("ocT_d", [8, 128, NT])]
    u2T_d = dscr("u2T_d", [8, 128, NT])
    aT_d = dscr("aT_d", [32, 128, NT])

    top = ExitStack()
    with top:
        S = Sched(nc, top)
        E = S.eng

        uid = [0]

        def sb(st, name, shape, dt):
            uid[0] += 1
            return st.enter_context(nc.sbuf_tensor(f"{name}_{uid[0]}", list(shape), dt))

        def ps(st, name, shape, dt=F32):
            uid[0] += 1
            return st.enter_context(nc.psum_tensor(f"{name}_{uid[0]}", list(shape), dt))

        ident = sb(top, "ident", [128, 128], BF16)
        ident8 = sb(top, "ident8", [128, 128], BF16)
        gvec = sb(top, "gvec", [128, 5, D], F32)
        gsm = sb(top, "gsm", [128, 3, 128], F32)
        lamt = sb(top, "lamt", [128, 256], F32)
        lam = sb(top, "lam", [128, 8], F32)
        b_const = Buf("const")
        b_gv = Buf("gvec")
        b_lam = Buf("lam")
        epsc = sb(top, "epsc", [128, 1], F32)
        S.op("pool", lambda e: e.memset(epsc[:], EPS), writes=[b_const])
        S.dma("pool", ident[:], idents[0], writes=[b_const])
        S.dma("pool", ident8[:], idents[1], writes=[b_const])

        def load_layer_consts(l):
            for i, g in enumerate((g_pre_mix, g_post_mix, g_pre_mlp, g_post_mlp, g_ple)):
                S.dma("sp", gvec[:, i, :], g[l:l + 1, :].partition_broadcast(128), writes=[b_gv])
            for i, g in enumerate((da_norm, gq_q_norm, gq_k_norm)):
                S.dma("sp", gsm[:, i, :], g[l:l + 1, :].partition_broadcast(128), writes=[b_gv])
            S.dma("sp", lamt[:], da_lambda[l:l + 1, :].partition_broadcast(128), writes=[b_lam])
            li = 0.8 - 0.6 * math.exp(-0.3 * l)
            S.op("dve", lambda e: e.tensor_tensor(out=lamt[:, 0:64], in0=lamt[:, 0:64], in1=lamt[:, 64:128], op=ALU.mult),
                 reads=[b_lam], writes=[b_lam])
            S.op("dve", lambda e: e.tensor_tensor(out=lamt[:, 128:192], in0=lamt[:, 128:192], in1=lamt[:, 192:256], op=ALU.mult),
                 reads=[b_lam], writes=[b_lam])
            S.op("dve", lambda e: e.tensor_reduce(out=lam[:, 0:1], in_=lamt[:, 0:64], axis=AX.X, op=ALU.add),
                 reads=[b_lam], writes=[b_lam])
            S.op("dve", lambda e: e.tensor_reduce(out=lam[:, 1:2], in_=lamt[:, 128:192], axis=AX.X, op=ALU.add),
                 reads=[b_lam], writes=[b_lam])
            S.op("act", lambda e: e.activation(out=lam[:, 3:5], in_=lam[:, 0:2], func=AF.Exp), reads=[b_lam], writes=[b_lam])
            S.op("dve", lambda e: e.tensor_tensor(out=lam[:, 2:3], in0=lam[:, 3:4], in1=lam[:, 4:5], op=ALU.subtract),
                 reads=[b_lam], writes=[b_lam])
            S.op("dve", lambda e: e.tensor_scalar(out=lam[:, 5:6], in0=lam[:, 2:3], scalar1=li, scalar2=-1.0, op0=ALU.add, op1=ALU.mult),
                 reads=[b_lam], writes=[b_lam])
            return li

        def rstd_from_ss(ss_ap, out_ap, n, bufs, post_mul=None):
            S.op("act", lambda e: e.activation(out=out_ap, in_=ss_ap, func=AF.Ln, scale=1.0 / n, bias=epsc[:, 0:1]),
                 reads=list(bufs) + [b_const], writes=bufs)
            S.op("act", lambda e: e.activation(out=out_ap, in_=out_ap, func=AF.Exp, scale=-0.5), reads=bufs, writes=bufs)
            if post_mul is not None:
                S.op("dve", lambda e: e.tensor_scalar(out=out_ap, in0=out_ap, scalar1=float(post_mul), scalar2=None, op0=ALU.mult),
                     reads=bufs, writes=bufs)

        def pe_seq(fns, reads, writes):
            waits = S._deps("pe", reads, writes)
            n = len(fns)
            S.cnt["pe"] += 1
            tick = ("pe", S.cnt["pe"])
            for b in writes:
                b.w = tick
                b.r = {}
            for b in reads:
                if b.r.get("pe", 0) < tick[1]:
                    b.r["pe"] = tick[1]
            for i, fn in enumerate(fns):
                S.ops["pe"].append((fn, waits if i == 0 else [], ("pe", 1) if i == n - 1 else None))

        def mm_group(mms, reads, writes):
            n = len(mms)
            pe_seq([(lambda e, o=o, l=l, r=r, a=(i == 0), z=(i == n - 1): e.matmul(o, l, r, start=a, stop=z))
                    for i, (o, l, r) in enumerate(mms)], reads, writes)

        def maybe_stop(tag):
            if stop_after == tag:
                S.dma("sp", y_out[:, :], x_in[:, :])
                S.barrier()
                S.flush()
                raise _Stop(nc)

        for l in range(depth):
            li = load_layer_consts(l)
            src_h = x_in if l == 0 else h_d
            dst_final = y_out if l == depth - 1 else h_d

            with ExitStack() as st:
                uT = sb(st, "uT", [128, 8, 2048], BF16)
                hb = [sb(st, f"hb{i}", [128, D], F32) for i in range(2)]
                junk = sb(st, "junkA", [128, D], F32)
                ub = [sb(st, f"ub{i}", [128, D], BF16) for i in range(2)]
                ssA = sb(st, "ssA", [128, 4], F32)
                wb = [sb(st, f"wb{i}", [128, 8, 512], BF16) for i in range(3)]
                stg = [sb(st, f"stg{i}", [128, 512], BF16) for i in range(4)]
                xs_f = [sb(st, f"xsf{i}", [128, 4, 128], F32) for i in range(2)]
                sq_f = sb(st, "sqf", [128, 4, 128], F32)
                xr_b = [sb(st, f"xrb{i}", [128, 4, 128], BF16) for i in range(2)]
                t_a = sb(st, "t_a", [128, 4, 64], F32)
                t_b = sb(st, "t_b", [128, 4, 64], F32)
                ss4 = sb(st, "ss4", [128, 8], F32)
                cst = sb(st, "cst", [128, 16, 64], F32)
                snt = sb(st, "snt", [128, 16, 64], F32)
                stT = [sb(st, f"stT{i}", [128, 4, 128], BF16) for i in range(2)]
                pA = [ps(st, f"pA{i}", [128, 512]) for i in range(4)]
                pT = [ps(st, f"pTA{i}", [128, 8, 128], BF16) for i in range(2)]
                r_hb = Rot(hb); r_ub = Rot(ub); r_wb = Rot(wb); r_stg = Rot(stg)
                r_pA = Rot(pA); r_pT = Rot(pT); r_xs = Rot(xs_f); r_xr = Rot(xr_b); r_stT = Rot(stT)
                b_uT = Buf("uT"); b_junk = Buf("junk"); b_ss = Buf("ssA"); b_tmp = Buf("tmpA"); b_cs = Buf("cs")
                evac_i = [0]

                def evac_engine():
                    evac_i[0] += 1
                    return "act" if evac_i[0] % 2 else "dve"

                def copy_op(eng, out, in_, reads, writes):
                    if eng == "act":
                        S.op("act", lambda e: e.copy(out=out, in_=in_), reads=reads, writes=writes)
                    else:
                        S.op(eng, lambda e: e.tensor_copy(out=out, in_=in_), reads=reads, writes=writes)

                for sc in range(len(segs)):
                    t0 = seg_off[sc]
                    dv = seg_dst[sc]
                    S.dma("sp", cst[:], cs_tab[t0:t0 + 2048, :].rearrange("(t p) f -> p t f", p=128), writes=[b_cs])
                    S.dma("sp", snt[:], sn_tab[t0:t0 + 2048, :].rearrange("(t p) f -> p t f", p=128), writes=[b_cs])
                    for tt in range(16):
                        ht, hbuf = r_hb.next()
                        S.dma("sp", ht[:], src_h[t0 + tt * 128:t0 + (tt + 1) * 128, :], writes=[hbuf])
                        S.op("act", lambda e, ht=ht: e.activation(out=junk[:], in_=ht[:], func=AF.Square, accum_out=ssA[:, 0:1]),
                             reads=[hbuf], writes=[b_junk, b_ss])
                        rstd_from_ss(ssA[:, 0:1], ssA[:, 1:2], D, [b_ss])
                        ut, ubuf = r_ub.next()
                        S.op("dve", lambda e, ht=ht, ut=ut: e.scalar_tensor_tensor(out=ut[:], in0=ht[:], scalar=ssA[:, 1:2], in1=gvec[:, 0, :],
                                                                                 op0=ALU.mult, op1=ALU.mult),
                             reads=[hbuf, b_ss, b_gv], writes=[ubuf])
                        pt, pbuf = r_pT.next()
                        pe_seq([(lambda e, pt=pt, ut=ut, c=c: e.transpose(pt[:, c, :], ut[:, c * 128:(c + 1) * 128], ident[:])) for c in range(8)],
                               [ubuf, b_const], [pbuf])
                        copy_op(evac_engine(), uT[:, :, tt * 128:(tt + 1) * 128], pt[:], [pbuf], [b_uT])

                    import os as _os
                    for blk in [int(v) for v in _os.environ.get('KDBG_BLKS', ','.join(map(str, range(21)))).split(',') if v != '']:
                        wt, wbuf = r_wb.next()
                        S.dma("pool", wt[:], w_in[l].rearrange("(c p) n -> p c n", p=128)[:, :, blk * 512:(blk + 1) * 512],
                              writes=[wbuf])
                        if blk in (0, 1, 2, 3, 6, 7, 8, 9):
                            dstT = {0: qaT_d, 1: qaT_d, 2: dv["kaT"], 3: dv["kaT"], 6: qnT_d, 7: qnT_d, 8: dv["knT"], 9: dv["knT"]}[blk]
                            tb0 = t0 if blk in (0, 1, 6, 7) else 0
                            for sub in range(8):
                                hm = (blk % 2) * 8 + sub
                                for tq in range(4):
                                    pa, pab = r_pA.next()
                                    mm_group([(pa[0:64, :], wt[:, c, sub * 64:(sub + 1) * 64], uT[:, c, tq * 512:(tq + 1) * 512])
                                              for c in range(8)], [wbuf, b_uT], [pab])
                                    sg, sgb = r_stg.next()
                                    copy_op(evac_engine(), sg[0:64, :], pa[0:64, :], [pab], [sgb])
                                    S.dma("sp", dstT[hm, :, tb0 + tq * 512:tb0 + (tq + 1) * 512], sg[0:64, :], reads=[sgb])
                        elif blk >= 15:
                            for sub in range(4):
                                ch = (blk - 15) * 4 + sub
                                for tq in range(4):
                                    pa, pab = r_pA.next()
                                    mm_group([(pa[:, :], wt[:, c, sub * 128:(sub + 1) * 128], uT[:, c, tq * 512:(tq + 1) * 512])
                                              for c in range(8)], [wbuf, b_uT], [pab])
                                    sg, sgb = r_stg.next()
                                    S.op("act", lambda e, sg=sg, pa=pa: e.activation(out=sg[:], in_=pa[:], func=AF.Sigmoid),
                                         reads=[pab], writes=[sgb])
                                    S.dma("sp", gT_d[ch, :, t0 + tq * 512:t0 + (tq + 1) * 512], sg[:], reads=[sgb])
                        elif blk in (4, 5, 10, 11):
                            dstV = dv["va"] if blk < 6 else dv["vn"]
                            c0 = (blk % 2) * 512
                            for tt in range(16):
                                pa, pab = r_pA.next()
                                mm_group([(pa[:, :], uT[:, c, tt * 128:(tt + 1) * 128], wt[:, c, :]) for c in range(8)],
                                         [wbuf, b_uT], [pab])
                                sg, sgb = r_stg.next()
                                copy_op(evac_engine(), sg[:], pa[:], [pab], [sgb])
                                S.dma("sp", dstV[tt * 128:(tt + 1) * 128, c0:c0 + 512], sg[:], reads=[sgb])
                        else:
                            for tt in range(16):
                                pa, pab = r_pA.next()
                                mm_group([(pa[:, :], uT[:, c, tt * 128:(tt + 1) * 128], wt[:, c, :]) for c in range(8)],
                                         [wbuf, b_uT], [pab])
                                nh = 4 if blk < 14 else 2
                                gi = 1 if blk < 14 else 2
                                xs_, xsb = r_xs.next()
                                S.op("act", lambda e, xs_=xs_, pa=pa, nh=nh: e.copy(out=xs_[:, 0:nh, :], in_=pa[:, 0:nh * 128].rearrange("p (h d) -> p h d", h=nh)),
                                     reads=[pab], writes=[xsb])
                                if blk == 14 and not _os.environ.get('KDBG_SKIPV'):
                                    sg, sgb = r_stg.next()
                                    S.op("act", lambda e, sg=sg, pa=pa: e.copy(out=sg[:, 0:256], in_=pa[:, 256:512]), reads=[pab], writes=[sgb])
                                    S.dma("sp", dv["vc"][tt * 128:(tt + 1) * 128, :], sg[:, 0:256], reads=[sgb])
                                S.op("dve", lambda e, xs_=xs_, nh=nh: e.tensor_tensor(out=sq_f[:, 0:nh, :], in0=xs_[:, 0:nh, :], in1=xs_[:, 0:nh, :], op=ALU.mult),
                                     reads=[xsb], writes=[b_tmp])
                                S.op("dve", lambda e, nh=nh: e.tensor_reduce(out=ss4[:, 0:nh], in_=sq_f[:, 0:nh, :], axis=AX.X, op=ALU.add),
                                     reads=[b_tmp], writes=[b_tmp])
                                rstd_from_ss(ss4[:, 0:nh], ss4[:, 4:4 + nh], 128, [b_tmp])
                                S.op("dve", lambda e, xs_=xs_, nh=nh: e.tensor_tensor(out=xs_[:, 0:nh, :], in0=xs_[:, 0:nh, :],
                                                                                    in1=ss4[:, 4:4 + nh].unsqueeze(2).to_broadcast([128, nh, 128]), op=ALU.mult),
                                     reads=[xsb, b_tmp], writes=[xsb])
                                S.op("dve", lambda e, xs_=xs_, nh=nh, gi=gi: e.tensor_tensor(out=xs_[:, 0:nh, :], in0=xs_[:, 0:nh, :],
                                                                                           in1=gsm[:, gi:gi + 1, :].to_broadcast([128, nh, 128]), op=ALU.mult),
                                     reads=[xsb, b_gv], writes=[xsb])
                                xr, xrb = r_xr.next()
                                x0 = xs_[:, 0:nh, :].rearrange("p h (i two) -> p h i two", two=2)[:, :, :, 0]
                                x1 = xs_[:, 0:nh, :].rearrange("p h (i two) -> p h i two", two=2)[:, :, :, 1]
                                o0 = xr[:, 0:nh, :].rearrange("p h (i two) -> p h i two", two=2)[:, :, :, 0]
                                o1 = xr[:, 0:nh, :].rearrange("p h (i two) -> p h i two", two=2)[:, :, :, 1]
                                cb = cst[:, tt:tt + 1, :].to_broadcast([128, nh, 64])
                                sbb = snt[:, tt:tt + 1, :].to_broadcast([128, nh, 64])
                                ta = t_a[:, 0:nh, :]
                                tb = t_b[:, 0:nh, :]
                                S.op("dve", lambda e, ta=ta, x0=x0, cb=cb: e.tensor_tensor(out=ta, in0=x0, in1=cb, op=ALU.mult), reads=[xsb, b_cs], writes=[b_tmp])
                                S.op("dve", lambda e, tb=tb, x1=x1, sbb=sbb: e.tensor_tensor(out=tb, in0=x1, in1=sbb, op=ALU.mult), reads=[xsb, b_cs, b_tmp], writes=[b_tmp])
                                S.op("dve", lambda e, ta=ta, tb=tb, o0=o0: e.tensor_tensor(out=o0, in0=ta, in1=tb, op=ALU.subtract), reads=[b_tmp], writes=[xrb])
                                S.op("dve", lambda e, ta=ta, x0=x0, sbb=sbb: e.tensor_tensor(out=ta, in0=x0, in1=sbb, op=ALU.mult), reads=[xsb, b_cs, xrb], writes=[b_tmp])
                                S.op("dve", lambda e, tb=tb, x1=x1, cb=cb: e.tensor_tensor(out=tb, in0=x1, in1=cb, op=ALU.mult), reads=[xsb, b_cs, b_tmp], writes=[b_tmp])
                                S.op("dve", lambda e, ta=ta, tb=tb, o1=o1: e.tensor_tensor(out=o1, in0=ta, in1=tb, op=ALU.add), reads=[b_tmp, xrb], writes=[xrb])
                                pt, pbuf = r_pT.next()
                                pe_seq([(lambda e, pt=pt, xr=xr, hh=hh: e.transpose(pt[:, hh, :], xr[:, hh, :], ident[:])) for hh in range(nh)],
                                       [xrb, b_const], [pbuf])
                                sT, sTb = r_stT.next()
                                copy_op(evac_engine(), sT[:, 0:nh, :], pt[:, 0:nh, :], [pbuf], [sTb])
                                dT = qcT_d[(blk - 12) * 4:(blk - 12) * 4 + 4] if blk < 14 else dv["kcT"]
                                tb0 = t0 if blk < 14 else 0
                                S.dma("sp", dT[:, :, tb0 + tt * 128:tb0 + (tt + 1) * 128].rearrange("h d t -> d h t"), sT[:, 0:nh, :], reads=[sTb])
                S.barrier()
                S.flush()

            def attention(kind):
                with ExitStack() as st:
                    Smax = max(s * r for _, s, r in segs)
                    nkmax = Smax // 128
                    KR = 69 if kind == "da" else 128
                    nmap = 2 if kind == "da" else 1
                    KT = [[sb(st, f"KT{i}{m}", [KR, Smax], BF16) for m in range(nmap)] for i in range(2)]
                    VT = [sb(st, f"VT{i}", [128, nkmax, 129], BF16) for i in range(2)]
                    b_KV = [Buf("kv0"), Buf("kv1")]
                    b_dc = Buf("dc")
                    if kind == "da":
                        QT = [sb(st, f"QT{i}", [KR, 2, 2, 512], BF16) for i in range(2)]
                        dct = sb(st, "dct", [128, len(segs), 8, 128], BF16)
                        for si_ in range(len(segs)):
                            S.dma("pool", dct[:, si_], dcorr[si_].rearrange("h k q -> k h q"), writes=[b_dc])
                        if has_S:
                            dct2 = sb(st, "dct2", [128, 8, 4, 128], BF16)
                            S.dma("pool", dct2[:], dcorr2.rearrange("h r k q -> k h r q"), writes=[b_dc])
                    else:
                        QT = [sb(st, f"QT{i}", [KR, 512], BF16) for i in range(3)]
                    r_QT = Rot(QT)
                    PT = [sb(st, f"PT{i}", [128, 512], BF16) for i in range(3)]
                    r_PT = Rot(PT)
                    Of = [sb(st, f"Of{i}", [128, 4, 129], F32) for i in range(4)]
                    r_Of = Rot(Of)
                    rr = sb(st, "rr", [128, 16], F32)
                    oc = sb(st, "oc", [128, 4, 128], F32)
                    oc2 = sb(st, "oc2", [128, 4, 128], F32)
                    on = [sb(st, f"on{i}", [128, 4, 128], BF16) for i in range(2)]
                    r_on = Rot(on)
                    sto = [sb(st, f"sto{i}", [128, 512], BF16) for i in range(2)]
                    r_sto = Rot(sto)
                    b_post = Buf("post")
                    pS = [ps(st, f"pS{i}", [128, 512]) for i in range(3)]
                    r_pS = Rot(pS)
                    pO = [ps(st, f"pO{i}", [128, 512])[:, 0:258].rearrange("p (s e) -> p s e", s=2) for i in range(4)]
                    pObuf = [Buf("pOa"), Buf("pOb")]
                    pTt = ps(st, "pTt", [128, 8, 128], BF16)[:, 0:4, :]
                    b_pTt = Buf("pTt")
                    for i in range(2):
                        S.op("pool", lambda e, i=i: e.memset(VT[i][:], 1.0), writes=[b_KV[i]])
                    scale = 0.125 if kind == "da" else 128.0 ** -0.5
                    tasks = []
                    it = [0]
                    pv_bank = [0]

                    def finish(grp, qh, q0):
                        ont, onb = r_on.next()
                        if kind == "da":
                            (o1, b1), (o2, b2) = grp
                            S.op("dve", lambda e: e.reciprocal(out=rr[:, 0:4], in_=o1[:, :, 128]), reads=[b1], writes=[b_post])
                            S.op("dve", lambda e: e.reciprocal(out=rr[:, 4:8], in_=o2[:, :, 128]), reads=[b2, b_post], writes=[b_post])
                            S.op("dve", lambda e: e.tensor_scalar(out=rr[:, 4:8], in0=rr[:, 4:8], scalar1=lam[:, 5:6], scalar2=None, op0=ALU.mult),
                                 reads=[b_post, b_lam], writes=[b_post])
                            S.op("dve", lambda e: e.tensor_tensor(out=oc[:], in0=o1[:, :, 0:128], in1=rr[:, 0:4].unsqueeze(2).to_broadcast([128, 4, 128]), op=ALU.mult),
                                 reads=[b1, b_post], writes=[b_post])
                            S.op("dve", lambda e: e.tensor_tensor(out=oc2[:], in0=o2[:, :, 0:128], in1=rr[:, 4:8].unsqueeze(2).to_broadcast([128, 4, 128]), op=ALU.mult),
                                 reads=[b2, b_post], writes=[b_post])
                            S.op("dve", lambda e: e.tensor_tensor(out=oc[:], in0=oc[:], in1=oc2[:], op=ALU.add), reads=[b_post], writes=[b_post])
                            S.op("dve", lambda e: e.tensor_tensor(out=oc2[:], in0=oc[:], in1=oc[:], op=ALU.mult), reads=[b_post], writes=[b_post])
                            S.op("dve", lambda e: e.tensor_reduce(out=rr[:, 8:12], in_=oc2[:], axis=AX.X, op=ALU.add), reads=[b_post], writes=[b_post])
                            rstd_from_ss(rr[:, 8:12], rr[:, 12:16], 128, [b_post], post_mul=(1.0 - li))
                            S.op("dve", lambda e: e.tensor_tensor(out=oc[:], in0=oc[:], in1=rr[:, 12:16].unsqueeze(2).to_broadcast([128, 4, 128]), op=ALU.mult),
                                 reads=[b_post], writes=[b_post])
                            S.op("dve", lambda e: e.tensor_tensor(out=ont[:], in0=oc[:], in1=gsm[:, 0:1, :].to_broadcast([128, 4, 128]), op=ALU.mult),
                                 reads=[b_post, b_gv], writes=[onb])
                        else:
                            (o1, b1), = grp
                            S.op("dve", lambda e: e.reciprocal(out=rr[:, 0:4], in_=o1[:, :, 128]), reads=[b1], writes=[b_post])
                            S.op("dve", lambda e: e.tensor_tensor(out=ont[:], in0=o1[:, :, 0:128], in1=rr[:, 0:4].unsqueeze(2).to_broadcast([128, 4, 128]), op=ALU.mult),
                                 reads=[b1, b_post], writes=[onb])
                        def part2():
                            pe_seq([(lambda e, s=s: e.transpose(pTt[:, s, :], ont[:, s, :], ident[:])) for s in range(4)], [onb, b_const], [b_pTt])
                            so, sob = r_sto.next()
                            S.op("act", lambda e: e.copy(out=so[:], in_=pTt[:].rearrange("p s q -> p (s q)")), reads=[b_pTt], writes=[sob])
                            dst = oT_d[0] if kind == "da" else oT_d[2]
                            S.dma("sp", dst[qh, :, q0:q0 + 512], so[:], reads=[sob])
                        return part2

                    for si, (sname, NQ, RK) in enumerate(segs):
                        tok0 = seg_off[si]
                        ko = kv_off[si]
                        SL = NQ * RK
                        nkc = SL // 128
                        nqc = NQ // 512
                        srcs = seg_src[si]
                        gdep = [b_gath] if RK > 1 else []
                        nkvh = 8 if kind == "da" else 2
                        for kvh in range(nkvh):
                            slot = it[0] % 2
                            it[0] += 1
                            kvb = b_KV[slot]

                            def load_kv(kvh=kvh, slot=slot, kvb=kvb, SL=SL, srcs=srcs, gdep=gdep, ko=ko, RK=RK):
                                for rho in range(RK):
                                    c0, c1 = rho * 2048, (rho + 1) * 2048
                                    if kind == "da":
                                        for m in range(2):
                                            S.dma("sp", KT[slot][m][0:64, c0:c1], srcs[rho].kaT(kvh * 2 + m), reads=gdep, writes=[kvb])
                                        vname = "va"
                                    else:
                                        S.dma("sp", KT[slot][0][:, c0:c1], srcs[rho].kcT(kvh), reads=gdep, writes=[kvb])
                                        vname = "vc"
                                    for (ko_, nk_, vap) in srcs[rho].vpieces(vname, 0, 2048, kvh * 128, (kvh + 1) * 128):
                                        S.dma("sp", VT[slot][:, rho * 16 + ko_:rho * 16 + ko_ + nk_, 0:128], vap, reads=gdep, writes=[kvb])
                                if kind == "da":
                                    for m in range(2):
                                        S.dma("pool", KT[slot][m][64:69, 0:SL], kaug[kvh, :, ko:ko + SL], writes=[kvb])

                            qheads = [kvh] if kind == "da" else [kvh * 4 + g for g in range(4)]
                            first = [True]
                            for qh in qheads:
                                for qc in range(nqc):
                                    q0 = tok0 + qc * 512
                                    qt, qb_ = r_QT.next()

                                    def load_q(qt=qt, qb_=qb_, qh=qh, q0=q0):
                                        if kind == "da":
                                            for m in range(2):
                                                for lr in range(2):
                                                    S.dma("sp", qt[0:64, m, lr, :], qaT_d[qh * 2 + m, :, q0:q0 + 512], writes=[qb_])
                                                    S.dma("pool", qt[64:69, m, lr, :], qaug[lr, :, q0:q0 + 512], writes=[qb_])
                                        else:
                                            S.dma("sp", qt[:, :], qcT_d[qh, :, q0:q0 + 512], writes=[qb_])

                                    grp_Of = []
                                    for m in range(nmap):
                                        for kc in range(nkc):
                                            last = (kc == nkc - 1)
                                            state = {}

                                            def qk(kc=kc, m=m, qt=qt, qb_=qb_, slot=slot, kvb=kvb, qc=qc, state=state, kvh=kvh, si=si, RK=RK):
                                                pst, psb = r_pS.next()
                                                state["ps"] = (pst, psb)
                                                kt = KT[slot][m]
                                                if kind != "da":
                                                    mm_group([(pst[:, :], kt[:, kc * 128:(kc + 1) * 128], qt[:, :])], [kvb, qb_], [psb])
                                                    return
                                                rho, c = kc // 16, kc % 16
                                                if c < 4 * qc or c >= 4 * qc + 4:
                                                    lr = 0 if c < 4 * qc else 1
                                                    mm_group([(pst[:, :], kt[0:69, kc * 128:(kc + 1) * 128], qt[0:69, m, lr, :])], [kvb, qb_], [psb])
                                                    return
                                                t = c - 4 * qc
                                                fns = []
                                                for s in range(4):
                                                    lr = 0 if s >= t else 1
                                                    fns.append(lambda e, pst=pst, kt=kt, qt=qt, s=s, lr=lr, d=(s == t): e.matmul(
                                                        pst[:, s * 128:(s + 1) * 128], kt[0:69, kc * 128:(kc + 1) * 128], qt[0:69, m, lr, s * 128:(s + 1) * 128],
                                                        start=True, stop=not d))
                                                    if s == t:
                                                        two = RK > 1
                                                        fns.append(lambda e, pst=pst, s=s, two=two: e.matmul(pst[:, s * 128:(s + 1) * 128], ident[:], dct[:, si, kvh, :],
                                                                                                             start=False, stop=not two))
                                                        if two:
                                                            fns.append(lambda e, pst=pst, s=s, rho=rho: e.matmul(pst[:, s * 128:(s + 1) * 128], ident[:], dct2[:, kvh, rho, :],
                                                                                                                 start=False, stop=True))
                                                pe_seq(fns, [kvb, qb_, b_dc, b_const], [psb])

                                            def ex(state=state):
                                                pst, psb = state["ps"]
                                                ptt, ptb = r_PT.next()
                                                state["pt"] = (ptt, ptb)
                                                S.op("act", lambda e: e.activation(out=ptt[:], in_=pst[:], func=AF.Exp, scale=scale),
                                                     reads=[psb], writes=[ptb])

                                            def pv(kc=kc, state=state, slot=slot, kvb=kvb, last=last):
                                                ptt, ptb = state["pt"]
                                                bank = pv_bank[0]
                                                pe_seq([(lambda e, s=s: e.matmul(pO[bank * 2 + s // 2][:, s % 2, :], ptt[:, s * 128:(s + 1) * 128], VT[slot][:, kc, :],
                                                                                 start=(kc == 0 and s % 2 == 0), stop=last, skip_group_check=True)) for s in range(4)],
                                                       [ptb, kvb], [pObuf[bank]])

                                            pre = None
                                            if kc == 0 and m == 0:
                                                def pre(load_q=load_q, load_kv=load_kv, f=first[0]):
                                                    if f:
                                                        load_kv()
                                                    load_q()
                                                first[0] = False
                                            postf = None
                                            if last:
                                                def postf(m=m, qh=qh, q0=q0, grp_Of=grp_Of):
                                                    oft, ofb = r_Of.next()
                                                    bank = pv_bank[0]
                                                    for half in range(2):
                                                        S.op("dve", lambda e, oft=oft, bank=bank, half=half: e.tensor_copy(out=oft[:, half * 2:half * 2 + 2, :], in_=pO[bank * 2 + half][:]),
                                                             reads=[pObuf[bank]], writes=[ofb])
                                                    pv_bank[0] = 1 - bank
                                                    grp_Of.append((oft, ofb))
                                                    if m == nmap - 1:
                                                        return finish(grp_Of, qh, q0)
                                                    return None
                                            tasks.append((pre, qk, ex, pv, postf))

                    emit_pipelined(tasks, LOOK=2, PRE=24, DEFER=4)
                    S.barrier()
                    S.flush()

            def na_attention():
                with ExitStack() as st:
                    Gt = sb(st, "Gt", [128, 16, NA_E * 64], BF16)
                    b_G = Buf("G")
                    for h in range(16):
                        S.dma("pool", Gt[:, h, :], na_g[l, h], writes=[b_G])
                    if has_S:
                        Gs = [sb(st, f"Gs{i}", [128, 4, 1152], BF16) for i in range(2)]
                        b_Gs = [Buf("Gs0"), Buf("Gs1")]
                        KTs = [sb(st, f"sKT{i}", [72, 4, 768], BF16) for i in range(4)]
                        VTs = [sb(st, f"sVT{i}", [128, 4, 6, 65], BF16) for i in range(4)]
                        QTs = [sb(st, f"sQT{i}", [72, 6, 512], BF16) for i in range(4)]
                        kvqs = [Buf(f"skvq{i}") for i in range(4)]
                        for i in range(4):
                            S.op("pool", lambda e, i=i: e.memset(VTs[i][:], 1.0), writes=[kvqs[i]])
                            for rho in range(4):
                                S.dma("pool", KTs[i][64:72, rho, :], na_kis[:, :], writes=[kvqs[i]])
                            S.dma("pool", QTs[i][64:72, :, :], na_mqs[0 if i == 0 else (2 if i == 3 else 1)], writes=[kvqs[i]])
                    KT = [sb(st, f"nKT{i}", [66, 1024], BF16) for i in range(3)]
                    VT = [sb(st, f"nVT{i}", [128, 8, 65], BF16) for i in range(3)]
                    QT = [sb(st, f"nQT{i}", [66, 8, 512], BF16) for i in range(3)]
                    kvq = [Buf(f"nkvq{i}") for i in range(3)]
                    for i in range(3):
                        S.op("pool", lambda e, i=i: e.memset(VT[i][:], 1.0), writes=[kvq[i]])
                        S.dma("pool", KT[i][64:66, :], na_ki[:, :], writes=[kvq[i]])
                    PT = [sb(st, f"nPT{i}", [128, 512], BF16) for i in range(3)]
                    r_PT = Rot(PT)
                    Of = [sb(st, f"nOf{i}", [128, 4, 65], F32) for i in range(2)]
                    r_Of = Rot(Of)
                    rr = sb(st, "nrr", [128, 4], F32)
                    on = [sb(st, f"non{i}", [128, 4, 64], BF16) for i in range(2)]
                    r_on = Rot(on)
                    sto = [sb(st, f"nsto{i}", [64, 512], BF16) for i in range(2)]
                    r_sto = Rot(sto)
                    b_post = Buf("npost")
                    pS = [ps(st, f"npS{i}", [128, 512]) for i in range(3)]
                    r_pS = Rot(pS)
                    pO = [ps(st, f"npO{i}", [128, 512])[:, 0:260].rearrange("p (s e) -> p s e", s=4) for i in range(2)]
                    pObuf = [Buf("npOa"), Buf("npOb")]
                    pTt = ps(st, "npTt", [128, 8, 128], BF16)[0:64, 0:4, :]
                    b_pTt = Buf("npTt")
                    tasks = []
                    it = [0]
                    its = [0]
                    pv_bank = [0]

                    def mk_post(h, q0):
                        def postf():
                            bank = pv_bank[0]
                            pv_bank[0] = 1 - bank
                            oft, ofb = r_Of.next()
                            S.op("dve", lambda e: e.tensor_copy(out=oft[:], in_=pO[bank][:]), reads=[pObuf[bank]], writes=[ofb])
                            S.op("dve", lambda e: e.reciprocal(out=rr[:, 0:4], in_=oft[:, :, 64]), reads=[ofb], writes=[b_post])
                            ont, onb = r_on.next()
                            S.op("dve", lambda e: e.tensor_tensor(out=ont[:], in0=oft[:, :, 0:64], in1=rr[:, 0:4].unsqueeze(2).to_broadcast([128, 4, 64]), op=ALU.mult),
                                 reads=[ofb, b_post], writes=[onb])
                            def part2():
                                pe_seq([(lambda e, s=s: e.transpose(pTt[:, s, :], ont[:, s, :], ident[:])) for s in range(4)], [onb, b_const], [b_pTt])
                                so, sob = r_sto.next()
                                S.op("act", lambda e: e.copy(out=so[:], in_=pTt[:].rearrange("p s q -> p (s q)")), reads=[b_pTt], writes=[sob])
                                S.dma("sp", oT_d[1][h // 2, (h % 2) * 64:(h % 2) * 64 + 64, q0:q0 + 512], so[:], reads=[sob])
                            return part2
                        return postf

                    def mk_ex(state):
                        def ex():
                            pst, psb = state["ps"]
                            ptt, ptb = r_PT.next()
                            state["pt"] = (ptt, ptb)
                            S.op("act", lambda e: e.activation(out=ptt[:], in_=pst[:], func=AF.Exp, scale=0.125), reads=[psb], writes=[ptb])
                        return ex

                    def mk_pv(state, vt_ap, kb, firstt, lastt):
                        def pv():
                            ptt, ptb = state["pt"]
                            po = pO[pv_bank[0]]
                            pe_seq([(lambda e, s=s: e.matmul(po[:, s, :], ptt[:, s * 128:(s + 1) * 128], vt_ap, start=(firstt and s == 0), stop=lastt, skip_group_check=True))
                                    for s in range(4)], [ptb, kb], [pObuf[pv_bank[0]]])
                        return pv

                    for si, (sname, NQ, RK) in enumerate(segs):
                        tok0 = seg_off[si]
                        if RK == 1:
                            sv = seg_src[si][0]
                            rows = NQ // 64
                            nqb = rows // 8
                            for qb in range(nqb):
                                var = 0 if qb == 0 else (2 if qb == nqb - 1 else 1)
                                R0 = 8 * qb
                                tlist = [t for t in range(8) if 0 <= R0 - 4 + 2 * t and R0 - 4 + 2 * t + 1 < rows]
                                for h in range(16):
                                    slot = it[0] % 3
                                    it[0] += 1
                                    kb = kvq[slot]

                                    def pre(slot=slot, kb=kb, h=h, R0=R0, tok0=tok0, tlist=tlist, sv=sv, var_of=(var,)):
                                        ta, tb = tlist[0], tlist[-1] + 1
                                        k0 = (R0 - 4) * 64
                                        S.dma("sp", KT[slot][0:64, ta * 128:tb * 128], sv.knT(h, k0 + ta * 128, k0 + tb * 128), writes=[kb])
                                        for (ko_, nk_, vap) in sv.vpieces("vn", k0 + ta * 128, k0 + tb * 128, h * 64, (h + 1) * 64):
                                            S.dma("sp", VT[slot][:, ta + ko_:ta + ko_ + nk_, 0:64], vap, writes=[kb])
                                        S.dma("sp", QT[slot][0:64, ta:tb, :],
                                              qnT_d[h, :, tok0 + R0 * 64:tok0 + R0 * 64 + 512].unsqueeze(1).to_broadcast([64, tb - ta, 512]), writes=[kb])
                                        if h < 3:
                                            S.dma("pool", QT[slot][64:66, :, :], na_mq[var_of[0]], writes=[kb])

                                    for ti, t in enumerate(tlist):
                                        state = {}
                                        lastt = (ti == len(tlist) - 1)

                                        def qk(t=t, slot=slot, kb=kb, h=h, var=var, state=state):
                                            pst, psb = r_pS.next()
                                            state["ps"] = (pst, psb)
                                            off = (14 - 2 * t) * 64
                                            mm_group([(pst[:, :], KT[slot][0:66, t * 128:(t + 1) * 128], QT[slot][0:66, t, :]),
                                                      (pst[:, :], ident8[:], Gt[:, h, off:off + 512])], [kb, b_G, b_const], [psb])

                                        tasks.append((pre if ti == 0 else None, qk, mk_ex(state), mk_pv(state, VT[slot][:, t, :], kb, ti == 0, lastt),
                                                      mk_post(h, tok0 + R0 * 64) if lastt else None))
                        else:
                            srcs = seg_src[si]
                            for h in range(16):
                                gslot = h % 2

                                def load_g(h=h, gslot=gslot):
                                    S.dma("pool", Gs[gslot][:], na_gs[l, h], writes=[b_Gs[gslot]])

                                for qb in range(4):
                                    var = 0 if qb == 0 else (2 if qb == 3 else 1)
                                    dl = [dd for dd in range(-1, 5) if 0 <= 4 * qb + dd <= 15]
                                    slot = qb
                                    kb = kvqs[slot]

                                    def pre(slot=slot, kb=kb, h=h, qb=qb, dl=dl, srcs=srcs, tok0=tok0, load_g=load_g, var_of=(var,)):
                                        if qb == 0:
                                            load_g()
                                        j0, j1 = dl[0] + 1, dl[-1] + 2
                                        k0 = 128 * (4 * qb - 1)
                                        for rho in range(4):
                                            S.dma("sp", KTs[slot][0:64, rho, j0 * 128:j1 * 128], srcs[rho].knT(h, k0 + j0 * 128, k0 + j1 * 128), reads=[b_gath], writes=[kb])
                                            for (ko_, nk_, vap) in srcs[rho].vpieces("vn", k0 + j0 * 128, k0 + j1 * 128, h * 64, (h + 1) * 64):
                                                S.dma("sp", VTs[slot][:, rho, j0 + ko_:j0 + ko_ + nk_, 0:64], vap, reads=[b_gath], writes=[kb])
                                        S.dma("sp", QTs[slot][0:64, j0:j1, :],
                                              qnT_d[h, :, tok0 + qb * 512:tok0 + qb * 512 + 512].unsqueeze(1).to_broadcast([64, j1 - j0, 512]), writes=[kb])

                                    combos = [(rho, dd) for rho in range(4) for dd in dl]
                                    for ci_, (rho, dd) in enumerate(combos):
                                        state = {}
                                        lastt = (ci_ == len(combos) - 1)

                                        def qk(rho=rho, dd=dd, slot=slot, kb=kb, gslot=gslot, var=var, state=state):
                                            pst, psb = r_pS.next()
                                            state["ps"] = (pst, psb)
                                            off = (32 - 8 * dd) * 16
                                            j = dd + 1
                                            mm_group([(pst[:, :], KTs[slot][0:72, rho, j * 128:(j + 1) * 128], QTs[slot][0:72, j, :]),
                                                      (pst[:, :], ident8[:], Gs[gslot][:, rho, off:off + 512])], [kb, b_Gs[gslot], b_G, b_const], [psb])

                                        tasks.append((pre if ci_ == 0 else None, qk, mk_ex(state), mk_pv(state, VTs[slot][:, rho, dd + 1, :], kb, ci_ == 0, lastt),
                                                      mk_post(h, tok0 + qb * 512) if lastt else None))
                    emit_pipelined(tasks, LOOK=2, PRE=6, DEFER=3)
                    S.barrier()
                    S.flush()

            if has_S:
                for bi in range(NBLK):
                    S.collective(lambda e, bi=bi: e.collective_compute("AllGather", ALU.bypass, replica_groups=[[0, 1, 2, 3], [4, 5, 6, 7]],
                                                                       ins=[kv_src[bi * 128:(bi + 1) * 128, :].opt()],
                                                                       outs=[kv_all[bi * 512:(bi + 1) * 512, :].opt()]), writes=[b_gath])
            maybe_stop("A")
            attention("da")
            maybe_stop("da")
            na_attention()
            maybe_stop("na")
            attention("gq")
            maybe_stop("gq")

            with ExitStack() as st:
                CH = 256
                Wb = [sb(st, f"Wb{i}", [128, 8, D], BF16) for i in range(4)]
                b_W = Buf("W")
                for i, w in enumerate(w_br + [w_o]):
                    S.dma("pool", Wb[i][:], w[l].rearrange("(c p) n -> p c n", p=128), writes=[b_W])
                oT = [[sb(st, f"oT{j}{i}", [128, 8, CH], BF16) for i in range(3)] for j in range(2)]
                gT = [sb(st, f"gT{j}", [128, 24, CH], BF16) for j in range(2)]
                b_in = [Buf("cin0"), Buf("cin1")]
                mm = [sb(st, f"mm{i}", [128, CH], F32) for i in range(3)]
                b_mm = [Buf(f"mm{i}") for i in range(3)]
                mT = sb(st, "mT", [128, 8, CH], BF16)
                b_mT = Buf("mT")
                hb = [sb(st, f"chb{i}", [128, D], F32) for i in range(2)]
                r_hb = Rot(hb)
                yb = sb(st, "yb", [128, D], F32)
                junk = sb(st, "junkC", [128, D], F32)
                ssC = sb(st, "ssC", [128, 4], F32)
                ub = [sb(st, f"cub{i}", [128, D], BF16) for i in range(2)]
                r_ub = Rot(ub)
                sT = [sb(st, f"csT{i}", [128, 8, 128], BF16) for i in range(2)]
                r_sT = Rot(sT)
                b_y = Buf("y"); b_ss = Buf("ssC"); b_junk = Buf("junkC")
                pB = [ps(st, f"pB{i}", [128, 512]) for i in range(4)]
                r_pB = Rot(pB)
                pOo = ps(st, "pOo", [128, D])
                b_pOo = Buf("pOo")
                pT = ps(st, "pTC", [128, 8, 128], BF16)
                b_pT = Buf("pTC")
                nch = NT // CH

                def load_c(ci):
                    j = ci % 2
                    t0 = ci * CH
                    for b in range(3):
                        S.dma("sp", oT[j][b][:], oT_d[b][:, :, t0:t0 + CH].rearrange("c p t -> p c t"), writes=[b_in[j]])
                    S.dma("sp", gT[j][:], gT_d[:, :, t0:t0 + CH].rearrange("c p t -> p c t"), writes=[b_in[j]])

                load_c(0)
                for ci in range(nch):
                    j = ci % 2
                    t0 = ci * CH
                    if ci + 1 < nch:
                        load_c(ci + 1)
                    for cc in range(8):
                        for b in range(3):
                            pb, pbb = r_pB.next()
                            mm_group([(pb[:, 0:CH], Wb[b][:, e_, cc * 128:(cc + 1) * 128], oT[j][b][:, e_, :]) for e_ in range(8)], [b_W, b_in[j]], [pbb])
                            S.op("dve", lambda e, pb=pb, b=b, cc=cc, j=j: e.tensor_tensor(out=mm[b][:], in0=pb[:, 0:CH], in1=gT[j][:, b * 8 + cc, :], op=ALU.mult),
                                 reads=[pbb, b_in[j]], writes=[b_mm[b]])
                        S.op("pool", lambda e: e.tensor_tensor(out=mm[0][:], in0=mm[0][:], in1=mm[1][:], op=ALU.add), reads=[b_mm[0], b_mm[1]], writes=[b_mm[0]])
                        S.op("pool", lambda e, cc=cc: e.tensor_tensor(out=mT[:, cc, :], in0=mm[0][:], in1=mm[2][:], op=ALU.add), reads=[b_mm[0], b_mm[2]], writes=[b_mT])
                    for tt in range(CH // 128):
                        tk = t0 + tt * 128
                        ht, hbuf = r_hb.next()
                        S.dma("sp", ht[:], src_h[tk:tk + 128, :], writes=[hbuf])
                        for nn in range(2):
                            mm_group([(pOo[:, nn * 512:(nn + 1) * 512], mT[:, cc, tt * 128:(tt + 1) * 128], Wb[3][:, cc, nn * 512:(nn + 1) * 512]) for cc in range(8)],
                                     [b_mT, b_W], [b_pOo])
                        S.op("act", lambda e: e.activation(out=junk[:], in_=pOo[:], func=AF.Square, accum_out=ssC[:, 0:1]), reads=[b_pOo], writes=[b_junk, b_ss])
                        rstd_from_ss(ssC[:, 0:1], ssC[:, 1:2], D, [b_ss])
                        S.op("dve", lambda e: e.scalar_tensor_tensor(out=yb[:], in0=pOo[:], scalar=ssC[:, 1:2], in1=gvec[:, 1, :], op0=ALU.mult, op1=ALU.mult),
                             reads=[b_pOo, b_ss, b_gv], writes=[b_y])
                        S.op("pool", lambda e, ht=ht: e.tensor_tensor(out=ht[:], in0=ht[:], in1=yb[:], op=ALU.add), reads=[hbuf, b_y], writes=[hbuf])
                        S.dma("sp", h_d[tk:tk + 128, :], ht[:], reads=[hbuf])
                        S.op("act", lambda e, ht=ht: e.activation(out=junk[:], in_=ht[:], func=AF.Square, accum_out=ssC[:, 2:3]), reads=[hbuf], writes=[b_junk, b_ss])
                        rstd_from_ss(ssC[:, 2:3], ssC[:, 3:4], D, [b_ss])
                        ut, ubuf = r_ub.next()
                        S.op("dve", lambda e, ht=ht, ut=ut: e.scalar_tensor_tensor(out=ut[:], in0=ht[:], scalar=ssC[:, 3:4], in1=gvec[:, 2, :], op0=ALU.mult, op1=ALU.mult),
                             reads=[hbuf, b_ss, b_gv], writes=[ubuf])
                        pe_seq([(lambda e, ut=ut, c=c: e.transpose(pT[:, c, :], ut[:, c * 128:(c + 1) * 128], ident[:])) for c in range(8)], [ubuf, b_const], [b_pT])
                        stt, stb = r_sT.next()
                        S.op("act", lambda e, stt=stt: e.copy(out=stt[:], in_=pT[:]), reads=[b_pT], writes=[stb])
                        S.dma("sp", u2T_d[:, :, tk:tk + 128].rearrange("c p t -> p c t"), stt[:], reads=[stb])
                S.barrier()
                S.flush()

            maybe_stop("C")
            with ExitStack() as st:
                uT = sb(st, "u2T", [128, 8, 2048], BF16)
                b_uT = Buf("u2T")
                wb = [sb(st, f"dwb{i}", [128, 8, 512], BF16) for i in range(3)]
                r_wb = Rot(wb)
                rl = [sb(st, f"rl{i}", [128, 512], F32) for i in range(3)]
                r_rl = Rot(rl)
                stg = [sb(st, f"dstg{i}", [128, 512], BF16) for i in range(3)]
                r_stg = Rot(stg)
                pA = [ps(st, f"dpA{i}", [128, 512]) for i in range(4)]
                r_pA = Rot(pA)
                for sc in range(NT // 2048):
                    t0 = sc * 2048
                    S.dma("sp", uT[:], u2T_d[:, :, t0:t0 + 2048].rearrange("c p t -> p c t"), writes=[b_uT])
                    for blk in range(8):
                        wt, wbuf = r_wb.next()
                        S.dma("pool", wt[:], w_up[l].rearrange("(c p) n -> p c n", p=128)[:, :, blk * 512:(blk + 1) * 512], writes=[wbuf])
                        for sub in range(4):
                            for tq in range(4):
                                pa, pab = r_pA.next()
                                mm_group([(pa[:, :], wt[:, c, sub * 128:(sub + 1) * 128], uT[:, c, tq * 512:(tq + 1) * 512]) for c in range(8)], [wbuf, b_uT], [pab])
                                rt, rb = r_rl.next()
                                S.op("act", lambda e, rt=rt, pa=pa: e.activation(out=rt[:], in_=pa[:], func=AF.Relu), reads=[pab], writes=[rb])
                                sg, sgb = r_stg.next()
                                S.op("pool", lambda e, rt=rt, sg=sg: e.tensor_tensor(out=sg[:], in0=rt[:], in1=rt[:], op=ALU.mult), reads=[rb], writes=[sgb])
                                S.dma("sp", aT_d[blk * 4 + sub, :, t0 + tq * 512:t0 + (tq + 1) * 512], sg[:], reads=[sgb])
                S.barrier()
                S.flush()

            maybe_stop("D1")
            with ExitStack() as st:
                Wd = sb(st, "Wd", [128, 32, D], BF16)
                Wg = sb(st, "Wg", [128, 8, D], BF16)
                Wp = sb(st, "Wp", [128, 2, D], BF16)
                b_W = Buf("W2")
                for q4 in range(4):
                    S.dma("pool", Wd[:, q4 * 8:(q4 + 1) * 8, :], w_down[l, q4 * 1024:(q4 + 1) * 1024, :].rearrange("(c p) n -> p c n", p=128), writes=[b_W])
                S.dma("pool", Wg[:], w_ple_gate[l].rearrange("(c p) n -> p c n", p=128), writes=[b_W])
                S.dma("pool", Wp[:], w_ple[l].rearrange("(c p) n -> p c n", p=128), writes=[b_W])
                aT = [sb(st, f"aT{j}", [128, 32, 512], BF16) for j in range(2)]
                b_a = [Buf("a0"), Buf("a1")]
                hb = [sb(st, f"ehb{i}", [128, D], F32) for i in range(2)]
                r_hb = Rot(hb)
                pl = [sb(st, f"pl{i}", [128, PLE], F32) for i in range(2)]
                r_pl = Rot(pl)
                plb = sb(st, "plb", [128, PLE], BF16)
                b_plb = Buf("plb")
                yb = sb(st, "eyb", [128, D], F32)
                gt = sb(st, "egt", [128, D], F32)
                junk = sb(st, "junkE", [128, D], F32)
                ssE = sb(st, "ssE", [128, 4], F32)
                hbf = sb(st, "hbf", [128, D], BF16)
                hT = sb(st, "hT", [128, 8, 128], BF16)
                pTs = sb(st, "pTs", [128, 2, 128], BF16)
                b_y = Buf("ey"); b_g = Buf("eg"); b_ss = Buf("ssE"); b_junk = Buf("junkE"); b_hbf = Buf("hbf"); b_hT = Buf("hT"); b_pTs = Buf("pTs")
                pF = ps(st, "pF", [128, D]); b_pF = Buf("pF")
                pG = ps(st, "pG", [128, D]); b_pG = Buf("pG")
                pE = ps(st, "pE", [128, D]); b_pE = Buf("pE")
                pT = ps(st, "pTE", [128, 8, 128], BF16); b_pT = Buf("pTE")
                pT2 = ps(st, "pTE2", [128, 8, 128], BF16)[:, 0:2, :]; b_pT2 = Buf("pTE2")
                nch = NT // 512

                def load_a(ci):
                    j = ci % 2
                    for q4 in range(4):
                        S.dma("sp", aT[j][:, q4 * 8:(q4 + 1) * 8, :], aT_d[q4 * 8:(q4 + 1) * 8, :, ci * 512:(ci + 1) * 512].rearrange("c p t -> p c t"), writes=[b_a[j]])

                load_a(0)
                for ci in range(nch):
                    j = ci % 2
                    if ci + 1 < nch:
                        load_a(ci + 1)
                    for tt in range(4):
                        tk = ci * 512 + tt * 128
                        ht, hbuf = r_hb.next()
                        S.dma("sp", ht[:], h_d[tk:tk + 128, :], writes=[hbuf])
                        plt, plbuf = r_pl.next()
                        S.dma("sp", plt[:], p_in[l, tk:tk + 128, :], writes=[plbuf])
                        for nn in range(2):
                            mm_group([(pF[:, nn * 512:(nn + 1) * 512], aT[j][:, ch, tt * 128:(tt + 1) * 128], Wd[:, ch, nn * 512:(nn + 1) * 512]) for ch in range(32)],
                                     [b_a[j], b_W], [b_pF])
                        S.op("act", lambda e: e.activation(out=junk[:], in_=pF[:], func=AF.Square, accum_out=ssE[:, 0:1]), reads=[b_pF], writes=[b_junk, b_ss])
                        rstd_from_ss(ssE[:, 0:1], ssE[:, 1:2], D, [b_ss])
                        S.op("dve", lambda e: e.scalar_tensor_tensor(out=yb[:], in0=pF[:], scalar=ssE[:, 1:2], in1=gvec[:, 3, :], op0=ALU.mult, op1=ALU.mult),
                             reads=[b_pF, b_ss, b_gv], writes=[b_y])
                        S.op("pool", lambda e, ht=ht: e.tensor_tensor(out=ht[:], in0=ht[:], in1=yb[:], op=ALU.add), reads=[hbuf, b_y], writes=[hbuf])
                        S.op("dve", lambda e, ht=ht: e.tensor_copy(out=hbf[:], in_=ht[:]), reads=[hbuf], writes=[b_hbf])
                        pe_seq([(lambda e, c=c: e.transpose(pT[:, c, :], hbf[:, c * 128:(c + 1) * 128], ident[:])) for c in range(8)], [b_hbf, b_const], [b_pT])
                        S.op("act", lambda e: e.copy(out=hT[:], in_=pT[:]), reads=[b_pT], writes=[b_hT])
                        S.op("pool", lambda e, plt=plt: e.tensor_copy(out=plb[:], in_=plt[:]), reads=[plbuf], writes=[b_plb])
                        pe_seq([(lambda e, c=c: e.transpose(pT2[:, c, :], plb[:, c * 128:(c + 1) * 128], ident[:])) for c in range(2)], [b_plb, b_const], [b_pT2])
                        S.op("dve", lambda e: e.tensor_copy(out=pTs[:], in_=pT2[:]), reads=[b_pT2], writes=[b_pTs])
                        for nn in range(2):
                            mm_group([(pG[:, nn * 512:(nn + 1) * 512], hT[:, c, :], Wg[:, c, nn * 512:(nn + 1) * 512]) for c in range(8)], [b_hT, b_W], [b_pG])
                        for nn in range(2):
                            mm_group([(pE[:, nn * 512:(nn + 1) * 512], pTs[:, c, :], Wp[:, c, nn * 512:(nn + 1) * 512]) for c in range(2)], [b_pTs, b_W], [b_pE])
                        S.op("act", lambda e: e.activation(out=gt[:], in_=pG[:], func=AF.Sigmoid), reads=[b_pG], writes=[b_g])
                        S.op("dve", lambda e: e.tensor_tensor(out=gt[:], in0=pE[:], in1=gt[:], op=ALU.mult), reads=[b_pE, b_g], writes=[b_g])
                        S.op("act", lambda e: e.activation(out=junk[:], in_=gt[:], func=AF.Square, accum_out=ssE[:, 2:3]), reads=[b_g], writes=[b_junk, b_ss])
                        rstd_from_ss(ssE[:, 2:3], ssE[:, 3:4], D, [b_ss])
                        S.op("dve", lambda e: e.scalar_tensor_tensor(out=yb[:], in0=gt[:], scalar=ssE[:, 3:4], in1=gvec[:, 4, :], op0=ALU.mult, op1=ALU.mult),
                             reads=[b_g, b_ss, b_gv], writes=[b_y])
                        S.op("pool", lambda e, ht=ht: e.tensor_tensor(out=ht[:], in0=ht[:], in1=yb[:], op=ALU.add), reads=[hbuf, b_y], writes=[hbuf])
                        S.dma("sp", dst_final[tk:tk + 128, :], ht[:], reads=[hbuf])
                S.barrier()
                S.flush()
    return nc


def _rope_tables(Sl):
    t = np.arange(Sl)
    row = (t // GRID_W).astype(np.float32)
    col = (t % GRID_W).astype(np.float32)
    half = 64
    freqs = (np.float32(10000.0) ** (-np.arange(0, half, 2, dtype=np.float32) / np.float32(half))).astype(np.float32)
    ang = np.concatenate([row[:, None] * freqs, col[:, None] * freqs], axis=-1).astype(np.float32)
    return np.cos(ang).astype(np.float32), np.sin(ang).astype(np.float32)


def _aug_tables(segs, rank):
    qcols, kcols = [], []
    dcorr = np.zeros((len(segs), 8, 128, 128), np.float32)
    dcorr2 = np.zeros((8, 4, 128, 128), np.float32)
    kk = np.arange(128)[:, None]
    qq = np.arange(128)[None, :]
    for si, (_, nq, R) in enumerate(segs):
        a = np.arange(nq)
        one = np.ones(nq, np.float32)
        qcols.append(np.stack([(a // 128).astype(np.float32), (a % 128).astype(np.float32), one, one, one]))
        mult = 1.0 if R == 1 else 4.0
        ks = np.zeros((8, 5, nq * R), np.float32)
        for h in range(8):
            m = 2.0 ** (-(h + 1))
            for rho in range(R):
                bidx = np.arange(nq)
                sl = slice(rho * nq, (rho + 1) * nq)
                ks[h, 0, sl] = -1024.0 * m * mult
                ks[h, 1, sl] = -8.0 * m * mult
                ks[h, 2, sl] = 1024.0 * m * mult * (bidx // 128)
                ks[h, 3, sl] = 8.0 * m * mult * (bidx % 128)
                ks[h, 4, sl] = 0.0 if R == 1 else -8.0 * m * (rank - rho)
                if R > 1:
                    d = 16.0 * m * (rank - rho)
                    dcorr2[h, rho] = d * (qq < kk) + min(0.0, d) * (qq == kk)
            dcorr[si, h] = -16.0 * m * mult * np.maximum(kk - qq, 0)
        kcols.append(ks)
    ql = np.concatenate(qcols, axis=1)
    qaug = np.stack([ql, -ql]).astype(np.float32)
    kaug = np.concatenate(kcols, axis=2).astype(np.float32)
    return qaug, kaug, dcorr, dcorr2


def _na_tables(na_rpb):
    c = np.arange(64)
    cs = np.clip(c - 8, 0, 48)
    colvalid = (c[:, None] >= cs[None, :]) & (c[:, None] < cs[None, :] + 16)
    dc = np.clip(c[:, None] - c[None, :], -15, 15) + 15
    L = na_rpb.shape[0]
    G = np.zeros((L, 16, 2, 64, NA_E, 64), np.float32)
    for krl in range(2):
        for e in range(NA_E):
            dr = 17 - e + krl
            if 0 <= dr <= 14:
                G[:, :, krl, :, e, :] = na_rpb[:, :, dr][:, :, dc]
    G = np.where(colvalid[None, None, None, :, None, :], G, np.float32(NEG_G)).astype(np.float32)
    G = G.reshape(L, 16, 128, NA_E * 64)
    M = np.zeros((3, 8, 2, 64, 8, 64), np.float32)
    for var in range(3):
        for t in range(8):
            for krl in range(2):
                kr = -4 + 2 * t + krl
                for qr in range(8):
                    if var == 0:
                        start = max(qr - 4, 0)
                    elif var == 1:
                        start = qr - 4
                    else:
                        start = min(qr - 4, 0)
                    ok = (start <= kr < start + 8)
                    if not ok:
                        M[var, t, krl, :, qr, :] = NEG_M
    Mq = np.ascontiguousarray(M[:, :, :, 0, :, :].transpose(0, 2, 1, 3, 4).reshape(3, 2, 8, 512))
    ki = np.zeros((2, 1024), np.float32)
    kk = np.arange(1024) % 128
    ki[0] = (kk < 64)
    ki[1] = (kk >= 64)
    return G, Mq, ki


def _na_tables_S(na_rpb, rank):
    L = na_rpb.shape[0]
    ap = np.arange(16)
    G = np.zeros((L, 16, 8, 16, 4, 72, 16), np.float32)
    for rho in range(4):
        c = 4 * ap + rank
        kc = 4 * ap + rho
        cs = np.clip(c - 8, 0, 48)
        colvalid = (kc[:, None] >= cs[None, :]) & (kc[:, None] < cs[None, :] + 16)
        dc = np.clip(kc[:, None] - c[None, :], -15, 15) + 15
        for Rk in range(8):
            for e in range(72):
                dr = Rk + 39 - e
                if 0 <= dr <= 14:
                    G[:, :, Rk, :, rho, e, :] = na_rpb[:, :, dr][:, :, dc]
        G[:, :, :, :, rho] = np.where(colvalid[None, None, None, :, None, :], G[:, :, :, :, rho], np.float32(NEG_G))
    G = G.reshape(L, 16, 128, 4, 72 * 16)
    M = np.zeros((3, 6, 8, 16, 32, 16), np.float32)
    for var in range(3):
        for j in range(6):
            dd = j - 1
            for Rk in range(8):
                kr = 8 * dd + Rk
                for Rq in range(32):
                    if var == 0:
                        start = max(Rq - 4, 0)
                    elif var == 1:
                        start = Rq - 4
                    else:
                        start = min(Rq - 4, 24)
                    if not (start <= kr < start + 8):
                        M[var, j, Rk, :, Rq, :] = NEG_M
    Mq = np.ascontiguousarray(M[:, :, :, 0, :, :].transpose(0, 2, 1, 3, 4).reshape(3, 8, 6, 512))
    ki = np.zeros((8, 768), np.float32)
    kk = (np.arange(768) % 128) // 16
    for j in range(8):
        ki[j] = (kk == j)
    return G, Mq, ki


_CACHE = {}


def _run(inputs, depth, segs_fn, n_cores=8, stop_after=None):
    key = (depth, tuple(segs_fn), stop_after)
    if key not in _CACHE:
        try:
            _CACHE[key] = build_program(depth, segs_fn, stop_after)
        except _Stop as e:
            _CACHE[key] = e.args[0]
    nc = _CACHE[key]
    f = lambda a: np.ascontiguousarray(np.asarray(a, dtype=np.float32))
    fl = lambda a: np.ascontiguousarray(np.asarray(a, dtype=np.float32)[:depth])
    has_S = any(r > 1 for _, _, r in segs_fn)
    rpb = fl(inputs["na_rpb"])
    na_g, na_mq, na_ki = _na_tables(rpb)
    idents = np.stack([np.eye(128, dtype=np.float32), 8.0 * np.eye(128, dtype=np.float32)])
    shared = {
        "w_in": fl(inputs["w_in"]), "da_lambda": fl(inputs["da_lambda"]).reshape(depth, 256),
        "da_norm": fl(inputs["da_norm"]), "gq_q_norm": fl(inputs["gq_q_norm"]), "gq_k_norm": fl(inputs["gq_k_norm"]),
        "w_br_a": fl(inputs["w_br_a"]), "w_br_b": fl(inputs["w_br_b"]), "w_br_c": fl(inputs["w_br_c"]), "w_o": fl(inputs["w_o"]),
        "g_pre_mix": fl(inputs["g_pre_mix"]), "g_post_mix": fl(inputs["g_post_mix"]), "g_pre_mlp": fl(inputs["g_pre_mlp"]),
        "g_post_mlp": fl(inputs["g_post_mlp"]), "w_up": fl(inputs["w_up"]), "w_down": fl(inputs["w_down"]),
        "w_ple": fl(inputs["w_ple"]), "w_ple_gate": fl(inputs["w_ple_gate"]), "g_ple": fl(inputs["g_ple"]),
        "na_g": na_g, "na_mq": na_mq, "na_ki": na_ki, "idents": idents,
    }
    xp, xs = f(inputs["x_prompt"]), f(inputs["x_sample"])
    pp, pS_ = f(inputs["p_prompt"]), f(inputs["p_sample"])
    cs_full = {s: _rope_tables(s) for s in (SEQ, DEC_SEQ)}
    per_rank = {}
    for rank in range(4):
        qaug, kaug, dcorr, dcorr2 = _aug_tables(segs_fn, rank)
        t = dict(qaug=qaug, kaug=kaug, dcorr=dcorr, dcorr2=dcorr2)
        if has_S:
            gs, mqs, kis = _na_tables_S(rpb, rank)
            t["na_gs"] = gs
            t["na_mqs"] = mqs
            t["na_kis"] = kis
        per_rank[rank] = t
    in_maps = []
    for c in range(n_cores):
        rank, grp = c % 4, c // 4
        parts_x, parts_p, parts_cs, parts_sn = [], [], [], []
        for name, s, R in segs_fn:
            if R == 1:
                parts_x.append(xp[c]); parts_p.append(pp[:depth, c])
                parts_cs.append(cs_full[SEQ][0]); parts_sn.append(cs_full[SEQ][1])
            else:
                parts_x.append(xs[grp, rank::4]); parts_p.append(pS_[:depth, grp, rank::4])
                parts_cs.append(cs_full[DEC_SEQ][0][rank::4]); parts_sn.append(cs_full[DEC_SEQ][1][rank::4])
        m = dict(shared)
        m.update(per_rank[rank])
        m["x_in"] = np.ascontiguousarray(np.concatenate(parts_x, axis=0))
        m["p_in"] = np.ascontiguousarray(np.concatenate(parts_p, axis=1))
        m["cs_tab"] = np.ascontiguousarray(np.concatenate(parts_cs, axis=0))
        m["sn_tab"] = np.ascontiguousarray(np.concatenate(parts_sn, axis=0))
        in_maps.append(m)
    res = run_bass_kernel_spmd(nc, in_maps, core_ids=list(range(n_cores)))
    _CACHE["last_results"] = res.results
    return [r["y_out"] for r in res.results]


def kernel(**inputs):
    segs_fn = (("P", SEQ, 1), ("S", DEC_SEQ // 4, 4))
    ys = _run(inputs, DEPTH, segs_fn)
    y_prompt = np.stack([ys[c][0:SEQ] for c in range(8)]).astype(np.float32)
    y_sample = np.zeros((2, DEC_SEQ, D), np.float32)
    for c in range(8):
        y_sample[c // 4, (c % 4)::4] = ys[c][SEQ:SEQ + DEC_SEQ // 4]
    return (y_prompt, y_sample)
```

```python
import math
from contextlib import ExitStack
import numpy as np
import concourse.bass as bass
import concourse.mybir as mybir
from concourse.bass_utils import run_bass_kernel_spmd

F32 = mybir.dt.float32
BF16 = mybir.dt.bfloat16
AF = mybir.ActivationFunctionType
ALU = mybir.AluOpType
AX = mybir.AxisListType

D = 1024
DEPTH = 4
SEQ = 2048
DEC_SEQ = 8192
GRID_W = 64
EPS = 1e-6
IN_W = 10752
D_FF = 4096
PLE = 256
NEG_M = -30000.0
NEG_G = -3000.0
NA_E = 22


class _Stop(Exception):
    pass


class Buf:
    __slots__ = ("name", "w", "r")

    def __init__(self, name):
        self.name = name
        self.w = None
        self.r = {}


class Sched:
    ENG = ("pe", "act", "dve", "pool", "sp")

    def __init__(self, nc, stack):
        self.nc = nc
        self.eng = dict(pe=nc.tensor, act=nc.scalar, dve=nc.vector, pool=nc.gpsimd, sp=nc.sync)
        self.sem = {e: stack.enter_context(nc.semaphore("s_" + e)) for e in self.ENG}
        self.cnt = {e: 0 for e in self.ENG}
        self.NDS = 12
        self.dsem = {q: [stack.enter_context(nc.semaphore(f"d_{q}{i}")) for i in range(self.NDS)]
                     for q in ("sp", "pool")}
        self.dcnt = {q: [0] * self.NDS for q in ("sp", "pool")}
        self.drr = {q: 0 for q in ("sp", "pool")}
        self.waited = {e: {} for e in self.ENG}
        self.ops = {e: [] for e in self.ENG}
        self.cc_sem = stack.enter_context(nc.semaphore("s_cc"))
        self.cc_cnt = 0

    def _need(self, e, tick, waits):
        if tick is None:
            return
        key, val = tick
        if e == "pe" and key == "pe":
            return
        if self.waited[e].get(key, 0) >= val:
            return
        self.waited[e][key] = val
        waits.append((key, val))

    def _semof(self, key):
        if key == "cc":
            return self.cc_sem
        if key in self.sem:
            return self.sem[key]
        q, i = key
        return self.dsem[q][i]

    def _deps(self, e, reads, writes):
        waits = []
        for b in reads:
            self._need(e, b.w, waits)
        for b in writes:
            self._need(e, b.w, waits)
            for k, v in b.r.items():
                self._need(e, (k, v), waits)
        return waits

    def op(self, e, fn, reads=(), writes=(), signal=True):
        waits = self._deps(e, reads, writes) if (reads or writes) else []
        inc = None
        if signal:
            self.cnt[e] += 1
            tick = (e, self.cnt[e])
            inc = (e, 1)
            for b in writes:
                b.w = tick
                b.r = {}
            for b in reads:
                if b.r.get(e, 0) < tick[1]:
                    b.r[e] = tick[1]
        self.ops[e].append((fn, waits, inc))

    def dma(self, q, out, in_, reads=(), writes=()):
        waits = self._deps(q, reads, writes)
        i = self.drr[q]
        self.drr[q] = (i + 1) % self.NDS
        self.dcnt[q][i] += 16
        key = (q, i)
        tick = (key, self.dcnt[q][i])
        for b in writes:
            b.w = tick
            b.r = {}
        for b in reads:
            b.r[key] = tick[1]
        self.ops[q].append((lambda eng, o=out, s=in_: eng.dma_start(out=o, in_=s), waits, (key, 16)))

    def collective(self, fn, writes):
        self.cc_cnt += 1
        tick = ("cc", self.cc_cnt)
        for b in writes:
            b.w = tick
            b.r = {}
        self.ops["pool"].append((fn, [], ("cc", None)))

    def barrier(self):
        for e in self.ENG:
            waits = []
            for e2 in self.ENG:
                if e2 != e and self.cnt[e2] > 0:
                    self._need(e, (e2, self.cnt[e2]), waits)
            for q in ("sp", "pool"):
                for i in range(self.NDS):
                    if self.dcnt[q][i] > 0:
                        self._need(e, ((q, i), self.dcnt[q][i]), waits)
            if waits:
                self.ops[e].append((None, waits, None))

    def flush(self):
        nc = self.nc
        with nc.Block() as block:
            for e, deco in (("pe", block.tensor), ("act", block.scalar), ("dve", block.vector),
                            ("pool", block.gpsimd), ("sp", block.sync)):
                lst = self.ops[e]

                def body(eng, lst=lst):
                    for fn, waits, inc in lst:
                        for key, val in waits:
                            eng.wait_ge(self._semof(key), val)
                        if fn is not None:
                            ins = fn(eng)
                            if inc is not None:
                                if inc[1] is None:
                                    ins.then_inc(self._semof(inc[0]))
                                else:
                                    ins.then_inc(self._semof(inc[0]), inc[1])
                deco(body)
        self.ops = {e: [] for e in self.ENG}


def emit_pipelined(tasks, LOOK=2, PRE=24, DEFER=4):
    n = len(tasks)
    state = {"pre": 0}

    def do_pre(upto):
        while state["pre"] < min(upto, n):
            p = tasks[state["pre"]][0]
            if p:
                p()
            state["pre"] += 1

    deferred = []
    do_pre(PRE)
    for i in range(min(LOOK, n)):
        tasks[i][1]()
    for i in range(n):
        do_pre(i + PRE + 1)
        tasks[i][2]()
        if i + LOOK < n:
            tasks[i + LOOK][1]()
        tasks[i][3]()
        if tasks[i][4]:
            d = tasks[i][4]()
            if d:
                deferred.append((i + DEFER, d))
        while deferred and deferred[0][0] <= i:
            deferred.pop(0)[1]()
    for _, d in deferred:
        d()


class Rot:
    def __init__(self, tiles):
        self.tiles = tiles
        self.bufs = [Buf("rot") for _ in tiles]
        self.i = 0

    def next(self):
        i = self.i
        self.i = (i + 1) % len(self.tiles)
        return self.tiles[i], self.bufs[i]


def build_program(depth, segs, stop_after=None):
    nc = bass.Bass("TRN2", target_bir_lowering=False)
    NT = sum(s for _, s, _r in segs)
    NKV = sum(s * r for _, s, r in segs)
    seg_off, kv_off = [], []
    o = ko = 0
    for _, s, r in segs:
        seg_off.append(o)
        kv_off.append(ko)
        o += s
        ko += s * r

    def din(name, shape, dt=F32):
        return nc.dram_tensor(name, list(shape), dt, kind="ExternalInput").ap()

    def dscr(name, shape, dt=BF16):
        import os as _os2
        if name in _os2.environ.get("KDBG_DUMP", "").split(","):
            return nc.dram_tensor(name, list(shape), dt, kind="ExternalOutput").ap()
        return nc.dram_tensor(name, list(shape), dt).ap()

    x_in = din("x_in", [NT, D])
    p_in = din("p_in", [depth, NT, PLE])
    w_in = din("w_in", [depth, D, IN_W])
    da_lambda = din("da_lambda", [depth, 256])
    da_norm = din("da_norm", [depth, 128])
    gq_q_norm = din("gq_q_norm", [depth, 128])
    gq_k_norm = din("gq_k_norm", [depth, 128])
    w_br = [din("w_br_a", [depth, D, D]), din("w_br_b", [depth, D, D]), din("w_br_c", [depth, D, D])]
    w_o = din("w_o", [depth, D, D])
    g_pre_mix = din("g_pre_mix", [depth, D])
    g_post_mix = din("g_post_mix", [depth, D])
    g_pre_mlp = din("g_pre_mlp", [depth, D])
    g_post_mlp = din("g_post_mlp", [depth, D])
    w_up = din("w_up", [depth, D, D_FF])
    w_down = din("w_down", [depth, D_FF, D])
    w_ple = din("w_ple", [depth, PLE, D])
    w_ple_gate = din("w_ple_gate", [depth, D, D])
    g_ple = din("g_ple", [depth, D])
    cs_tab = din("cs_tab", [NT, 64])
    sn_tab = din("sn_tab", [NT, 64])
    qaug = din("qaug", [2, 5, NT])
    kaug = din("kaug", [8, 5, NKV])
    dcorr = din("dcorr", [len(segs), 8, 128, 128])
    dcorr2 = din("dcorr2", [8, 4, 128, 128])
    na_g = din("na_g", [depth, 16, 128, NA_E * 64])
    na_m = din("na_m", [3, 8, 128, 512])
    has_S = any(r > 1 for _, _, r in segs)
    if has_S:
        na_gs = din("na_gs", [depth, 16, 128, 4, 1152])
        na_ms = din("na_ms", [3, 6, 128, 512])
    idents = din("idents", [2, 128, 128])

    y_out = nc.dram_tensor("y_out", [NT, D], F32, kind="ExternalOutput").ap()

    h_d = dscr("h_d", [NT, D], F32)
    qaT_d = dscr("qaT_d", [16, 64, NT])
    qnT_d = dscr("qnT_d", [16, 64, NT])
    qcT_d = dscr("qcT_d", [8, 128, NT])
    KVR = 4608

    def kvviews(a2):
        return dict(
            kaT=a2[0:1024, :].rearrange("(h d) t -> h d t", d=64),
            knT=a2[1024:2048, :].rearrange("(h d) t -> h d t", d=64),
            kcT=a2[2048:2304, :].rearrange("(h d) t -> h d t", d=128),
            va=a2[2304:3328, :].rearrange("r (two c) -> (r two) c", two=2),
            vn=a2[3328:4352, :].rearrange("r (two c) -> (r two) c", two=2),
            vc=a2[4352:4608, :].rearrange("r (e c) -> (r e) c", e=8),
        )

    class LocalKV:
        def __init__(self, slab):
            self.v = kvviews(slab)

        def kaT(self, hm):
            return self.v["kaT"][hm]

        def knT(self, h, a, b):
            return self.v["knT"][h, :, a:b]

        def kcT(self, n):
            return self.v["kcT"][n]

        def vpieces(self, name, ta, tb, ca, cb):
            return [(0, (tb - ta) // 128, self.v[name][ta:tb, ca:cb].rearrange("(k p) e -> p k e", p=128))]

    class GatheredKV:
        def __init__(self, allbuf, rho, nr):
            self.a, self.rho, self.nr = allbuf, rho, nr

        def blk(self, i):
            r0 = (i * self.nr + self.rho) * 128
            return self.a[r0:r0 + 128, :]

        def kaT(self, hm):
            return self.blk(hm // 2)[(hm % 2) * 64:(hm % 2) * 64 + 64, :]

        def knT(self, h, a, b):
            return self.blk(8 + h // 2)[(h % 2) * 64:(h % 2) * 64 + 64, a:b]

        def kcT(self, n):
            return self.blk(16 + n)

        def vpieces(self, name, ta, tb, ca, cb):
            base, tpb = {"va": (18, 256), "vn": (26, 256), "vc": (34, 1024)}[name]
            out = []
            for j in range(ta // tpb, (tb + tpb - 1) // tpb):
                a = max(ta, j * tpb)
                b = min(tb, (j + 1) * tpb)
                if name == "vc":
                    v = self.blk(base + j).rearrange("r (e c) -> (r e) c", e=8)
                else:
                    v = self.blk(base + j).rearrange("r (two c) -> (r two) c", two=2)
                out.append(((a - ta) // 128, (b - a) // 128, v[a - j * tpb:b - j * tpb, ca:cb].rearrange("(k p) e -> p k e", p=128)))
            return out

    seg_dst, seg_src = [], []
    b_gath = Buf("gathered")
    kv_src = kv_all = None
    NBLK = KVR // 128
    for si, (sname, s, r) in enumerate(segs):
        if r == 1:
            loc = dscr(f"kv_loc{si}", [KVR, 2048])
            seg_dst.append(kvviews(loc))
            seg_src.append([LocalKV(loc)])
        else:
            kv_src = dscr("kv_src", [KVR, 2048])
            kv_all = dscr("kv_all", [r * KVR, 2048])
            seg_dst.append(kvviews(kv_src))
            seg_src.append([GatheredKV(kv_all, q, r) for q in range(r)])

    gT_d = dscr("gT_d", [24, 128, NT])
    oT_d = [dscr("oaT_d", [8, 128, NT]), dscr("obT_d", [8, 128, NT]), dscr("ocT_d", [8, 128, NT])]
    u2T_d = dscr("u2T_d", [8, 128, NT])
    aT_d = dscr("aT_d", [32, 128, NT])

    top = ExitStack()
    with top:
        S = Sched(nc, top)
        E = S.eng

        uid = [0]

        def sb(st, name, shape, dt):
            uid[0] += 1
            return st.enter_context(nc.sbuf_tensor(f"{name}_{uid[0]}", list(shape), dt))

        def ps(st, name, shape, dt=F32):
            uid[0] += 1
            return st.enter_context(nc.psum_tensor(f"{name}_{uid[0]}", list(shape), dt))

        ident = sb(top, "ident", [128, 128], BF16)
        ident8 = sb(top, "ident8", [128, 128], BF16)
        gvec = sb(top, "gvec", [128, 5, D], F32)
        gsm = sb(top, "gsm", [128, 3, 128], F32)
        lamt = sb(top, "lamt", [128, 256], F32)
        lam = sb(top, "lam", [128, 8], F32)
        b_const = Buf("const")
        b_gv = Buf("gvec")
        b_lam = Buf("lam")
        epsc = sb(top, "epsc", [128, 1], F32)
        S.op("pool", lambda e: e.memset(epsc[:], EPS), writes=[b_const])
        S.dma("pool", ident[:], idents[0], writes=[b_const])
        S.dma("pool", ident8[:], idents[1], writes=[b_const])

        def load_layer_consts(l):
            for i, g in enumerate((g_pre_mix, g_post_mix, g_pre_mlp, g_post_mlp, g_ple)):
                S.dma("sp", gvec[:, i, :], g[l:l + 1, :].partition_broadcast(128), writes=[b_gv])
            for i, g in enumerate((da_norm, gq_q_norm, gq_k_norm)):
                S.dma("sp", gsm[:, i, :], g[l:l + 1, :].partition_broadcast(128), writes=[b_gv])
            S.dma("sp", lamt[:], da_lambda[l:l + 1, :].partition_broadcast(128), writes=[b_lam])
            li = 0.8 - 0.6 * math.exp(-0.3 * l)
            S.op("dve", lambda e: e.tensor_tensor(out=lamt[:, 0:64], in0=lamt[:, 0:64], in1=lamt[:, 64:128], op=ALU.mult),
                 reads=[b_lam], writes=[b_lam])
            S.op("dve", lambda e: e.tensor_tensor(out=lamt[:, 128:192], in0=lamt[:, 128:192], in1=lamt[:, 192:256], op=ALU.mult),
                 reads=[b_lam], writes=[b_lam])
            S.op("dve", lambda e: e.tensor_reduce(out=lam[:, 0:1], in_=lamt[:, 0:64], axis=AX.X, op=ALU.add),
                 reads=[b_lam], writes=[b_lam])
            S.op("dve", lambda e: e.tensor_reduce(out=lam[:, 1:2], in_=lamt[:, 128:192], axis=AX.X, op=ALU.add),
                 reads=[b_lam], writes=[b_lam])
            S.op("act", lambda e: e.activation(out=lam[:, 3:5], in_=lam[:, 0:2], func=AF.Exp), reads=[b_lam], writes=[b_lam])
            S.op("dve", lambda e: e.tensor_tensor(out=lam[:, 2:3], in0=lam[:, 3:4], in1=lam[:, 4:5], op=ALU.subtract),
                 reads=[b_lam], writes=[b_lam])
            S.op("dve", lambda e: e.tensor_scalar(out=lam[:, 5:6], in0=lam[:, 2:3], scalar1=li, scalar2=-1.0, op0=ALU.add, op1=ALU.mult),
                 reads=[b_lam], writes=[b_lam])
            return li

        def rstd_from_ss(ss_ap, out_ap, n, bufs, post_mul=None):
            S.op("act", lambda e: e.activation(out=out_ap, in_=ss_ap, func=AF.Ln, scale=1.0 / n, bias=epsc[:, 0:1]),
                 reads=list(bufs) + [b_const], writes=bufs)
            S.op("act", lambda e: e.activation(out=out_ap, in_=out_ap, func=AF.Exp, scale=-0.5), reads=bufs, writes=bufs)
            if post_mul is not None:
                S.op("dve", lambda e: e.tensor_scalar(out=out_ap, in0=out_ap, scalar1=float(post_mul), scalar2=None, op0=ALU.mult),
                     reads=bufs, writes=bufs)

        def pe_seq(fns, reads, writes):
            waits = S._deps("pe", reads, writes)
            n = len(fns)
            S.cnt["pe"] += 1
            tick = ("pe", S.cnt["pe"])
            for b in writes:
                b.w = tick
                b.r = {}
            for b in reads:
                if b.r.get("pe", 0) < tick[1]:
                    b.r["pe"] = tick[1]
            for i, fn in enumerate(fns):
                S.ops["pe"].append((fn, waits if i == 0 else [], ("pe", 1) if i == n - 1 else None))

        def mm_group(mms, reads, writes):
            n = len(mms)
            pe_seq([(lambda e, o=o, l=l, r=r, a=(i == 0), z=(i == n - 1): e.matmul(o, l, r, start=a, stop=z))
                    for i, (o, l, r) in enumerate(mms)], reads, writes)

        def maybe_stop(tag):
            if stop_after == tag:
                S.dma("sp", y_out[:, :], x_in[:, :])
                S.barrier()
                S.flush()
                raise _Stop(nc)

        for l in range(depth):
            li = load_layer_consts(l)
            src_h = x_in if l == 0 else h_d
            dst_final = y_out if l == depth - 1 else h_d

            with ExitStack() as st:
                uT = sb(st, "uT", [128, 8, 2048], BF16)
                hb = [sb(st, f"hb{i}", [128, D], F32) for i in range(2)]
                junk = sb(st, "junkA", [128, D], F32)
                ub = [sb(st, f"ub{i}", [128, D], BF16) for i in range(2)]
                ssA = sb(st, "ssA", [128, 4], F32)
                wb = [sb(st, f"wb{i}", [128, 8, 512], BF16) for i in range(3)]
                stg = [sb(st, f"stg{i}", [128, 512], BF16) for i in range(4)]
                xs_f = [sb(st, f"xsf{i}", [128, 4, 128], F32) for i in range(2)]
                sq_f = sb(st, "sqf", [128, 4, 128], F32)
                xr_b = [sb(st, f"xrb{i}", [128, 4, 128], BF16) for i in range(2)]
                t_a = sb(st, "t_a", [128, 4, 64], F32)
                t_b = sb(st, "t_b", [128, 4, 64], F32)
                ss4 = sb(st, "ss4", [128, 8], F32)
                cst = sb(st, "cst", [128, 16, 64], F32)
                snt = sb(st, "snt", [128, 16, 64], F32)
                stT = [sb(st, f"stT{i}", [128, 4, 128], BF16) for i in range(2)]
                pA = [ps(st, f"pA{i}", [128, 512]) for i in range(4)]
                pT = [ps(st, f"pTA{i}", [128, 8, 128], BF16) for i in range(2)]
                r_hb = Rot(hb); r_ub = Rot(ub); r_wb = Rot(wb); r_stg = Rot(stg)
                r_pA = Rot(pA); r_pT = Rot(pT); r_xs = Rot(xs_f); r_xr = Rot(xr_b); r_stT = Rot(stT)
                b_uT = Buf("uT"); b_junk = Buf("junk"); b_ss = Buf("ssA"); b_tmp = Buf("tmpA"); b_cs = Buf("cs")
                evac_i = [0]

                def evac_engine():
                    evac_i[0] += 1
                    return "act" if evac_i[0] % 2 else "dve"

                def copy_op(eng, out, in_, reads, writes):
                    if eng == "act":
                        S.op("act", lambda e: e.copy(out=out, in_=in_), reads=reads, writes=writes)
                    else:
                        S.op(eng, lambda e: e.tensor_copy(out=out, in_=in_), reads=reads, writes=writes)

                for sc in range(len(segs)):
                    t0 = seg_off[sc]
                    dv = seg_dst[sc]
                    S.dma("sp", cst[:], cs_tab[t0:t0 + 2048, :].rearrange("(t p) f -> p t f", p=128), writes=[b_cs])
                    S.dma("sp", snt[:], sn_tab[t0:t0 + 2048, :].rearrange("(t p) f -> p t f", p=128), writes=[b_cs])
                    for tt in range(16):
                        ht, hbuf = r_hb.next()
                        S.dma("sp", ht[:], src_h[t0 + tt * 128:t0 + (tt + 1) * 128, :], writes=[hbuf])
                        S.op("act", lambda e, ht=ht: e.activation(out=junk[:], in_=ht[:], func=AF.Square, accum_out=ssA[:, 0:1]),
                             reads=[hbuf], writes=[b_junk, b_ss])
                        rstd_from_ss(ssA[:, 0:1], ssA[:, 1:2], D, [b_ss])
                        ut, ubuf = r_ub.next()
                        S.op("dve", lambda e, ht=ht, ut=ut: e.scalar_tensor_tensor(out=ut[:], in0=ht[:], scalar=ssA[:, 1:2], in1=gvec[:, 0, :],
                                                                                 op0=ALU.mult, op1=ALU.mult),
                             reads=[hbuf, b_ss, b_gv], writes=[ubuf])
                        pt, pbuf = r_pT.next()
                        pe_seq([(lambda e, pt=pt, ut=ut, c=c: e.transpose(pt[:, c, :], ut[:, c * 128:(c + 1) * 128], ident[:])) for c in range(8)],
                               [ubuf, b_const], [pbuf])
                        copy_op(evac_engine(), uT[:, :, tt * 128:(tt + 1) * 128], pt[:], [pbuf], [b_uT])

                    import os as _os
                    for blk in [int(v) for v in _os.environ.get('KDBG_BLKS', ','.join(map(str, range(21)))).split(',') if v != '']:
                        wt, wbuf = r_wb.next()
                        S.dma("pool", wt[:], w_in[l].rearrange("(c p) n -> p c n", p=128)[:, :, blk * 512:(blk + 1) * 512],
                              writes=[wbuf])
                        if blk in (0, 1, 2, 3, 6, 7, 8, 9):
                            dstT = {0: qaT_d, 1: qaT_d, 2: dv["kaT"], 3: dv["kaT"], 6: qnT_d, 7: qnT_d, 8: dv["knT"], 9: dv["knT"]}[blk]
                            tb0 = t0 if blk in (0, 1, 6, 7) else 0
                            for sub in range(8):
                                hm = (blk % 2) * 8 + sub
                                for tq in range(4):
                                    pa, pab = r_pA.next()
                                    mm_group([(pa[0:64, :], wt[:, c, sub * 64:(sub + 1) * 64], uT[:, c, tq * 512:(tq + 1) * 512])
                                              for c in range(8)], [wbuf, b_uT], [pab])
                                    sg, sgb = r_stg.next()
                                    copy_op(evac_engine(), sg[0:64, :], pa[0:64, :], [pab], [sgb])
                                    S.dma("sp", dstT[hm, :, tb0 + tq * 512:tb0 + (tq + 1) * 512], sg[0:64, :], reads=[sgb])
                        elif blk >= 15:
                            for sub in range(4):
                                ch = (blk - 15) * 4 + sub
                                for tq in range(4):
                                    pa, pab = r_pA.next()
                                    mm_group([(pa[:, :], wt[:, c, sub * 128:(sub + 1) * 128], uT[:, c, tq * 512:(tq + 1) * 512])
                                              for c in range(8)], [wbuf, b_uT], [pab])
                                    sg, sgb = r_stg.next()
                                    S.op("act", lambda e, sg=sg, pa=pa: e.activation(out=sg[:], in_=pa[:], func=AF.Sigmoid),
                                         reads=[pab], writes=[sgb])
                                    S.dma("sp", gT_d[ch, :, t0 + tq * 512:t0 + (tq + 1) * 512], sg[:], reads=[sgb])
                        elif blk in (4, 5, 10, 11):
                            dstV = dv["va"] if blk < 6 else dv["vn"]
                            c0 = (blk % 2) * 512
                            for tt in range(16):
                                pa, pab = r_pA.next()
                                mm_group([(pa[:, :], uT[:, c, tt * 128:(tt + 1) * 128], wt[:, c, :]) for c in range(8)],
                                         [wbuf, b_uT], [pab])
                                sg, sgb = r_stg.next()
                                copy_op(evac_engine(), sg[:], pa[:], [pab], [sgb])
                                S.dma("sp", dstV[tt * 128:(tt + 1) * 128, c0:c0 + 512], sg[:], reads=[sgb])
                        else:
                            for tt in range(16):
                                pa, pab = r_pA.next()
                                mm_group([(pa[:, :], uT[:, c, tt * 128:(tt + 1) * 128], wt[:, c, :]) for c in range(8)],
                                         [wbuf, b_uT], [pab])
                                nh = 4 if blk < 14 else 2
                                gi = 1 if blk < 14 else 2
                                xs_, xsb = r_xs.next()
                                S.op("act", lambda e, xs_=xs_, pa=pa, nh=nh: e.copy(out=xs_[:, 0:nh, :], in_=pa[:, 0:nh * 128].rearrange("p (h d) -> p h d", h=nh)),
                                     reads=[pab], writes=[xsb])
                                if blk == 14 and not _os.environ.get('KDBG_SKIPV'):
                                    sg, sgb = r_stg.next()
                                    S.op("act", lambda e, sg=sg, pa=pa: e.copy(out=sg[:, 0:256], in_=pa[:, 256:512]), reads=[pab], writes=[sgb])
                                    S.dma("sp", dv["vc"][tt * 128:(tt + 1) * 128, :], sg[:, 0:256], reads=[sgb])
                                S.op("dve", lambda e, xs_=xs_, nh=nh: e.tensor_tensor(out=sq_f[:, 0:nh, :], in0=xs_[:, 0:nh, :], in1=xs_[:, 0:nh, :], op=ALU.mult),
                                     reads=[xsb], writes=[b_tmp])
                                S.op("dve", lambda e, nh=nh: e.tensor_reduce(out=ss4[:, 0:nh], in_=sq_f[:, 0:nh, :], axis=AX.X, op=ALU.add),
                                     reads=[b_tmp], writes=[b_tmp])
                                rstd_from_ss(ss4[:, 0:nh], ss4[:, 4:4 + nh], 128, [b_tmp])
                                S.op("dve", lambda e, xs_=xs_, nh=nh: e.tensor_tensor(out=xs_[:, 0:nh, :], in0=xs_[:, 0:nh, :],
                                                                                    in1=ss4[:, 4:4 + nh].unsqueeze(2).to_broadcast([128, nh, 128]), op=ALU.mult),
                                     reads=[xsb, b_tmp], writes=[xsb])
                                S.op("dve", lambda e, xs_=xs_, nh=nh, gi=gi: e.tensor_tensor(out=xs_[:, 0:nh, :], in0=xs_[:, 0:nh, :],
                                                                                           in1=gsm[:, gi:gi + 1, :].to_broadcast([128, nh, 128]), op=ALU.mult),
                                     reads=[xsb, b_gv], writes=[xsb])
                                xr, xrb = r_xr.next()
                                x0 = xs_[:, 0:nh, :].rearrange("p h (i two) -> p h i two", two=2)[:, :, :, 0]
                                x1 = xs_[:, 0:nh, :].rearrange("p h (i two) -> p h i two", two=2)[:, :, :, 1]
                                o0 = xr[:, 0:nh, :].rearrange("p h (i two) -> p h i two", two=2)[:, :, :, 0]
                                o1 = xr[:, 0:nh, :].rearrange("p h (i two) -> p h i two", two=2)[:, :, :, 1]
                                cb = cst[:, tt:tt + 1, :].to_broadcast([128, nh, 64])
                                sbb = snt[:, tt:tt + 1, :].to_broadcast([128, nh, 64])
                                ta = t_a[:, 0:nh, :]
                                tb = t_b[:, 0:nh, :]
                                S.op("dve", lambda e, ta=ta, x0=x0, cb=cb: e.tensor_tensor(out=ta, in0=x0, in1=cb, op=ALU.mult), reads=[xsb, b_cs], writes=[b_tmp])
                                S.op("dve", lambda e, tb=tb, x1=x1, sbb=sbb: e.tensor_tensor(out=tb, in0=x1, in1=sbb, op=ALU.mult), reads=[xsb, b_cs, b_tmp], writes=[b_tmp])
                                S.op("dve", lambda e, ta=ta, tb=tb, o0=o0: e.tensor_tensor(out=o0, in0=ta, in1=tb, op=ALU.subtract), reads=[b_tmp], writes=[xrb])
                                S.op("dve", lambda e, ta=ta, x0=x0, sbb=sbb: e.tensor_tensor(out=ta, in0=x0, in1=sbb, op=ALU.mult), reads=[xsb, b_cs, xrb], writes=[b_tmp])
                                S.op("dve", lambda e, tb=tb, x1=x1, cb=cb: e.tensor_tensor(out=tb, in0=x1, in1=cb, op=ALU.mult), reads=[xsb, b_cs, b_tmp], writes=[b_tmp])
                                S.op("dve", lambda e, ta=ta, tb=tb, o1=o1: e.tensor_tensor(out=o1, in0=ta, in1=tb, op=ALU.add), reads=[b_tmp, xrb], writes=[xrb])
                                pt, pbuf = r_pT.next()
                                pe_seq([(lambda e, pt=pt, xr=xr, hh=hh: e.transpose(pt[:, hh, :], xr[:, hh, :], ident[:])) for hh in range(nh)],
                                       [xrb, b_const], [pbuf])
                                sT, sTb = r_stT.next()
                                copy_op(evac_engine(), sT[:, 0:nh, :], pt[:, 0:nh, :], [pbuf], [sTb])
                                dT = qcT_d[(blk - 12) * 4:(blk - 12) * 4 + 4] if blk < 14 else dv["kcT"]
                                tb0 = t0 if blk < 14 else 0
                                S.dma("sp", dT[:, :, tb0 + tt * 128:tb0 + (tt + 1) * 128].rearrange("h d t -> d h t"), sT[:, 0:nh, :], reads=[sTb])
                S.barrier()
                S.flush()

            def attention(kind):
                with ExitStack() as st:
                    Smax = max(s * r for _, s, r in segs)
                    nkmax = Smax // 128
                    KR = 69 if kind == "da" else 128
                    nmap = 2 if kind == "da" else 1
                    KT = [[sb(st, f"KT{i}{m}", [KR, Smax], BF16) for m in range(nmap)] for i in range(2)]
                    VT = [sb(st, f"VT{i}", [128, nkmax, 129], BF16) for i in range(2)]
                    b_KV = [Buf("kv0"), Buf("kv1")]
                    b_dc = Buf("dc")
                    if kind == "da":
                        QT = [sb(st, f"QT{i}", [KR, 2, 2, 512], BF16) for i in range(2)]
                        dct = sb(st, "dct", [128, len(segs), 8, 128], BF16)
                        for si_ in range(len(segs)):
                            S.dma("pool", dct[:, si_], dcorr[si_].rearrange("h k q -> k h q"), writes=[b_dc])
                        if has_S:
                            dct2 = sb(st, "dct2", [128, 8, 4, 128], BF16)
                            S.dma("pool", dct2[:], dcorr2.rearrange("h r k q -> k h r q"), writes=[b_dc])
                    else:
                        QT = [sb(st, f"QT{i}", [KR, 512], BF16) for i in range(3)]
                    r_QT = Rot(QT)
                    PT = [sb(st, f"PT{i}", [128, 512], BF16) for i in range(3)]
                    r_PT = Rot(PT)
                    Of = [sb(st, f"Of{i}", [128, 4, 129], F32) for i in range(4)]
                    r_Of = Rot(Of)
                    rr = sb(st, "rr", [128, 16], F32)
                    oc = sb(st, "oc", [128, 4, 128], F32)
                    oc2 = sb(st, "oc2", [128, 4, 128], F32)
                    on = [sb(st, f"on{i}", [128, 4, 128], BF16) for i in range(2)]
                    r_on = Rot(on)
                    sto = [sb(st, f"sto{i}", [128, 512], BF16) for i in range(2)]
                    r_sto = Rot(sto)
                    b_post = Buf("post")
                    pS = [ps(st, f"pS{i}", [128, 512]) for i in range(3)]
                    r_pS = Rot(pS)
                    pO = [ps(st, f"pO{i}", [128, 512])[:, 0:258].rearrange("p (s e) -> p s e", s=2) for i in range(4)]
                    pObuf = [Buf("pOa"), Buf("pOb")]
                    pTt = ps(st, "pTt", [128, 8, 128], BF16)[:, 0:4, :]
                    b_pTt = Buf("pTt")
                    for i in range(2):
                        S.op("pool", lambda e, i=i: e.memset(VT[i][:], 1.0), writes=[b_KV[i]])
                    scale = 0.125 if kind == "da" else 128.0 ** -0.5
                    tasks = []
                    it = [0]
                    pv_bank = [0]

                    def finish(grp, qh, q0):
                        ont, onb = r_on.next()
                        if kind == "da":
                            (o1, b1), (o2, b2) = grp
                            S.op("dve", lambda e: e.reciprocal(out=rr[:, 0:4], in_=o1[:, :, 128]), reads=[b1], writes=[b_post])
                            S.op("dve", lambda e: e.reciprocal(out=rr[:, 4:8], in_=o2[:, :, 128]), reads=[b2, b_post], writes=[b_post])
                            S.op("dve", lambda e: e.tensor_scalar(out=rr[:, 4:8], in0=rr[:, 4:8], scalar1=lam[:, 5:6], scalar2=None, op0=ALU.mult),
                                 reads=[b_post, b_lam], writes=[b_post])
                            S.op("dve", lambda e: e.tensor_tensor(out=oc[:], in0=o1[:, :, 0:128], in1=rr[:, 0:4].unsqueeze(2).to_broadcast([128, 4, 128]), op=ALU.mult),
                                 reads=[b1, b_post], writes=[b_post])
                            S.op("dve", lambda e: e.tensor_tensor(out=oc2[:], in0=o2[:, :, 0:128], in1=rr[:, 4:8].unsqueeze(2).to_broadcast([128, 4, 128]), op=ALU.mult),
                                 reads=[b2, b_post], writes=[b_post])
                            S.op("dve", lambda e: e.tensor_tensor(out=oc[:], in0=oc[:], in1=oc2[:], op=ALU.add), reads=[b_post], writes=[b_post])
                            S.op("dve", lambda e: e.tensor_tensor(out=oc2[:], in0=oc[:], in1=oc[:], op=ALU.mult), reads=[b_post], writes=[b_post])
                            S.op("dve", lambda e: e.tensor_reduce(out=rr[:, 8:12], in_=oc2[:], axis=AX.X, op=ALU.add), reads=[b_post], writes=[b_post])
                            rstd_from_ss(rr[:, 8:12], rr[:, 12:16], 128, [b_post], post_mul=(1.0 - li))
                            S.op("dve", lambda e: e.tensor_tensor(out=oc[:], in0=oc[:], in1=rr[:, 12:16].unsqueeze(2).to_broadcast([128, 4, 128]), op=ALU.mult),
                                 reads=[b_post], writes=[b_post])
                            S.op("dve", lambda e: e.tensor_tensor(out=ont[:], in0=oc[:], in1=gsm[:, 0:1, :].to_broadcast([128, 4, 128]), op=ALU.mult),
                                 reads=[b_post, b_gv], writes=[onb])
                        else:
                            (o1, b1), = grp
                            S.op("dve", lambda e: e.reciprocal(out=rr[:, 0:4], in_=o1[:, :, 128]), reads=[b1], writes=[b_post])
                            S.op("dve", lambda e: e.tensor_tensor(out=ont[:], in0=o1[:, :, 0:128], in1=rr[:, 0:4].unsqueeze(2).to_broadcast([128, 4, 128]), op=ALU.mult),
                                 reads=[b1, b_post], writes=[onb])
                        def part2():
                            pe_seq([(lambda e, s=s: e.transpose(pTt[:, s, :], ont[:, s, :], ident[:])) for s in range(4)], [onb, b_const], [b_pTt])
                            so, sob = r_sto.next()
                            S.op("act", lambda e: e.copy(out=so[:], in_=pTt[:].rearrange("p s q -> p (s q)")), reads=[b_pTt], writes=[sob])
                            dst = oT_d[0] if kind == "da" else oT_d[2]
                            S.dma("sp", dst[qh, :, q0:q0 + 512], so[:], reads=[sob])
                        return part2

                    for si, (sname, NQ, RK) in enumerate(segs):
                        tok0 = seg_off[si]
                        ko = kv_off[si]
                        SL = NQ * RK
                        nkc = SL // 128
                        nqc = NQ // 512
                        srcs = seg_src[si]
                        gdep = [b_gath] if RK > 1 else []
                        nkvh = 8 if kind == "da" else 2
                        for kvh in range(nkvh):
                            slot = it[0] % 2
                            it[0] += 1
                            kvb = b_KV[slot]

                            def load_kv(kvh=kvh, slot=slot, kvb=kvb, SL=SL, srcs=srcs, gdep=gdep, ko=ko, RK=RK):
                                for rho in range(RK):
                                    c0, c1 = rho * 2048, (rho + 1) * 2048
                                    if kind == "da":
                                        for m in range(2):
                                            S.dma("sp", KT[slot][m][0:64, c0:c1], srcs[rho].kaT(kvh * 2 + m), reads=gdep, writes=[kvb])
                                        vname = "va"
                                    else:
                                        S.dma("sp", KT[slot][0][:, c0:c1], srcs[rho].kcT(kvh), reads=gdep, writes=[kvb])
                                        vname = "vc"
                                    for (ko_, nk_, vap) in srcs[rho].vpieces(vname, 0, 2048, kvh * 128, (kvh + 1) * 128):
                                        S.dma("sp", VT[slot][:, rho * 16 + ko_:rho * 16 + ko_ + nk_, 0:128], vap, reads=gdep, writes=[kvb])
                                if kind == "da":
                                    for m in range(2):
                                        S.dma("pool", KT[slot][m][64:69, 0:SL], kaug[kvh, :, ko:ko + SL], writes=[kvb])

                            qheads = [kvh] if kind == "da" else [kvh * 4 + g for g in range(4)]
                            first = [True]
                            for qh in qheads:
                                for qc in range(nqc):
                                    q0 = tok0 + qc * 512
                                    qt, qb_ = r_QT.next()

                                    def load_q(qt=qt, qb_=qb_, qh=qh, q0=q0):
                                        if kind == "da":
                                            for m in range(2):
                                                for lr in range(2):
                                                    S.dma("sp", qt[0:64, m, lr, :], qaT_d[qh * 2 + m, :, q0:q0 + 512], writes=[qb_])
                                                    S.dma("pool", qt[64:69, m, lr, :], qaug[lr, :, q0:q0 + 512], writes=[qb_])
                                        else:
                                            S.dma("sp", qt[:, :], qcT_d[qh, :, q0:q0 + 512], writes=[qb_])

                                    grp_Of = []
                                    for m in range(nmap):
                                        for kc in range(nkc):
                                            last = (kc == nkc - 1)
                                            state = {}

                                            def qk(kc=kc, m=m, qt=qt, qb_=qb_, slot=slot, kvb=kvb, qc=qc, state=state, kvh=kvh, si=si, RK=RK):
                                                pst, psb = r_pS.next()
                                                state["ps"] = (pst, psb)
                                                kt = KT[slot][m]
                                                if kind != "da":
                                                    mm_group([(pst[:, :], kt[:, kc * 128:(kc + 1) * 128], qt[:, :])], [kvb, qb_], [psb])
                                                    return
                                                rho, c = kc // 16, kc % 16
                                                if c < 4 * qc or c >= 4 * qc + 4:
                                                    lr = 0 if c < 4 * qc else 1
                                                    mm_group([(pst[:, :], kt[0:69, kc * 128:(kc + 1) * 128], qt[0:69, m, lr, :])], [kvb, qb_], [psb])
                                                    return
                                                t = c - 4 * qc
                                                fns = []
                                                for s in range(4):
                                                    lr = 0 if s >= t else 1
                                                    fns.append(lambda e, pst=pst, kt=kt, qt=qt, s=s, lr=lr, d=(s == t): e.matmul(
                                                        pst[:, s * 128:(s + 1) * 128], kt[0:69, kc * 128:(kc + 1) * 128], qt[0:69, m, lr, s * 128:(s + 1) * 128],
                                                        start=True, stop=not d))
                                                    if s == t:
                                                        two = RK > 1
                                                        fns.append(lambda e, pst=pst, s=s, two=two: e.matmul(pst[:, s * 128:(s + 1) * 128], ident[:], dct[:, si, kvh, :],
                                                                                                             start=False, stop=not two))
                                                        if two:
                                                            fns.append(lambda e, pst=pst, s=s, rho=rho: e.matmul(pst[:, s * 128:(s + 1) * 128], ident[:], dct2[:, kvh, rho, :],
                                                                                                                 start=False, stop=True))
                                                pe_seq(fns, [kvb, qb_, b_dc, b_const], [psb])

                                            def ex(state=state):
                                                pst, psb = state["ps"]
                                                ptt, ptb = r_PT.next()
                                                state["pt"] = (ptt, ptb)
                                                S.op("act", lambda e: e.activation(out=ptt[:], in_=pst[:], func=AF.Exp, scale=scale),
                                                     reads=[psb], writes=[ptb])

                                            def pv(kc=kc, state=state, slot=slot, kvb=kvb, last=last):
                                                ptt, ptb = state["pt"]
                                                bank = pv_bank[0]
                                                pe_seq([(lambda e, s=s: e.matmul(pO[bank * 2 + s // 2][:, s % 2, :], ptt[:, s * 128:(s + 1) * 128], VT[slot][:, kc, :],
                                                                                 start=(kc == 0 and s % 2 == 0), stop=last, skip_group_check=True)) for s in range(4)],
                                                       [ptb, kvb], [pObuf[bank]])

                                            pre = None
                                            if kc == 0 and m == 0:
                                                def pre(load_q=load_q, load_kv=load_kv, f=first[0]):
                                                    if f:
                                                        load_kv()
                                                    load_q()
                                                first[0] = False
                                            postf = None
                                            if last:
                                                def postf(m=m, qh=qh, q0=q0, grp_Of=grp_Of):
                                                    oft, ofb = r_Of.next()
                                                    bank = pv_bank[0]
                                                    for half in range(2):
                                                        S.op("dve", lambda e, oft=oft, bank=bank, half=half: e.tensor_copy(out=oft[:, half * 2:half * 2 + 2, :], in_=pO[bank * 2 + half][:]),
                                                             reads=[pObuf[bank]], writes=[ofb])
                                                    pv_bank[0] = 1 - bank
                                                    grp_Of.append((oft, ofb))
                                                    if m == nmap - 1:
                                                        return finish(grp_Of, qh, q0)
                                                    return None
                                            tasks.append((pre, qk, ex, pv, postf))

                    emit_pipelined(tasks, LOOK=2, PRE=24, DEFER=4)
                    S.barrier()
                    S.flush()

            def na_attention():
                with ExitStack() as st:
                    Gt = sb(st, "Gt", [128, 16, NA_E * 64], BF16)
                    Mt = sb(st, "Mt", [128, 3, 8, 512], BF16)
                    b_G = Buf("G")
                    for h in range(16):
                        S.dma("pool", Gt[:, h, :], na_g[l, h], writes=[b_G])
                    for v in range(3):
                        S.dma("pool", Mt[:, v], na_m[v].rearrange("t k q -> k t q"), writes=[b_G])
                    if has_S:
                        Ms = sb(st, "Ms", [128, 3, 6, 512], BF16)
                        for v in range(3):
                            S.dma("pool", Ms[:, v], na_ms[v].rearrange("t k q -> k t q"), writes=[b_G])
                        Gs = [sb(st, f"Gs{i}", [128, 4, 1152], BF16) for i in range(2)]
                        b_Gs = [Buf("Gs0"), Buf("Gs1")]
                        KTs = [sb(st, f"sKT{i}", [96, 4, 768], BF16) for i in range(2)]
                        VTs = [sb(st, f"sVT{i}", [128, 4, 6, 65], BF16) for i in range(2)]
                        QTs = [sb(st, f"sQT{i}", [96, 512], BF16) for i in range(2)]
                        kvqs = [Buf("skvq0"), Buf("skvq1")]
                        for i in range(2):
                            S.op("pool", lambda e, i=i: e.memset(VTs[i][:], 1.0), writes=[kvqs[i]])
                            S.op("pool", lambda e, i=i: e.memset(KTs[i][64:96, :, :], 0.0), writes=[kvqs[i]])
                            S.op("pool", lambda e, i=i: e.memset(QTs[i][64:96, :], 0.0), writes=[kvqs[i]])
                    KT = [sb(st, f"nKT{i}", [96, 1024], BF16) for i in range(3)]
                    VT = [sb(st, f"nVT{i}", [128, 8, 65], BF16) for i in range(3)]
                    QT = [sb(st, f"nQT{i}", [96, 512], BF16) for i in range(3)]
                    kvq = [Buf(f"nkvq{i}") for i in range(3)]
                    for i in range(3):
                        S.op("pool", lambda e, i=i: e.memset(VT[i][:], 1.0), writes=[kvq[i]])
                        S.op("pool", lambda e, i=i: e.memset(KT[i][64:96, :], 0.0), writes=[kvq[i]])
                        S.op("pool", lambda e, i=i: e.memset(QT[i][64:96, :], 0.0), writes=[kvq[i]])
                    PT = [sb(st, f"nPT{i}", [128, 512], BF16) for i in range(3)]
                    r_PT = Rot(PT)
                    Of = [sb(st, f"nOf{i}", [128, 4, 65], F32) for i in range(2)]
                    r_Of = Rot(Of)
                    rr = sb(st, "nrr", [128, 4], F32)
                    on = [sb(st, f"non{i}", [128, 4, 64], BF16) for i in range(2)]
                    r_on = Rot(on)
                    sto = [sb(st, f"nsto{i}", [64, 512], BF16) for i in range(2)]
                    r_sto = Rot(sto)
                    b_post = Buf("npost")
                    pS = [ps(st, f"npS{i}", [128, 512]) for i in range(3)]
                    r_pS = Rot(pS)
                    pO = [ps(st, f"npO{i}", [128, 512])[:, 0:260].rearrange("p (s e) -> p s e", s=4) for i in range(2)]
                    pObuf = [Buf("npOa"), Buf("npOb")]
                    pTt = ps(st, "npTt", [128, 8, 128], BF16)[0:64, 0:4, :]
                    b_pTt = Buf("npTt")
                    tasks = []
                    it = [0]
                    its = [0]
                    pv_bank = [0]

                    def mk_post(h, q0):
                        def postf():
                            bank = pv_bank[0]
                            pv_bank[0] = 1 - bank
                            oft, ofb = r_Of.next()
                            S.op("dve", lambda e: e.tensor_copy(out=oft[:], in_=pO[bank][:]), reads=[pObuf[bank]], writes=[ofb])
                            S.op("dve", lambda e: e.reciprocal(out=rr[:, 0:4], in_=oft[:, :, 64]), reads=[ofb], writes=[b_post])
                            ont, onb = r_on.next()
                            S.op("dve", lambda e: e.tensor_tensor(out=ont[:], in0=oft[:, :, 0:64], in1=rr[:, 0:4].unsqueeze(2).to_broadcast([128, 4, 64]), op=ALU.mult),
                                 reads=[ofb, b_post], writes=[onb])
                            def part2():
                                pe_seq([(lambda e, s=s: e.transpose(pTt[:, s, :], ont[:, s, :], ident[:])) for s in range(4)], [onb, b_const], [b_pTt])
                                so, sob = r_sto.next()
                                S.op("act", lambda e: e.copy(out=so[:], in_=pTt[:].rearrange("p s q -> p (s q)")), reads=[b_pTt], writes=[sob])
                                S.dma("sp", oT_d[1][h // 2, (h % 2) * 64:(h % 2) * 64 + 64, q0:q0 + 512], so[:], reads=[sob])
                            return part2
                        return postf

                    def mk_ex(state):
                        def ex():
                            pst, psb = state["ps"]
                            ptt, ptb = r_PT.next()
                            state["pt"] = (ptt, ptb)
                            S.op("act", lambda e: e.activation(out=ptt[:], in_=pst[:], func=AF.Exp, scale=0.125), reads=[psb], writes=[ptb])
                        return ex

                    def mk_pv(state, vt_ap, kb, firstt, lastt):
                        def pv():
                            ptt, ptb = state["pt"]
                            po = pO[pv_bank[0]]
                            pe_seq([(lambda e, s=s: e.matmul(po[:, s, :], ptt[:, s * 128:(s + 1) * 128], vt_ap, start=(firstt and s == 0), stop=lastt, skip_group_check=True))
                                    for s in range(4)], [ptb, kb], [pObuf[pv_bank[0]]])
                        return pv

                    for si, (sname, NQ, RK) in enumerate(segs):
                        tok0 = seg_off[si]
                        if RK == 1:
                            sv = seg_src[si][0]
                            rows = NQ // 64
                            nqb = rows // 8
                            for qb in range(nqb):
                                var = 0 if qb == 0 else (2 if qb == nqb - 1 else 1)
                                R0 = 8 * qb
                                tlist = [t for t in range(8) if 0 <= R0 - 4 + 2 * t and R0 - 4 + 2 * t + 1 < rows]
                                for h in range(16):
                                    slot = it[0] % 3
                                    it[0] += 1
                                    kb = kvq[slot]

                                    def pre(slot=slot, kb=kb, h=h, R0=R0, tok0=tok0, tlist=tlist, sv=sv):
                                        ta, tb = tlist[0], tlist[-1] + 1
                                        k0 = (R0 - 4) * 64
                                        S.dma("sp", KT[slot][0:64, ta * 128:tb * 128], sv.knT(h, k0 + ta * 128, k0 + tb * 128), writes=[kb])
                                        for (ko_, nk_, vap) in sv.vpieces("vn", k0 + ta * 128, k0 + tb * 128, h * 64, (h + 1) * 64):
                                            S.dma("sp", VT[slot][:, ta + ko_:ta + ko_ + nk_, 0:64], vap, writes=[kb])
                                        S.dma("sp", QT[slot][0:64, :], qnT_d[h, :, tok0 + R0 * 64:tok0 + R0 * 64 + 512], writes=[kb])

                                    for ti, t in enumerate(tlist):
                                        state = {}
                                        lastt = (ti == len(tlist) - 1)

                                        def qk(t=t, slot=slot, kb=kb, h=h, var=var, state=state):
                                            pst, psb = r_pS.next()
                                            state["ps"] = (pst, psb)
                                            off = (14 - 2 * t) * 64
                                            mm_group([(pst[:, :], KT[slot][0:72, t * 128:(t + 1) * 128], QT[slot][0:72, :]),
                                                      (pst[:, :], ident8[:], Gt[:, h, off:off + 512]),
                                                      (pst[:, :], ident[:], Mt[:, var, t, :])], [kb, b_G, b_const], [psb])

                                        tasks.append((pre if ti == 0 else None, qk, mk_ex(state), mk_pv(state, VT[slot][:, t, :], kb, ti == 0, lastt),
                                                      mk_post(h, tok0 + R0 * 64) if lastt else None))
                        else:
                            srcs = seg_src[si]
                            for h in range(16):
                                gslot = h % 2

                                def load_g(h=h, gslot=gslot):
                                    S.dma("pool", Gs[gslot][:], na_gs[l, h], writes=[b_Gs[gslot]])

                                for qb in range(4):
                                    var = 0 if qb == 0 else (2 if qb == 3 else 1)
                                    dl = [dd for dd in range(-1, 5) if 0 <= 4 * qb + dd <= 15]
                                    slot = its[0] % 2
                                    its[0] += 1
                                    kb = kvqs[slot]

                                    def pre(slot=slot, kb=kb, h=h, qb=qb, dl=dl, srcs=srcs, tok0=tok0, load_g=load_g):
                                        if qb == 0:
                                            load_g()
                                        j0, j1 = dl[0] + 1, dl[-1] + 2
                                        k0 = 128 * (4 * qb - 1)
                                        for rho in range(4):
                                            S.dma("sp", KTs[slot][0:64, rho, j0 * 128:j1 * 128], srcs[rho].knT(h, k0 + j0 * 128, k0 + j1 * 128), reads=[b_gath], writes=[kb])
                                            for (ko_, nk_, vap) in srcs[rho].vpieces("vn", k0 + j0 * 128, k0 + j1 * 128, h * 64, (h + 1) * 64):
                                                S.dma("sp", VTs[slot][:, rho, j0 + ko_:j0 + ko_ + nk_, 0:64], vap, reads=[b_gath], writes=[kb])
                                        S.dma("sp", QTs[slot][0:64, :], qnT_d[h, :, tok0 + qb * 512:tok0 + qb * 512 + 512], writes=[kb])

                                    combos = [(rho, dd) for rho in range(4) for dd in dl]
                                    for ci_, (rho, dd) in enumerate(combos):
                                        state = {}
                                        lastt = (ci_ == len(combos) - 1)

                                        def qk(rho=rho, dd=dd, slot=slot, kb=kb, gslot=gslot, var=var, state=state):
                                            pst, psb = r_pS.next()
                                            state["ps"] = (pst, psb)
                                            off = (32 - 8 * dd) * 16
                                            j = dd + 1
                                            mm_group([(pst[:, :], KTs[slot][0:72, rho, j * 128:(j + 1) * 128], QTs[slot][0:72, :]),
                                                      (pst[:, :], ident8[:], Gs[gslot][:, rho, off:off + 512]),
                                                      (pst[:, :], ident[:], Ms[:, var, j, :])], [kb, b_Gs[gslot], b_G, b_const], [psb])

                                        tasks.append((pre if ci_ == 0 else None, qk, mk_ex(state), mk_pv(state, VTs[slot][:, rho, dd + 1, :], kb, ci_ == 0, lastt),
                                                      mk_post(h, tok0 + qb * 512) if lastt else None))
                    emit_pipelined(tasks, LOOK=2, PRE=6, DEFER=3)
                    S.barrier()
                    S.flush()

            if has_S:
                for bi in range(NBLK):
                    S.collective(lambda e, bi=bi: e.collective_compute("AllGather", ALU.bypass, replica_groups=[[0, 1, 2, 3], [4, 5, 6, 7]],
                                                                       ins=[kv_src[bi * 128:(bi + 1) * 128, :].opt()],
                                                                       outs=[kv_all[bi * 512:(bi + 1) * 512, :].opt()]), writes=[b_gath])
            maybe_stop("A")
            attention("da")
            maybe_stop("da")
            na_attention()
            maybe_stop("na")
            attention("gq")
            maybe_stop("gq")

            with ExitStack() as st:
                CH = 256
                Wb = [sb(st, f"Wb{i}", [128, 8, D], BF16) for i in range(4)]
                b_W = Buf("W")
                for i, w in enumerate(w_br + [w_o]):
                    S.dma("pool", Wb[i][:], w[l].rearrange("(c p) n -> p c n", p=128), writes=[b_W])
                oT = [[sb(st, f"oT{j}{i}", [128, 8, CH], BF16) for i in range(3)] for j in range(2)]
                gT = [sb(st, f"gT{j}", [128, 24, CH], BF16) for j in range(2)]
                b_in = [Buf("cin0"), Buf("cin1")]
                mm = [sb(st, f"mm{i}", [128, CH], F32) for i in range(3)]
                b_mm = [Buf(f"mm{i}") for i in range(3)]
                mT = sb(st, "mT", [128, 8, CH], BF16)
                b_mT = Buf("mT")
                hb = [sb(st, f"chb{i}", [128, D], F32) for i in range(2)]
                r_hb = Rot(hb)
                yb = sb(st, "yb", [128, D], F32)
                junk = sb(st, "junkC", [128, D], F32)
                ssC = sb(st, "ssC", [128, 4], F32)
                ub = [sb(st, f"cub{i}", [128, D], BF16) for i in range(2)]
                r_ub = Rot(ub)
                sT = [sb(st, f"csT{i}", [128, 8, 128], BF16) for i in range(2)]
                r_sT = Rot(sT)
                b_y = Buf("y"); b_ss = Buf("ssC"); b_junk = Buf("junkC")
                pB = [ps(st, f"pB{i}", [128, 512]) for i in range(4)]
                r_pB = Rot(pB)
                pOo = ps(st, "pOo", [128, D])
                b_pOo = Buf("pOo")
                pT = ps(st, "pTC", [128, 8, 128], BF16)
                b_pT = Buf("pTC")
                nch = NT // CH

                def load_c(ci):
                    j = ci % 2
                    t0 = ci * CH
                    for b in range(3):
                        S.dma("sp", oT[j][b][:], oT_d[b][:, :, t0:t0 + CH].rearrange("c p t -> p c t"), writes=[b_in[j]])
                    S.dma("sp", gT[j][:], gT_d[:, :, t0:t0 + CH].rearrange("c p t -> p c t"), writes=[b_in[j]])

                load_c(0)
                for ci in range(nch):
                    j = ci % 2
                    t0 = ci * CH
                    if ci + 1 < nch:
                        load_c(ci + 1)
                    for cc in range(8):
                        for b in range(3):
                            pb, pbb = r_pB.next()
                            mm_group([(pb[:, 0:CH], Wb[b][:, e_, cc * 128:(cc + 1) * 128], oT[j][b][:, e_, :]) for e_ in range(8)], [b_W, b_in[j]], [pbb])
                            S.op("dve", lambda e, pb=pb, b=b, cc=cc, j=j: e.tensor_tensor(out=mm[b][:], in0=pb[:, 0:CH], in1=gT[j][:, b * 8 + cc, :], op=ALU.mult),
                                 reads=[pbb, b_in[j]], writes=[b_mm[b]])
                        S.op("pool", lambda e: e.tensor_tensor(out=mm[0][:], in0=mm[0][:], in1=mm[1][:], op=ALU.add), reads=[b_mm[0], b_mm[1]], writes=[b_mm[0]])
                        S.op("pool", lambda e, cc=cc: e.tensor_tensor(out=mT[:, cc, :], in0=mm[0][:], in1=mm[2][:], op=ALU.add), reads=[b_mm[0], b_mm[2]], writes=[b_mT])
                    for tt in range(CH // 128):
                        tk = t0 + tt * 128
                        ht, hbuf = r_hb.next()
                        S.dma("sp", ht[:], src_h[tk:tk + 128, :], writes=[hbuf])
                        for nn in range(2):
                            mm_group([(pOo[:, nn * 512:(nn + 1) * 512], mT[:, cc, tt * 128:(tt + 1) * 128], Wb[3][:, cc, nn * 512:(nn + 1) * 512]) for cc in range(8)],
                                     [b_mT, b_W], [b_pOo])
                        S.op("act", lambda e: e.activation(out=junk[:], in_=pOo[:], func=AF.Square, accum_out=ssC[:, 0:1]), reads=[b_pOo], writes=[b_junk, b_ss])
                        rstd_from_ss(ssC[:, 0:1], ssC[:, 1:2], D, [b_ss])
                        S.op("dve", lambda e: e.scalar_tensor_tensor(out=yb[:], in0=pOo[:], scalar=ssC[:, 1:2], in1=gvec[:, 1, :], op0=ALU.mult, op1=ALU.mult),
                             reads=[b_pOo, b_ss, b_gv], writes=[b_y])
                        S.op("pool", lambda e, ht=ht: e.tensor_tensor(out=ht[:], in0=ht[:], in1=yb[:], op=ALU.add), reads=[hbuf, b_y], writes=[hbuf])
                        S.dma("sp", h_d[tk:tk + 128, :], ht[:], reads=[hbuf])
                        S.op("act", lambda e, ht=ht: e.activation(out=junk[:], in_=ht[:], func=AF.Square, accum_out=ssC[:, 2:3]), reads=[hbuf], writes=[b_junk, b_ss])
                        rstd_from_ss(ssC[:, 2:3], ssC[:, 3:4], D, [b_ss])
                        ut, ubuf = r_ub.next()
                        S.op("dve", lambda e, ht=ht, ut=ut: e.scalar_tensor_tensor(out=ut[:], in0=ht[:], scalar=ssC[:, 3:4], in1=gvec[:, 2, :], op0=ALU.mult, op1=ALU.mult),
                             reads=[hbuf, b_ss, b_gv], writes=[ubuf])
                        pe_seq([(lambda e, ut=ut, c=c: e.transpose(pT[:, c, :], ut[:, c * 128:(c + 1) * 128], ident[:])) for c in range(8)], [ubuf, b_const], [b_pT])
                        stt, stb = r_sT.next()
                        S.op("act", lambda e, stt=stt: e.copy(out=stt[:], in_=pT[:]), reads=[b_pT], writes=[stb])
                        S.dma("sp", u2T_d[:, :, tk:tk + 128].rearrange("c p t -> p c t"), stt[:], reads=[stb])
                S.barrier()
                S.flush()

            maybe_stop("C")
            with ExitStack() as st:
                uT = sb(st, "u2T", [128, 8, 2048], BF16)
                b_uT = Buf("u2T")
                wb = [sb(st, f"dwb{i}", [128, 8, 512], BF16) for i in range(3)]
                r_wb = Rot(wb)
                rl = [sb(st, f"rl{i}", [128, 512], F32) for i in range(3)]
                r_rl = Rot(rl)
                stg = [sb(st, f"dstg{i}", [128, 512], BF16) for i in range(3)]
                r_stg = Rot(stg)
                pA = [ps(st, f"dpA{i}", [128, 512]) for i in range(4)]
                r_pA = Rot(pA)
                for sc in range(NT // 2048):
                    t0 = sc * 2048
                    S.dma("sp", uT[:], u2T_d[:, :, t0:t0 + 2048].rearrange("c p t -> p c t"), writes=[b_uT])
                    for blk in range(8):
                        wt, wbuf = r_wb.next()
                        S.dma("pool", wt[:], w_up[l].rearrange("(c p) n -> p c n", p=128)[:, :, blk * 512:(blk + 1) * 512], writes=[wbuf])
                        for sub in range(4):
                            for tq in range(4):
                                pa, pab = r_pA.next()
                                mm_group([(pa[:, :], wt[:, c, sub * 128:(sub + 1) * 128], uT[:, c, tq * 512:(tq + 1) * 512]) for c in range(8)], [wbuf, b_uT], [pab])
                                rt, rb = r_rl.next()
                                S.op("act", lambda e, rt=rt, pa=pa: e.activation(out=rt[:], in_=pa[:], func=AF.Relu), reads=[pab], writes=[rb])
                                sg, sgb = r_stg.next()
                                S.op("pool", lambda e, rt=rt, sg=sg: e.tensor_tensor(out=sg[:], in0=rt[:], in1=rt[:], op=ALU.mult), reads=[rb], writes=[sgb])
                                S.dma("sp", aT_d[blk * 4 + sub, :, t0 + tq * 512:t0 + (tq + 1) * 512], sg[:], reads=[sgb])
                S.barrier()
                S.flush()

            maybe_stop("D1")
            with ExitStack() as st:
                Wd = sb(st, "Wd", [128, 32, D], BF16)
                Wg = sb(st, "Wg", [128, 8, D], BF16)
                Wp = sb(st, "Wp", [128, 2, D], BF16)
                b_W = Buf("W2")
                for q4 in range(4):
                    S.dma("pool", Wd[:, q4 * 8:(q4 + 1) * 8, :], w_down[l, q4 * 1024:(q4 + 1) * 1024, :].rearrange("(c p) n -> p c n", p=128), writes=[b_W])
                S.dma("pool", Wg[:], w_ple_gate[l].rearrange("(c p) n -> p c n", p=128), writes=[b_W])
                S.dma("pool", Wp[:], w_ple[l].rearrange("(c p) n -> p c n", p=128), writes=[b_W])
                aT = [sb(st, f"aT{j}", [128, 32, 512], BF16) for j in range(2)]
                b_a = [Buf("a0"), Buf("a1")]
                hb = [sb(st, f"ehb{i}", [128, D], F32) for i in range(2)]
                r_hb = Rot(hb)
                pl = [sb(st, f"pl{i}", [128, PLE], F32) for i in range(2)]
                r_pl = Rot(pl)
                plb = sb(st, "plb", [128, PLE], BF16)
                b_plb = Buf("plb")
                yb = sb(st, "eyb", [128, D], F32)
                gt = sb(st, "egt", [128, D], F32)
                junk = sb(st, "junkE", [128, D], F32)
                ssE = sb(st, "ssE", [128, 4], F32)
                hbf = sb(st, "hbf", [128, D], BF16)
                hT = sb(st, "hT", [128, 8, 128], BF16)
                pTs = sb(st, "pTs", [128, 2, 128], BF16)
                b_y = Buf("ey"); b_g = Buf("eg"); b_ss = Buf("ssE"); b_junk = Buf("junkE"); b_hbf = Buf("hbf"); b_hT = Buf("hT"); b_pTs = Buf("pTs")
                pF = ps(st, "pF", [128, D]); b_pF = Buf("pF")
                pG = ps(st, "pG", [128, D]); b_pG = Buf("pG")
                pE = ps(st, "pE", [128, D]); b_pE = Buf("pE")
                pT = ps(st, "pTE", [128, 8, 128], BF16); b_pT = Buf("pTE")
                pT2 = ps(st, "pTE2", [128, 8, 128], BF16)[:, 0:2, :]; b_pT2 = Buf("pTE2")
                nch = NT // 512

                def load_a(ci):
                    j = ci % 2
                    for q4 in range(4):
                        S.dma("sp", aT[j][:, q4 * 8:(q4 + 1) * 8, :], aT_d[q4 * 8:(q4 + 1) * 8, :, ci * 512:(ci + 1) * 512].rearrange("c p t -> p c t"), writes=[b_a[j]])

                load_a(0)
                for ci in range(nch):
                    j = ci % 2
                    if ci + 1 < nch:
                        load_a(ci + 1)
                    for tt in range(4):
                        tk = ci * 512 + tt * 128
                        ht, hbuf = r_hb.next()
                        S.dma("sp", ht[:], h_d[tk:tk + 128, :], writes=[hbuf])
                        plt, plbuf = r_pl.next()
                        S.dma("sp", plt[:], p_in[l, tk:tk + 128, :], writes=[plbuf])
                        for nn in range(2):
                            mm_group([(pF[:, nn * 512:(nn + 1) * 512], aT[j][:, ch, tt * 128:(tt + 1) * 128], Wd[:, ch, nn * 512:(nn + 1) * 512]) for ch in range(32)],
                                     [b_a[j], b_W], [b_pF])
                        S.op("act", lambda e: e.activation(out=junk[:], in_=pF[:], func=AF.Square, accum_out=ssE[:, 0:1]), reads=[b_pF], writes=[b_junk, b_ss])
                        rstd_from_ss(ssE[:, 0:1], ssE[:, 1:2], D, [b_ss])
                        S.op("dve", lambda e: e.scalar_tensor_tensor(out=yb[:], in0=pF[:], scalar=ssE[:, 1:2], in1=gvec[:, 3, :], op0=ALU.mult, op1=ALU.mult),
                             reads=[b_pF, b_ss, b_gv], writes=[b_y])
                        S.op("pool", lambda e, ht=ht: e.tensor_tensor(out=ht[:], in0=ht[:], in1=yb[:], op=ALU.add), reads=[hbuf, b_y], writes=[hbuf])
                        S.op("dve", lambda e, ht=ht: e.tensor_copy(out=hbf[:], in_=ht[:]), reads=[hbuf], writes=[b_hbf])
                        pe_seq([(lambda e, c=c: e.transpose(pT[:, c, :], hbf[:, c * 128:(c + 1) * 128], ident[:])) for c in range(8)], [b_hbf, b_const], [b_pT])
                        S.op("act", lambda e: e.copy(out=hT[:], in_=pT[:]), reads=[b_pT], writes=[b_hT])
                        S.op("pool", lambda e, plt=plt: e.tensor_copy(out=plb[:], in_=plt[:]), reads=[plbuf], writes=[b_plb])
                        pe_seq([(lambda e, c=c: e.transpose(pT2[:, c, :], plb[:, c * 128:(c + 1) * 128], ident[:])) for c in range(2)], [b_plb, b_const], [b_pT2])
                        S.op("dve", lambda e: e.tensor_copy(out=pTs[:], in_=pT2[:]), reads=[b_pT2], writes=[b_pTs])
                        for nn in range(2):
                            mm_group([(pG[:, nn * 512:(nn + 1) * 512], hT[:, c, :], Wg[:, c, nn * 512:(nn + 1) * 512]) for c in range(8)], [b_hT, b_W], [b_pG])
                        for nn in range(2):
                            mm_group([(pE[:, nn * 512:(nn + 1) * 512], pTs[:, c, :], Wp[:, c, nn * 512:(nn + 1) * 512]) for c in range(2)], [b_pTs, b_W], [b_pE])
                        S.op("act", lambda e: e.activation(out=gt[:], in_=pG[:], func=AF.Sigmoid), reads=[b_pG], writes=[b_g])
                        S.op("dve", lambda e: e.tensor_tensor(out=gt[:], in0=pE[:], in1=gt[:], op=ALU.mult), reads=[b_pE, b_g], writes=[b_g])
                        S.op("act", lambda e: e.activation(out=junk[:], in_=gt[:], func=AF.Square, accum_out=ssE[:, 2:3]), reads=[b_g], writes=[b_junk, b_ss])
                        rstd_from_ss(ssE[:, 2:3], ssE[:, 3:4], D, [b_ss])
                        S.op("dve", lambda e: e.scalar_tensor_tensor(out=yb[:], in0=gt[:], scalar=ssE[:, 3:4], in1=gvec[:, 4, :], op0=ALU.mult, op1=ALU.mult),
                             reads=[b_g, b_ss, b_gv], writes=[b_y])
                        S.op("pool", lambda e, ht=ht: e.tensor_tensor(out=ht[:], in0=ht[:], in1=yb[:], op=ALU.add), reads=[hbuf, b_y], writes=[hbuf])
                        S.dma("sp", dst_final[tk:tk + 128, :], ht[:], reads=[hbuf])
                S.barrier()
                S.flush()
    return nc


def _rope_tables(Sl):
    t = np.arange(Sl)
    row = (t // GRID_W).astype(np.float32)
    col = (t % GRID_W).astype(np.float32)
    half = 64
    freqs = (np.float32(10000.0) ** (-np.arange(0, half, 2, dtype=np.float32) / np.float32(half))).astype(np.float32)
    ang = np.concatenate([row[:, None] * freqs, col[:, None] * freqs], axis=-1).astype(np.float32)
    return np.cos(ang).astype(np.float32), np.sin(ang).astype(np.float32)


def _aug_tables(segs, rank):
    qcols, kcols = [], []
    dcorr = np.zeros((len(segs), 8, 128, 128), np.float32)
    dcorr2 = np.zeros((8, 4, 128, 128), np.float32)
    kk = np.arange(128)[:, None]
    qq = np.arange(128)[None, :]
    for si, (_, nq, R) in enumerate(segs):
        a = np.arange(nq)
        one = np.ones(nq, np.float32)
        qcols.append(np.stack([(a // 128).astype(np.float32), (a % 128).astype(np.float32), one, one, one]))
        mult = 1.0 if R == 1 else 4.0
        ks = np.zeros((8, 5, nq * R), np.float32)
        for h in range(8):
            m = 2.0 ** (-(h + 1))
            for rho in range(R):
                bidx = np.arange(nq)
                sl = slice(rho * nq, (rho + 1) * nq)
                ks[h, 0, sl] = -1024.0 * m * mult
                ks[h, 1, sl] = -8.0 * m * mult
                ks[h, 2, sl] = 1024.0 * m * mult * (bidx // 128)
                ks[h, 3, sl] = 8.0 * m * mult * (bidx % 128)
                ks[h, 4, sl] = 0.0 if R == 1 else -8.0 * m * (rank - rho)
                if R > 1:
                    d = 16.0 * m * (rank - rho)
                    dcorr2[h, rho] = d * (qq < kk) + min(0.0, d) * (qq == kk)
            dcorr[si, h] = -16.0 * m * mult * np.maximum(kk - qq, 0)
        kcols.append(ks)
    ql = np.concatenate(qcols, axis=1)
    qaug = np.stack([ql, -ql]).astype(np.float32)
    kaug = np.concatenate(kcols, axis=2).astype(np.float32)
    return qaug, kaug, dcorr, dcorr2


def _na_tables(na_rpb):
    c = np.arange(64)
    cs = np.clip(c - 8, 0, 48)
    colvalid = (c[:, None] >= cs[None, :]) & (c[:, None] < cs[None, :] + 16)
    dc = np.clip(c[:, None] - c[None, :], -15, 15) + 15
    L = na_rpb.shape[0]
    G = np.zeros((L, 16, 2, 64, NA_E, 64), np.float32)
    for krl in range(2):
        for e in range(NA_E):
            dr = 17 - e + krl
            if 0 <= dr <= 14:
                G[:, :, krl, :, e, :] = na_rpb[:, :, dr][:, :, dc]
    G = np.where(colvalid[None, None, None, :, None, :], G, np.float32(NEG_G)).astype(np.float32)
    G = G.reshape(L, 16, 128, NA_E * 64)
    M = np.zeros((3, 8, 2, 64, 8, 64), np.float32)
    for var in range(3):
        for t in range(8):
            for krl in range(2):
                kr = -4 + 2 * t + krl
                for qr in range(8):
                    if var == 0:
                        start = max(qr - 4, 0)
                    elif var == 1:
                        start = qr - 4
                    else:
                        start = min(qr - 4, 0)
                    ok = (start <= kr < start + 8)
                    if not ok:
                        M[var, t, krl, :, qr, :] = NEG_M
    M = M.reshape(3, 8, 128, 512)
    return G, M


def _na_tables_S(na_rpb, rank):
    L = na_rpb.shape[0]
    ap = np.arange(16)
    G = np.zeros((L, 16, 8, 16, 4, 72, 16), np.float32)
    for rho in range(4):
        c = 4 * ap + rank
        kc = 4 * ap + rho
        cs = np.clip(c - 8, 0, 48)
        colvalid = (kc[:, None] >= cs[None, :]) & (kc[:, None] < cs[None, :] + 16)
        dc = np.clip(kc[:, None] - c[None, :], -15, 15) + 15
        for Rk in range(8):
            for e in range(72):
                dr = Rk + 39 - e
                if 0 <= dr <= 14:
                    G[:, :, Rk, :, rho, e, :] = na_rpb[:, :, dr][:, :, dc]
        G[:, :, :, :, rho] = np.where(colvalid[None, None, None, :, None, :], G[:, :, :, :, rho], np.float32(NEG_G))
    G = G.reshape(L, 16, 128, 4, 72 * 16)
    M = np.zeros((3, 6, 8, 16, 32, 16), np.float32)
    for var in range(3):
        for j in range(6):
            dd = j - 1
            for Rk in range(8):
                kr = 8 * dd + Rk
                for Rq in range(32):
                    if var == 0:
                        start = max(Rq - 4, 0)
                    elif var == 1:
                        start = Rq - 4
                    else:
                        start = min(Rq - 4, 24)
                    if not (start <= kr < start + 8):
                        M[var, j, Rk, :, Rq, :] = NEG_M
    M = M.reshape(3, 6, 128, 512)
    return G, M


_CACHE = {}


def _run(inputs, depth, segs_fn, n_cores=8, stop_after=None):
    key = (depth, tuple(segs_fn), stop_after)
    if key not in _CACHE:
        try:
            _CACHE[key] = build_program(depth, segs_fn, stop_after)
        except _Stop as e:
            _CACHE[key] = e.args[0]
    nc = _CACHE[key]
    f = lambda a: np.ascontiguousarray(np.asarray(a, dtype=np.float32))
    fl = lambda a: np.ascontiguousarray(np.asarray(a, dtype=np.float32)[:depth])
    has_S = any(r > 1 for _, _, r in segs_fn)
    rpb = fl(inputs["na_rpb"])
    na_g, na_m = _na_tables(rpb)
    idents = np.stack([np.eye(128, dtype=np.float32), 8.0 * np.eye(128, dtype=np.float32)])
    shared = {
        "w_in": fl(inputs["w_in"]), "da_lambda": fl(inputs["da_lambda"]).reshape(depth, 256),
        "da_norm": fl(inputs["da_norm"]), "gq_q_norm": fl(inputs["gq_q_norm"]), "gq_k_norm": fl(inputs["gq_k_norm"]),
        "w_br_a": fl(inputs["w_br_a"]), "w_br_b": fl(inputs["w_br_b"]), "w_br_c": fl(inputs["w_br_c"]), "w_o": fl(inputs["w_o"]),
        "g_pre_mix": fl(inputs["g_pre_mix"]), "g_post_mix": fl(inputs["g_post_mix"]), "g_pre_mlp": fl(inputs["g_pre_mlp"]),
        "g_post_mlp": fl(inputs["g_post_mlp"]), "w_up": fl(inputs["w_up"]), "w_down": fl(inputs["w_down"]),
        "w_ple": fl(inputs["w_ple"]), "w_ple_gate": fl(inputs["w_ple_gate"]), "g_ple": fl(inputs["g_ple"]),
        "na_g": na_g, "na_m": na_m, "idents": idents,
    }
    xp, xs = f(inputs["x_prompt"]), f(inputs["x_sample"])
    pp, pS_ = f(inputs["p_prompt"]), f(inputs["p_sample"])
    cs_full = {s: _rope_tables(s) for s in (SEQ, DEC_SEQ)}
    per_rank = {}
    for rank in range(4):
        qaug, kaug, dcorr, dcorr2 = _aug_tables(segs_fn, rank)
        t = dict(qaug=qaug, kaug=kaug, dcorr=dcorr, dcorr2=dcorr2)
        if has_S:
            gs, ms = _na_tables_S(rpb, rank)
            t["na_gs"] = gs
            t["na_ms"] = ms
        per_rank[rank] = t
    in_maps = []
    for c in range(n_cores):
        rank, grp = c % 4, c // 4
        parts_x, parts_p, parts_cs, parts_sn = [], [], [], []
        for name, s, R in segs_fn:
            if R == 1:
                parts_x.append(xp[c]); parts_p.append(pp[:depth, c])
                parts_cs.append(cs_full[SEQ][0]); parts_sn.append(cs_full[SEQ][1])
            else:
                parts_x.append(xs[grp, rank::4]); parts_p.append(pS_[:depth, grp, rank::4])
                parts_cs.append(cs_full[DEC_SEQ][0][rank::4]); parts_sn.append(cs_full[DEC_SEQ][1][rank::4])
        m = dict(shared)
        m.update(per_rank[rank])
        m["x_in"] = np.ascontiguousarray(np.concatenate(parts_x, axis=0))
        m["p_in"] = np.ascontiguousarray(np.concatenate(parts_p, axis=1))
        m["cs_tab"] = np.ascontiguousarray(np.concatenate(parts_cs, axis=0))
        m["sn_tab"] = np.ascontiguousarray(np.concatenate(parts_sn, axis=0))
        in_maps.append(m)
    res = run_bass_kernel_spmd(nc, in_maps, core_ids=list(range(n_cores)))
    _CACHE["last_results"] = res.results
    return [r["y_out"] for r in res.results]


def kernel(**inputs):
    segs_fn = (("P", SEQ, 1), ("S", DEC_SEQ // 4, 4))
    ys = _run(inputs, DEPTH, segs_fn)
    y_prompt = np.stack([ys[c][0:SEQ] for c in range(8)]).astype(np.float32)
    y_sample = np.zeros((2, DEC_SEQ, D), np.float32)
    for c in range(8):
        y_sample[c // 4, (c % 4)::4] = ys[c][SEQ:SEQ + DEC_SEQ // 4]
    return (y_prompt, y_sample)
```
